# Optimizing a Trainium2 kernel written in Bass

```python
import functools
import jax, jax.numpy as jnp
from jax import lax
import numpy as np

D_MODEL = 1024
BATCH = 2
SEQ = 16384
DEPTH = 1
DEC_BATCH = 32
DEC_SEQ = 16
PAST_LEN = 2048

CHUNK = 64
WINDOW = 128
N_WIN_CHUNKS = WINDOW // CHUNK
ATTN_WIDTH = D_MODEL // 2
N_HEADS = 8
HEAD_DIM = ATTN_WIDTH // N_HEADS
N_KV_HEADS = 2
GQA_GROUP = N_HEADS // N_KV_HEADS
Q_DIM = N_HEADS * HEAD_DIM
KV_DIM = N_KV_HEADS * HEAD_DIM
POOL_WIDTH = D_MODEL - ATTN_WIDTH
POOL_WINDOWS = (2, 4, 8, 16)
N_POOL_GROUPS = len(POOL_WINDOWS)
POOL_GROUP_DIM = POOL_WIDTH // N_POOL_GROUPS
POOL_HIST = max(POOL_WINDOWS) - 1
IN_PROJ_DIM = Q_DIM + 2 * KV_DIM + POOL_WIDTH
D_FF = 4 * D_MODEL
PLE_DIM = 256
RMS_EPS = 1e-6

kernel_name = "hymba_swa_sink_pool_stream_step"


def rmsnorm(x, g):
    xf = x.astype(jnp.float32)
    y = xf * lax.rsqrt(jnp.mean(xf * xf, axis=-1, keepdims=True) + RMS_EPS)
    return (y * g.astype(jnp.float32)).astype(x.dtype)


def sink_softmax(s, valid, sink_b):
    s = jnp.where(valid, s, -jnp.inf)
    m = jnp.maximum(jnp.max(s, axis=-1, keepdims=True), sink_b)
    e = jnp.exp(s - m)
    denom = jnp.sum(e, axis=-1, keepdims=True) + jnp.exp(sink_b - m)
    return e / denom


def attn_prompt(q, k, v, sinks):
    B, S = q.shape[:2]
    nc = S // CHUNK
    kl = WINDOW + CHUNK
    pad = ((0, 0), (WINDOW, 0), (0, 0), (0, 0))
    kp = jnp.pad(k, pad).reshape(B, nc + N_WIN_CHUNKS, CHUNK, N_KV_HEADS, HEAD_DIM)
    vp = jnp.pad(v, pad).reshape(B, nc + N_WIN_CHUNKS, CHUNK, N_KV_HEADS, HEAD_DIM)
    kb = jnp.concatenate([kp[:, j:j + nc] for j in range(N_WIN_CHUNKS + 1)], axis=2)
    vb = jnp.concatenate([vp[:, j:j + nc] for j in range(N_WIN_CHUNKS + 1)], axis=2)
    qb = q.reshape(B, nc, CHUNK, N_KV_HEADS, GQA_GROUP, HEAD_DIM)
    s = jnp.einsum('bcqkgd,bcskd->bckgqs', qb, kb,
                   preferred_element_type=jnp.float32) * (HEAD_DIM ** -0.5)
    key_pos = (jnp.arange(nc)[:, None] - N_WIN_CHUNKS) * CHUNK + jnp.arange(kl)[None, :]
    valid = (key_pos >= 0)[None, :, None, None, None, :]
    sink_b = sinks.astype(jnp.float32).reshape(1, 1, N_KV_HEADS, GQA_GROUP, 1, 1)
    p = sink_softmax(s, valid, sink_b)
    o = jnp.einsum('bckgqs,bcskd->bcqkgd', p.astype(v.dtype), vb)
    return o.reshape(B, S, ATTN_WIDTH)


def attn_sample(q, k, v, cache_k, cache_v, sinks):
    B, T = q.shape[:2]
    kf = jnp.concatenate([cache_k.astype(k.dtype), k], axis=1)
    vf = jnp.concatenate([cache_v.astype(v.dtype), v], axis=1)
    qs = q.reshape(B, T, N_KV_HEADS, GQA_GROUP, HEAD_DIM)
    s = jnp.einsum('btkgd,bskd->bkgts', qs, kf,
                   preferred_element_type=jnp.float32) * (HEAD_DIM ** -0.5)
    sink_b = sinks.astype(jnp.float32).reshape(1, N_KV_HEADS, GQA_GROUP, 1, 1)
    p = sink_softmax(s, True, sink_b)
    o = jnp.einsum('bkgts,bskd->btkgd', p.astype(v.dtype), vf)
    return o.reshape(B, T, ATTN_WIDTH)


def multiscale_pool(u, hist, pos0, w_pool, pool_scale):
    T = u.shape[1]
    ext = jnp.concatenate([hist.astype(u.dtype), u], axis=1)
    extf = ext.astype(jnp.float32)
    cs = jnp.cumsum(jnp.pad(extf, ((0, 0), (1, 0), (0, 0))), axis=1)
    pos = pos0 + jnp.arange(T)
    outs = []
    for g, w in enumerate(POOL_WINDOWS):
        lo_c, hi_c = g * POOL_GROUP_DIM, (g + 1) * POOL_GROUP_DIM
        hi = cs[:, POOL_HIST + 1:POOL_HIST + 1 + T, lo_c:hi_c]
        lo = cs[:, POOL_HIST + 1 - w:POOL_HIST + 1 - w + T, lo_c:hi_c]
        cnt = jnp.minimum(pos + 1, w).astype(jnp.float32)[None, :, None]
        d = (hi - lo) / cnt - extf[:, POOL_HIST:, lo_c:hi_c]
        outs.append(jnp.einsum('btc,ce->bte', d, w_pool[g].astype(jnp.float32)))
    b = jnp.concatenate(outs, axis=-1) * pool_scale.astype(jnp.float32)
    return b.astype(u.dtype), ext[:, -POOL_HIST:]


def trunk_layer(x, ple, attn_fn, pool_hist, pos0, ln_mix, w_in, g_attn_out, w_pool,
                pool_scale, g_pool_out, w_out, ln_ffn, w_up, w_down, ln_ple,
                w_ple_gate, w_ple_proj):
    B, T, _ = x.shape
    h = rmsnorm(x, ln_mix)
    z = jnp.einsum('btd,de->bte', h, w_in)
    q = z[..., :Q_DIM].reshape(B, T, N_HEADS, HEAD_DIM)
    k = z[..., Q_DIM:Q_DIM + KV_DIM].reshape(B, T, N_KV_HEADS, HEAD_DIM)
    v = z[..., Q_DIM + KV_DIM:Q_DIM + 2 * KV_DIM].reshape(B, T, N_KV_HEADS, HEAD_DIM)
    u = z[..., Q_DIM + 2 * KV_DIM:]
    a = attn_fn(q, k, v)
    b, new_pool = multiscale_pool(u, pool_hist, pos0, w_pool, pool_scale)
    mix = jnp.concatenate([rmsnorm(a, g_attn_out), rmsnorm(b, g_pool_out)], axis=-1)
    x = x + jnp.einsum('bte,ed->btd', mix, w_out)
    hf = rmsnorm(x, ln_ffn)
    x = x + jnp.einsum('btf,fd->btd', jnp.square(jax.nn.relu(jnp.einsum('btd,df->btf', hf, w_up))), w_down)
    gate = jax.nn.sigmoid(jnp.einsum('btd,de->bte', rmsnorm(x, ln_ple), w_ple_gate))
    x = x + gate * jnp.einsum('btp,pd->btd', ple, w_ple_proj)
    return x, k, v, new_pool


def setup_inputs(seed: int = 0) -> dict:
    key = jax.random.key(seed)
    ks = jax.random.split(key, 24)
    f32 = jnp.float32

    def nrm(k, shape, scale=1.0):
        return jax.random.normal(k, shape, f32) * scale

    def gain(k, shape):
        return 1.0 + 0.05 * jax.random.normal(k, shape, f32)

    return {
        "x_prompt": nrm(ks[0], (BATCH, SEQ, D_MODEL)),
        "x_sample": nrm(ks[1], (DEC_BATCH, DEC_SEQ, D_MODEL)),
        "cache_k": nrm(ks[2], (DEPTH, DEC_BATCH, WINDOW, N_KV_HEADS, HEAD_DIM)),
        "cache_v": nrm(ks[3], (DEPTH, DEC_BATCH, WINDOW, N_KV_HEADS, HEAD_DIM)),
        "state_pool": nrm(ks[4], (DEPTH, DEC_BATCH, POOL_HIST, POOL_WIDTH)),
        "p_prompt": nrm(ks[5], (DEPTH, BATCH, SEQ, PLE_DIM)),
        "p_sample": nrm(ks[6], (DEPTH, DEC_BATCH, DEC_SEQ, PLE_DIM)),
        "ln_mix": gain(ks[7], (DEPTH, D_MODEL)),
        "w_in": nrm(ks[8], (DEPTH, D_MODEL, IN_PROJ_DIM), D_MODEL ** -0.5),
        "attn_sinks": nrm(ks[9], (DEPTH, N_HEADS), 0.5),
        "g_attn_out": gain(ks[10], (DEPTH, ATTN_WIDTH)),
        "w_pool": nrm(ks[11], (DEPTH, N_POOL_GROUPS, POOL_GROUP_DIM, POOL_GROUP_DIM), POOL_GROUP_DIM ** -0.5),
        "pool_scale": 0.5 + 0.05 * jax.random.normal(ks[12], (DEPTH, POOL_WIDTH), f32),
        "g_pool_out": gain(ks[13], (DEPTH, POOL_WIDTH)),
        "w_out": nrm(ks[14], (DEPTH, D_MODEL, D_MODEL), D_MODEL ** -0.5),
        "ln_ffn": gain(ks[15], (DEPTH, D_MODEL)),
        "w_up": nrm(ks[16], (DEPTH, D_MODEL, D_FF), D_MODEL ** -0.5),
        "w_down": nrm(ks[17], (DEPTH, D_FF, D_MODEL), D_FF ** -0.5),
        "ln_ple": gain(ks[18], (DEPTH, D_MODEL)),
        "w_ple_gate": nrm(ks[19], (DEPTH, D_MODEL, D_MODEL), D_MODEL ** -0.5),
        "w_ple_proj": nrm(ks[20], (DEPTH, PLE_DIM, D_MODEL), PLE_DIM ** -0.5),
        "ln_final": gain(ks[21], (D_MODEL,)),
    }


def reference(x_prompt, x_sample, cache_k, cache_v, state_pool, p_prompt, p_sample,
              ln_mix, w_in, attn_sinks, g_attn_out, w_pool, pool_scale, g_pool_out,
              w_out, ln_ffn, w_up, w_down, ln_ple, w_ple_gate, w_ple_proj, ln_final):
    xp, xs = x_prompt, x_sample
    kp_l, vp_l, pp_l, ks_l, vs_l, ps_l = [], [], [], [], [], []
    for i in range(DEPTH):
        shared = (ln_mix[i], w_in[i], g_attn_out[i], w_pool[i], pool_scale[i], g_pool_out[i],
                  w_out[i], ln_ffn[i], w_up[i], w_down[i], ln_ple[i], w_ple_gate[i], w_ple_proj[i])
        fn_p = functools.partial(attn_prompt, sinks=attn_sinks[i])
        hist_p = jnp.zeros((xp.shape[0], POOL_HIST, POOL_WIDTH), xp.dtype)
        xp, k_p, v_p, pool_p = trunk_layer(xp, p_prompt[i], fn_p, hist_p, 0, *shared)
        fn_s = functools.partial(attn_sample, cache_k=cache_k[i], cache_v=cache_v[i], sinks=attn_sinks[i])
        xs, k_s, v_s, pool_s = trunk_layer(xs, p_sample[i], fn_s, state_pool[i], PAST_LEN, *shared)
        kp_l.append(k_p[:, -WINDOW:])
        vp_l.append(v_p[:, -WINDOW:])
        pp_l.append(pool_p)
        ks_l.append(k_s)
        vs_l.append(v_s)
        ps_l.append(pool_s)
    y_prompt = rmsnorm(xp, ln_final)
    y_sample = rmsnorm(xs, ln_final)
    new_k_prompt = jnp.stack(kp_l)
    new_v_prompt = jnp.stack(vp_l)
    new_pool_prompt = jnp.stack(pp_l)
    new_k_sample = jnp.stack(ks_l)
    new_v_sample = jnp.stack(vs_l)
    new_pool_sample = jnp.stack(ps_l)
    return (y_prompt, y_sample, new_k_prompt, new_v_prompt, new_pool_prompt,
            new_k_sample, new_v_sample, new_pool_sample)
```

```python
import contextlib
import numpy as np
import concourse.bass as bass
import concourse.mybir as mybir
from concourse.bass_utils import run_bass_kernel_spmd

F32 = mybir.dt.float32
BF16 = mybir.dt.bfloat16
AF = mybir.ActivationFunctionType
ALU = mybir.AluOpType

NCORES = 8
D = 1024
SEG = 4096
T = 512
NT = SEG // T
NBLK = T // 128
HALO = 128
NSAMP = 64
EPS = 1e-6
NSLOT = 4
SLOT_COLS = 4096

C_LNMIX, C_LNFFN, C_LNPLE, C_GATT, C_GPOOL, C_PSCALE, C_HBIAS, C_SINK, C_INVW = 0, 8, 16, 24, 28, 32, 36, 37, 45
NCOLS = 49
import os as _os
DBG_STOP = _os.environ.get('KDBG_STOP', '')


class Sched:
    ENG = ("pe", "act", "dve", "pool", "sp")

    def __init__(self, nc, stack):
        self.nc = nc
        self.stack = stack
        self.streams = {e: [] for e in self.ENG}
        self.csem = {e: stack.enter_context(nc.semaphore("c_" + e)) for e in ("pe", "act", "dve", "pool")}
        self.dsem = {}
        self.dcount = {}

    def add(self, eng, fn, deps=()):
        lst = self.streams[eng]
        lst.append(dict(kind="c", fn=fn, deps=self._flat(deps), sig=False))
        return ("c", eng, len(lst) - 1)

    def dma_sem(self, sem):
        if sem not in self.dsem:
            self.dsem[sem] = self.stack.enter_context(self.nc.semaphore("d_" + sem))
            self.dcount[sem] = 0

    def dma(self, queue, fn, sem, deps=()):
        self.dma_sem(sem)
        self.dcount[sem] += 16
        self.streams[queue].append(dict(kind="d", fn=fn, deps=self._flat(deps), sem=sem))
        return ("d", sem, self.dcount[sem])

    def _flat(self, deps):
        out = []
        for d in deps:
            if d is None:
                continue
            if isinstance(d, list):
                out.extend(self._flat(d))
            else:
                assert isinstance(d, tuple) and d[0] in ("c", "d"), d
                out.append(d)
        return out

    def emit(self, block):
        for e in self.ENG:
            for op in self.streams[e]:
                for d in op["deps"]:
                    if d[0] == "c":
                        if d[1] == "pe" and e == "pe":
                            continue
                        self.streams[d[1]][d[2]]["sig"] = True
        sigval = {}
        for e in ("pe", "act", "dve", "pool"):
            n = 0
            for i, op in enumerate(self.streams[e]):
                if op["kind"] == "c" and op["sig"]:
                    n += 1
                    sigval[(e, i)] = n
        self.maxsig = {e: max([v for (ee, i), v in sigval.items() if ee == e] or [0]) for e in ("pe", "act", "dve", "pool")}
        self.nops = {e: len(self.streams[e]) for e in self.ENG}
        if DBG_STOP:
            print("SCHED maxsig", self.maxsig, "nops", self.nops, "dma", self.dcount)
        engobj = {"pe": block.tensor, "act": block.scalar, "dve": block.vector, "pool": block.gpsimd,
                  "sp": block.sync}

        def run(e):
            def body(eng):
                waited = {}
                for i, op in enumerate(self.streams[e]):
                    need = {}
                    for d in op["deps"]:
                        if d[0] == "c":
                            if d[1] == "pe" and e == "pe":
                                continue
                            key = ("c", d[1])
                            val = sigval[(d[1], d[2])]
                        else:
                            key = ("d", d[1])
                            val = d[2]
                        if val > need.get(key, 0):
                            need[key] = val
                    for key, val in need.items():
                        if waited.get(key, 0) >= val:
                            continue
                        sem = self.csem[key[1]] if key[0] == "c" else self.dsem[key[1]]
                        eng.wait_ge(sem, val)
                        waited[key] = val
                    ins = op["fn"](eng)
                    if op["kind"] == "d":
                        ins.then_inc(self.dsem[op["sem"]], 16)
                    elif op["sig"]:
                        ins.then_inc(self.csem[e], 1)
            engobj[e](body)

        for e in self.ENG:
            run(e)


class Banks:
    def __init__(self, tensors):
        self.t = tensors
        self.free = [[] for _ in tensors]
        self.rr = 0

    def get(self):
        i = self.rr
        self.rr = (i + 1) % len(self.t)
        return i, self.t[i], self.free[i]

    def release(self, i, toks):
        self.free[i] = list(toks) if isinstance(toks, list) else [toks]


def build_program(NT=NT):
    SEG = NT * T
    nc = bass.Bass("TRN2", target_bir_lowering=False)

    def din(name, shape, dt=F32):
        return nc.dram_tensor(name, list(shape), dt, kind="ExternalInput").ap()

    def dout(name, shape, dt=F32):
        return nc.dram_tensor(name, list(shape), dt, kind="ExternalOutput").ap()

    xin = din("xin", [HALO + SEG, D])
    pin = din("pin", [SEG, 256])
    xs_d = din("xs", [NSAMP, D])
    ps_d = din("ps", [NSAMP, 256])
    ck_d = din("ck", [4 * 128, 128])
    cv_d = din("cv", [4 * 128, 128])
    spool_d = din("spool", [60, 512])
    w_in_d = din("w_in_p", [D, 1920])
    w_out_d = din("w_out", [D, D])
    w_up_d = din("w_up", [D, 4096])
    w_down_d = din("w_down", [4096, D])
    w_gate_d = din("w_gate", [D, D])
    w_ple_d = din("w_ple", [256, D])
    w_pool_d = din("w_pool", [128, 512])
    cols_d = din("cols", [128, NCOLS])
    lnf_d = din("lnf", [128, D])
    invcnt_d = din("invcnt", [128, 64])
    bmask_d = din("bmask", [64, 64])
    ident_d = din("ident", [128, 128])

    y_d = dout("y", [SEG, D])
    ys_d = dout("ys", [NSAMP, D])
    kvu_last_d = dout("kvu_last", [128, 768])
    kvu_s_d = dout("kvu_s", [NSAMP, 768])

    wblocks = {}

    def wblk(name, src3, k, c):
        sc = nc.dram_tensor("sc_" + name, [128, k * c], BF16, kind="Internal").ap()
        wblocks[name] = dict(src=src3, sc=sc, k=k, c=c)

    w_in3 = w_in_d.rearrange("(k p) n -> p k n", p=128)
    wblk("inq", w_in3[:, :, 0:512], 8, 512)
    wblk("inku", w_in3[:, :, 512:1024], 8, 512)
    wblk("inuv", w_in3[:, :, 1024:1280], 8, 256)
    wblk("int1", w_in3[:, :, 1280:1792], 8, 512)
    wblk("int2", w_in3[:, :, 1792:1920], 8, 128)
    w_out3 = w_out_d.rearrange("(k p) n -> p k n", p=128)
    for hf in range(2):
        wblk(f"out{hf}", w_out3[:, :, hf * 512:(hf + 1) * 512], 8, 512)
    w_up3 = w_up_d.rearrange("(k p) n -> p k n", p=128)
    w_dn3 = w_down_d.rearrange("(k p) n -> p k n", p=128)
    for fh in range(2):
        for i in range(4):
            j = fh * 4 + i
            wblk(f"up{j}", w_up3[:, :, j * 512:(j + 1) * 512], 8, 512)
        for ch in range(2):
            for kq in range(2):
                k0 = fh * 16 + kq * 8
                wblk(f"dn{fh}{ch}{kq}", w_dn3[:, k0:k0 + 8, ch * 512:(ch + 1) * 512], 8, 512)
    w_g3 = w_gate_d.rearrange("(k p) n -> p k n", p=128)
    w_p3 = w_ple_d.rearrange("(k p) n -> p k n", p=128)
    wblk("ple", w_p3[:, :, :], 2, 1024)
    for hf in range(2):
        wblk(f"gate{hf}", w_g3[:, :, hf * 512:(hf + 1) * 512], 8, 512)

    def tile_wseq(kind):
        seq = ["inq", "inku", "inuv"]
        if kind in ("last", "sample"):
            seq += ["int1", "int2"]
        seq += ["out0", "out1"]
        for fh in range(2):
            seq += [f"up{fh * 4 + i}" for i in range(4)]
            seq += [f"dn{fh}{ch}{kq}" for ch in range(2) for kq in range(2)]
        seq += ["ple", "gate0", "gate1"]
        return seq

    tiles = []
    for t in range(NT):
        kind = "first" if t == 0 else ("last" if t == NT - 1 else "mid")
        tiles.append(dict(kind=kind, idx=t))
    tiles.append(dict(kind="sample", idx=NT))
    wseq = []
    for tl in tiles:
        wseq += tile_wseq(tl["kind"])

    with contextlib.ExitStack() as stack:
        def sb(name, shape, dt):
            return stack.enter_context(nc.sbuf_tensor("s_" + name, list(shape), dt))

        S = Sched(nc, stack)

        xbuf = [sb(f"xbuf{i}", [128, NBLK, D], F32) for i in range(2)]
        pbuf = [sb(f"pbuf{i}", [128, NBLK, 256], BF16) for i in range(2)]
        junk = sb("junk", [128, D], BF16)
        sqj = sb("sqj", [128, 512], BF16)
        xsb = [sb(f"xsb{i}", [128, D], BF16) for i in range(2)]
        hTA = sb("hTA", [128, 8, HALO + T], BF16)
        hT = sb("hT", [128, 8, T], BF16)
        QT = sb("QT", [128, 4, T], BF16)
        KT2 = [sb(f"KT{i}", [128, HALO + T], BF16) for i in range(2)]
        KTs = sb("KTs", [128, NSAMP], BF16)
        V12 = [sb(f"V1_{i}", [128, NBLK + 1, 2, 65], BF16) for i in range(2)]
        V1s = sb("V1s", [128, 2, 65], BF16)
        uT = sb("uT", [128, 4, 16 + T], F32)
        ptmp = [sb(f"ptmp{i}", [128, 16 + T], F32) for i in range(2)]
        dT = sb("dT", [128, 4, T], BF16)
        uTb = sb("uTb", [128, 4, T], BF16)
        wpool_n = sb("wpool_n", [128, 512], BF16)
        psgw = sb("psgw", [128, 4], F32)
        pscw = sb("pscw", [128, 4], F32)
        PT = [sb(f"PT{i}", [128, 2048], BF16) for i in range(2)]
        anb = [sb(f"anb{i}", [128, 512], BF16) for i in range(2)]
        mixT = sb("mixT", [128, 8, T], BF16)
        sqb = sb("sqb", [128, 4, T], BF16)
        hidT = sb("hidT", [128, 16, T], BF16)
        tg = [sb(f"tg{i}", [128, 512], F32) for i in range(2)]
        tp = [sb(f"tp{i}", [128, 512], F32) for i in range(1)]
        pT = sb("pT", [128, 2, T], BF16)
        wring = [sb(f"wring{i}", [128, SLOT_COLS], BF16) for i in range(NSLOT)]
        stage = sb("stage", [128, 768], F32)
        lnf = sb("lnf", [128, D], F32)
        cols = sb("cols", [128, NCOLS], F32)
        psg = sb("psg", [128, 4], F32)
        esink = sb("esink", [128, 8], F32)
        invcnt = sb("invcnt", [128, 64], F32)
        bmask = sb("bmask", [64, 64], F32)
        identf = sb("identf", [128, 128], F32)
        identb = sb("identb", [128, 128], BF16)
        wpool = sb("wpool", [128, 512], BF16)
        ones = sb("ones", [128, 1], BF16)
        epsb = sb("epsb", [128, 1], F32)
        oneb = sb("oneb", [128, 1], F32)
        fint = sb("fint", [128, 1], F32)
        NSS = 384
        ssb = sb("ssb", [128, NSS], F32)
        vvb = sb("vvb", [128, NSS], F32)
        rsb = sb("rsb", [128, NSS], F32)
        den = [sb(f"den{i}", [128, 8], F32) for i in range(2)]
        rcp = [sb(f"rcp{i}", [128, 8], F32) for i in range(2)]
        ckf = sb("ckf", [128, 4, 128], F32)
        cvf = sb("cvf", [128, 4, 128], F32)
        spf = sb("spf", [64, 512], F32)
        ckb = sb("ckb", [128, 4, 128], BF16)
        spb = sb("spb", [64, 512], BF16)
        KcT = sb("KcT", [128, 4, 128], BF16)
        Vc1 = sb("Vc1", [128, 4, 2, 65], BF16)
        uTs = sb("uTs", [128, 4, 4, 32], F32)
        ptmps = [sb(f"ptmps{i}", [128, 4, 32], F32) for i in range(2)]
        etmp = sb("etmp", [64, 512], F32)

        pbank = [stack.enter_context(nc.psum_tensor(f"pb{i}", [128, 512], F32)) for i in range(8)]
        banks = Banks(pbank)
        block = stack.enter_context(nc.Block())

        ss_ctr = [0]

        def ss_col():
            c = ss_ctr[0]
            ss_ctr[0] += 1
            assert c < NSS
            return c

        conv_tok = {}
        conv_order = list(dict.fromkeys(wseq))
        for name in conv_order:
            S.dma_sem("cv_" + name)
            conv_tok[name] = ("d", "cv_" + name, 16)
        conv_issued = [0]

        def issue_convs(n):
            while conv_issued[0] < min(n, len(conv_order)):
                name = conv_order[conv_issued[0]]
                wb = wblocks[name]
                dst = wb["sc"].rearrange("p (k c) -> p k c", k=wb["k"])
                tok = S.dma("pool", (lambda eng, dst=dst, src=wb["src"]: eng.dma_start(out=dst, in_=src)),
                            "cv_" + name)
                assert tok == conv_tok[name]
                conv_issued[0] += 1

        def _setup_pool(e):
            e.memset(ones[:], 1.0)
            e.memset(epsb[:], EPS)
            e.memset(oneb[:], 1.0)
            e.memset(ssb[:], 0.0)
            e.memset(V12[0][:], 1.0)
            e.memset(V12[1][:], 1.0)
            e.memset(V1s[:], 1.0)
            e.memset(Vc1[:], 1.0)
            e.memset(uTs[:], 0.0)
            e.memset(PT[0][:, :], 0.0)
            e.memset(PT[1][:, :], 0.0)
            return e.memset(uT[:], 0.0)
        issue_convs(1)
        t_setup = S.add("pool", _setup_pool)

        c_tok = []
        for dst, src in ((cols[:], cols_d[:, :]), (lnf[:], lnf_d[:, :]), (invcnt[:], invcnt_d[:, :]),
                         (bmask[:], bmask_d[:, :]), (identf[:], ident_d[:, :])):
            c_tok.append(S.dma("sp", (lambda eng, dst=dst, src=src: eng.dma_start(out=dst, in_=src)), "const"))
        const_tok = c_tok[-1]

        t_ident = S.add("pool", lambda e: e.tensor_copy(identb[:], identf[:]), [const_tok])
        issue_convs(3)
        wp_tok = S.dma("pool", lambda eng: eng.dma_start(out=wpool[:], in_=w_pool_d[:, :]), "wpool")
        issue_convs(5)
        if DBG_STOP:
            issue_convs(len(conv_order))

        t_psg = S.add("dve", lambda e: e.tensor_tensor(out=psg[:], in0=cols[:, C_PSCALE:C_PSCALE + 4],
                                                       in1=cols[:, C_GPOOL:C_GPOOL + 4], op=ALU.mult), [const_tok])
        t_esink = S.add("act", lambda e: e.activation(out=esink[:], in_=cols[:, C_SINK:C_SINK + 8], func=AF.Exp),
                        [const_tok])
        t_psgw = S.add("dve", lambda e: e.tensor_tensor(out=psgw[:], in0=psg[:], in1=cols[:, C_INVW:C_INVW + 4],
                                                        op=ALU.mult), [t_psg, const_tok])
        t_pscw = S.add("dve", lambda e: e.tensor_tensor(out=pscw[:], in0=cols[:, C_PSCALE:C_PSCALE + 4],
                                                        in1=cols[:, C_INVW:C_INVW + 4], op=ALU.mult), [const_tok])
        t_wpn = None
        for g_ in range(4):
            t_wpn = S.add("dve", lambda e, g_=g_: e.tensor_scalar_mul(
                out=wpool_n[:, g_ * 128:(g_ + 1) * 128], in0=wpool[:, g_ * 128:(g_ + 1) * 128],
                scalar1=-float(2 << g_)), [wp_tok] + ([t_wpn] if t_wpn else []))

        wstate = dict(next_load=0, next_use=0, done=set(), slot_free={}, loaded={})

        def w_issue_loads():
            while wstate["next_load"] < len(wseq):
                i = wstate["next_load"]
                if i >= NSLOT and (i - NSLOT) not in wstate["done"]:
                    break
                name = wseq[i]
                wb = wblocks[name]
                slot = i % NSLOT
                n = wb["k"] * wb["c"]
                deps = [conv_tok[name]] + wstate["slot_free"].get(i - NSLOT, [])
                tok = S.dma("sp", (lambda eng, slot=slot, n=n, sc=wb["sc"]:
                                   eng.dma_start(out=wring[slot][:, 0:n], in_=sc[:, :])), f"w{slot}", deps)
                wstate["loaded"][i] = tok
                wstate["next_load"] += 1

        def w_next(expect):
            i = wstate["next_use"]
            assert wseq[i] == expect, (wseq[i], expect)
            w_issue_loads()
            assert i in wstate["loaded"], ("weight ring over-subscribed", i, expect)
            wstate["next_use"] += 1
            wb = wblocks[expect]
            slot = i % NSLOT
            view = wring[slot][:, 0:wb["k"] * wb["c"]].rearrange("p (k c) -> p k c", k=wb["k"])
            return view, wstate["loaded"][i], i

        def w_done(i, tok):
            wstate["slot_free"][i] = [tok]
            wstate["done"].add(i)
            w_issue_loads()

        def rstd_chain(ss_ap_fn, ss_tok, col, ncol, n, width, escale=-0.5):
            t1 = S.add("act", lambda e: e.activation(out=vvb[0:n, col:col + ncol], in_=ss_ap_fn(),
                                                     func=AF.Ln, bias=epsb[0:n, :], scale=1.0 / width),
                       [ss_tok, t_setup])
            t2 = S.add("act", lambda e: e.activation(out=rsb[0:n, col:col + ncol], in_=vvb[0:n, col:col + ncol],
                                                     func=AF.Exp, scale=escale), [t1])
            return t2

        xsb_free = [[], []]
        xsb_rr = [0]

        def norm_p1(xap, n, x_deps, defer=None):
            col = ss_col()
            i = xsb_rr[0]
            xsb_rr[0] ^= 1
            if defer is not None:
                t_xs = S.add("act", lambda e: e.activation(out=xsb[i][0:n, :], in_=xap, func=AF.Copy),
                             x_deps + xsb_free[i])
                t_ss = S.add("act", lambda e: e.activation(out=junk[0:n, :], in_=xap, func=AF.Square,
                                                           accum_out=ssb[0:n, col:col + 1]), x_deps + [t_setup])
                t_r = rstd_chain(lambda: ssb[0:n, col:col + 1], t_ss, col, 1, n, D, escale=defer)
                return dict(i=i, t_xs=t_ss, t_cast=t_xs, n=n, col=col, t_r=t_r)
            t_ss = S.add("act", lambda e: e.activation(out=junk[0:n, :], in_=xap, func=AF.Square,
                                                       accum_out=ssb[0:n, col:col + 1]), x_deps + [t_setup])
            t_r = rstd_chain(lambda: ssb[0:n, col:col + 1], t_ss, col, 1, n, D)
            t_xs = S.add("act", lambda e: e.activation(out=xsb[i][0:n, :], in_=xap, func=AF.Copy,
                                                       scale=rsb[0:n, col:col + 1]), [t_r] + xsb_free[i])
            return dict(i=i, t_xs=t_xs, t_cast=t_xs, n=n, col=col, t_r=t_r)

        def norm_p2(ctx, gcol, dst, hcol0, hT_war):
            i, t_xs, n = ctx["i"], ctx["t_cast"], ctx["n"]
            bi, bk, bfree = banks.get()
            bkb = bk.bitcast(BF16)

            def _tr(e):
                ins = None
                for c in range(8):
                    ins = e.transpose(bkb[:, c * 128:c * 128 + n], xsb[i][0:n, c * 128:(c + 1) * 128],
                                      identb[0:n, 0:n])
                return ins
            t_tr = S.add("pe", _tr, [t_xs, t_ident] + bfree)
            xsb_free[i] = [t_tr]
            src = bkb[:, :].rearrange("p (c t) -> p c t", c=8)[:, :, 0:n]
            gap = cols[:, gcol:gcol + 8].unsqueeze(2).to_broadcast([128, 8, n])
            t_h = S.add("dve", lambda e: e.tensor_tensor(out=dst[:, :, hcol0:hcol0 + n], in0=src, in1=gap,
                                                         op=ALU.mult), [t_tr, const_tok] + hT_war)
            banks.release(bi, t_h)
            return t_h

        def norm_to_hT(xap, n, gcol, hcol0, x_deps, hT_war, dst=None, defer=None, info=None):
            ctx = norm_p1(xap, n, x_deps, defer=defer)
            t_h = norm_p2(ctx, gcol, hT if dst is None else dst, hcol0, hT_war)
            if info is not None:
                info.update(col=ctx["col"], t_r=ctx["t_r"])
            return t_h, ctx["t_xs"]

        hT_readers = []
        hTA_readers = []
        x_ld = {}
        p_ld = {}
        y_st = {}
        out_toks = []
        pt_free = [[], []]
        ab_free = [[], []]
        an_free = [[], []]
        tg_free = [[], []]
        tp_free = [[]]
        state = dict(mix_readers=[], hid_readers=[], QT_readers=[], uT_readers=[], dT_readers=[], sq_readers=[],
                     pT_readers=[], stage_free=[], ptmp_free=[], uT_halo=[], KTs_readers=[], halo_read=[], pre={},
                     kt_readers=[[], []], v_readers=[[], []], rstda={}, uTb_readers=[])

        def issue_x_load(t):
            bi = t % 2
            deps = []
            if t - 2 in y_st:
                deps.append(y_st[t - 2])
            if t == 1:
                deps += state["halo_read"]
            pdeps = list(state["pT_readers"]) if t >= 2 else []
            if t < NT:
                src = xin[HALO + t * T: HALO + (t + 1) * T, :].rearrange("(b p) d -> p b d", p=128)
                x_ld[t] = S.dma("sp", lambda eng: eng.dma_start(out=xbuf[bi][:, :, :], in_=src), f"x{bi}", deps)
                psrc = pin[t * T:(t + 1) * T, :].rearrange("(b p) d -> p b d", p=128)
                p_ld[t] = S.dma("pool", lambda eng: eng.dma_start(out=pbuf[bi][:, :, :], in_=psrc), f"p{bi}", pdeps)
            else:
                x_ld[t] = S.dma("sp", lambda eng: eng.dma_start(out=xbuf[bi][0:NSAMP, 0, :], in_=xs_d[:, :]),
                                f"x{bi}", deps)
                p_ld[t] = S.dma("pool", lambda eng: eng.dma_start(out=pbuf[bi][0:NSAMP, 0, :], in_=ps_d[:, :]),
                                f"p{bi}", pdeps)

        xhalo = xbuf[1][:, 0, :]
        halo_tok = S.dma("sp", lambda eng: eng.dma_start(out=xhalo, in_=xin[0:HALO, :]), "x1")
        issue_x_load(0)

        S.dma("sp", lambda eng: eng.dma_start(out=ckf[:, :, :], in_=ck_d.rearrange("(s p) f -> p s f", p=128)), "ckv")
        S.dma("sp", lambda eng: eng.dma_start(out=cvf[:, :, :], in_=cv_d.rearrange("(s p) f -> p s f", p=128)), "ckv")
        ckv_tok = S.dma("sp", lambda eng: eng.dma_start(out=spf[0:60, :], in_=spool_d[:, :]), "ckv")

        def attn_post_a(ob, tpv, n, seq):
            i = seq % 2
            o3 = [ob[0][1][0:n, 0:260].rearrange("p (h d) -> p h d", h=4),
                  ob[1][1][0:n, 0:260].rearrange("p (h d) -> p h d", h=4)]
            tds = []
            for hb in range(2):
                td = S.add("dve", lambda e, hb=hb: e.tensor_tensor(
                    out=den[i][0:n, hb * 4:(hb + 1) * 4].unsqueeze(2), in0=o3[hb][:, :, 64:65],
                    in1=esink[0:n, hb * 4:(hb + 1) * 4].unsqueeze(2), op=ALU.add), [tpv, t_esink] + ab_free[i])
                tds.append(td)
            trc = S.add("dve", lambda e: e.reciprocal(out=rcp[i][0:n, :], in_=den[i][0:n, :]), tds)
            tas = []
            for hb in range(2):
                ta = S.add("dve", lambda e, hb=hb: e.tensor_tensor(
                    out=anb[i][0:n, hb * 256:(hb + 1) * 256].rearrange("p (h d) -> p h d", h=4),
                    in0=o3[hb][:, :, 0:64],
                    in1=rcp[i][0:n, hb * 4:(hb + 1) * 4].unsqueeze(2).to_broadcast([n, 4, 64]), op=ALU.mult),
                    [trc] + an_free[i])
                tas.append(ta)
                banks.release(ob[hb][0], [ta, tds[hb]])
            col = ss_col()
            t_ss = S.add("dve", lambda e: e.scalar_tensor_tensor(
                out=sqj[0:n, :], in0=anb[i][0:n, :], scalar=1.0, in1=anb[i][0:n, :], op0=ALU.mult, op1=ALU.mult,
                accum_out=ssb[0:n, col:col + 1]), tas + [t_setup])
            t_r = rstd_chain(lambda: ssb[0:n, col:col + 1], t_ss, col, 1, n, 512)
            ab_free[i] = tas
            return dict(tas=tas + [t_ss], col=col, t_r=t_r)

        def attn_post_b(actx, n, b, seq, mix_war):
            i = seq % 2
            t_an = actx["tas"]
            state["rstda"][b] = (actx["col"], actx["t_r"])
            bi, bk, bfree = banks.get()
            bkb = bk.bitcast(BF16)

            def _tr(e):
                ins = None
                for c in range(4):
                    ins = e.transpose(bkb[:, c * 128:c * 128 + n], anb[i][0:n, c * 128:(c + 1) * 128],
                                      identb[0:n, 0:n])
                return ins
            t_tr = S.add("pe", _tr, t_an + [t_ident] + bfree)
            an_free[i] = [t_tr]
            src = bkb[:, 0:512].rearrange("p (c t) -> p c t", c=4)[:, :, 0:n]
            gap = cols[:, C_GATT:C_GATT + 4].unsqueeze(2).to_broadcast([128, 4, n])
            t_m = S.add("dve", lambda e: e.tensor_tensor(out=mixT[:, 0:4, b * 128:b * 128 + n], in0=src, in1=gap,
                                                         op=ALU.mult), [t_tr, const_tok] + mix_war)
            banks.release(bi, t_m)
            return t_m

        def attn_post(ob, tpv, n, b, seq, mix_war):
            return attn_post_b(attn_post_a(ob, tpv, n, seq), n, b, seq, mix_war)

        def prompt_attn_scores(t, b, q_toks, k_tok):
            gb = 1 + t * NBLK + b
            seq = t * NBLK + b
            pi = seq % 2
            ptv = PT[pi][:, :].rearrange("p (kb kv h q) -> p kb kv h q", kb=2, kv=2, h=4)
            exp_toks = []

            def score(kv, kbi, kb):
                bi, bk, bfree = banks.get()
                kdeps = state["k_halo_tok"] if kb == 0 else []
                tm = S.add("pe", lambda e: e.matmul(
                    bk[:, 0:512].rearrange("p (h q) -> p h q", h=4),
                    lhsT=KT2[t % 2][kv * 64:(kv + 1) * 64, (b + kbi) * 128:(b + kbi + 1) * 128],
                    rhs=QT[kv * 64:(kv + 1) * 64, :, b * 128:(b + 1) * 128], start=True, stop=True),
                    [k_tok] + q_toks + kdeps + bfree)
                state["QT_readers"].append(tm)
                state["kt_readers"][t % 2] = [tm]
                bkv = bk[:, 0:512].rearrange("p (h q) -> p h q", h=4)
                deps = [tm, const_tok, t_setup] + pt_free[pi]
                if kbi == 0:
                    if kb == 0:
                        ta = S.add("act", lambda e: e.activation(
                            out=ptv[:, 0, kv, :, 0:64], in_=bkv[:, :, 0:64], func=AF.Exp,
                            bias=cols[:, C_HBIAS:C_HBIAS + 1], scale=0.125), deps)
                        tb = S.add("act", lambda e: e.activation(
                            out=ptv[64:128, 0, kv, :, 64:128], in_=bkv[64:128, :, 64:128], func=AF.Exp,
                            bias=cols[64:128, C_HBIAS:C_HBIAS + 1], scale=0.125), deps)
                    else:
                        ta = S.add("act", lambda e: e.activation(
                            out=ptv[:, 0, kv, :, 0:64], in_=bkv[:, :, 0:64], func=AF.Exp, scale=0.125), deps)
                        tb = S.add("act", lambda e: e.activation(
                            out=ptv[64:128, 0, kv, :, 64:128], in_=bkv[64:128, :, 64:128], func=AF.Exp,
                            scale=0.125), deps)
                else:
                    ta = S.add("act", lambda e: e.activation(
                        out=ptv[:, 1, kv, :, 64:128], in_=bkv[:, :, 64:128], func=AF.Exp, scale=0.125), deps)
                    tb = S.add("act", lambda e: e.activation(
                        out=ptv[0:64, 1, kv, :, 0:64], in_=bkv[0:64, :, 0:64], func=AF.Exp, scale=0.125), deps)
                banks.release(bi, [ta, tb])
                exp_toks.extend([ta, tb])

            for kv in range(2):
                for kbi, kb in enumerate((gb - 1, gb)):
                    score(kv, kbi, kb)
            return dict(gb=gb, seq=seq, pi=pi, ptv=ptv, exp_toks=exp_toks, b=b, t=t)

        def prompt_attn_pv(ctx, v_toks):
            gb, seq, pi, ptv, exp_toks, b, t = (ctx[k] for k in ("gb", "seq", "pi", "ptv", "exp_toks", "b", "t"))
            ob = [banks.get(), banks.get()]
            vdeps = [v_toks[b]]
            if b == 0:
                vdeps.append(state["v_prev_tok"] if t > 0 else v_toks["halo"])
            else:
                vdeps.append(v_toks[b - 1])

            def _pv(e):
                ins = None
                for h in range(8):
                    kv, g = divmod(h, 4)
                    o = ob[h // 4][1]
                    for kbi, kb in enumerate((gb - 1, gb)):
                        ins = e.matmul(o[:, (h % 4) * 65:(h % 4) * 65 + 65], lhsT=ptv[:, kbi, kv, g, :],
                                       rhs=V12[t % 2][:, b + kbi, kv, :], start=(kbi == 0), stop=(kbi == 1))
                return ins
            tpv = S.add("pe", _pv, exp_toks + vdeps + ob[0][2] + ob[1][2])
            pt_free[pi] = [tpv]
            state["v_readers"][t % 2] = [tpv]
            return attn_post_a(ob, tpv, 128, seq)

        def sample_attention(q_toks, k_tok, v_tok, mix_war):
            n = NSAMP
            kc_toks = []

            tcb = S.add("pool", lambda e: e.tensor_copy(ckb[:, :, :], ckf[:, :, :]), [ckv_tok])
            bi, bk, bfree = banks.get()
            bkb = bk.bitcast(BF16)

            def _ktr(e):
                ins = None
                for s in range(4):
                    ins = e.transpose(bkb[:, s * 128:(s + 1) * 128], ckb[:, s, :], identb[:, :])
                return ins
            tm = S.add("pe", _ktr, [tcb, t_ident] + bfree)
            te = S.add("dve", lambda e: e.tensor_copy(KcT[:, :, :], bkb[:, 0:512].rearrange("p (s k) -> p s k", s=4)),
                       [tm])
            banks.release(bi, te)
            kc_toks.append(te)
            tvc = S.add("dve", lambda e: e.tensor_copy(Vc1[:, :, :, 0:64],
                                                       cvf[:, :, :].rearrange("p s (k d) -> p s k d", k=2)),
                        [ckv_tok, t_setup])
            ptc = PT[0][:, :].rearrange("p (s h q) -> p s h q", s=4, h=8)
            if DBG_STOP == "2:SA1":
                return None
            ptn = PT[1][0:64, 0:512].rearrange("p (h q) -> p h q", h=8)
            tz = S.add("pool", lambda e: e.memset(PT[0][:, :], 0.0), pt_free[0])
            if DBG_STOP == "2:SA1b":
                return None
            cs = [[banks.get() for half in range(2)] for kv in range(2)]

            def _sc(e):
                ins = None
                for s in range(4):
                    for kv in range(2):
                        o = cs[kv][s // 2][1][:, (s % 2) * 256:(s % 2 + 1) * 256].rearrange("p (h q) -> p h q", h=4)
                        ins = e.matmul(o, lhsT=KcT[kv * 64:(kv + 1) * 64, s, :],
                                       rhs=QT[kv * 64:(kv + 1) * 64, :, 0:64], start=True, stop=True)
                return ins
            tsc = S.add("pe", _sc, kc_toks + q_toks + [x for r in cs for bkx in r for x in bkx[2]])
            state["QT_readers"].append(tsc)
            if DBG_STOP == "2:SA2a":
                S.add("dve", lambda e: e.tensor_copy(etmp[:, 0:64], cs[0][0][1][0:64, 0:64]), [tsc])
                return None
            e_toks = []
            rel = {(kv, half): [] for kv in range(2) for half in range(2)}
            for s in range(4):
                for kv in range(2):
                    src = cs[kv][s // 2][1][:, (s % 2) * 256:(s % 2 + 1) * 256].rearrange(
                        "p (h q) -> p h q", h=4)[:, :, s * 16:(s + 1) * 16]
                    te = S.add("act", lambda e, s=s, kv=kv, src=src: e.activation(
                        out=ptc[:, s, kv * 4:(kv + 1) * 4, s * 16:(s + 1) * 16], in_=src,
                        func=AF.Exp, scale=0.125), [tsc, tz])
                    e_toks.append(te)
                    rel[(kv, s // 2)].append(te)
            for kv in range(2):
                for half in range(2):
                    banks.release(cs[kv][half][0], rel[(kv, half)])
            if DBG_STOP == "2:SA2":
                return None
            ns = [banks.get() for kv in range(2)]

            def _sn(e):
                ins = None
                for kv in range(2):
                    ins = e.matmul(ns[kv][1][0:64, 0:256].rearrange("p (h q) -> p h q", h=4),
                                   lhsT=KTs[kv * 64:(kv + 1) * 64, 0:64],
                                   rhs=QT[kv * 64:(kv + 1) * 64, :, 0:64], start=True, stop=True)
                return ins
            tsn = S.add("pe", _sn, [k_tok] + q_toks + ns[0][2] + ns[1][2])
            state["QT_readers"].append(tsn)
            state["KTs_readers"] = [tsn]
            tens = []
            for kv in range(2):
                ten = S.add("act", lambda e, kv=kv: e.activation(out=etmp[:, kv * 256:(kv + 1) * 256],
                                                                 in_=ns[kv][1][0:64, 0:256], func=AF.Exp, scale=0.125),
                            [tsn])
                banks.release(ns[kv][0], ten)
                tens.append(ten)
            tmk = S.add("dve", lambda e: e.tensor_tensor(
                out=ptn, in0=etmp[:, :].rearrange("p (h q) -> p h q", h=8),
                in1=bmask[:, :].unsqueeze(1).to_broadcast([64, 8, 64]), op=ALU.mult), tens + [const_tok] + pt_free[1])
            if DBG_STOP == "2:SA3":
                return None
            ob = [banks.get(), banks.get()]

            def _pv(e):
                ins = None
                for h in range(8):
                    kv = h // 4
                    o = ob[h // 4][1]
                    oc = o[0:64, (h % 4) * 65:(h % 4) * 65 + 65]
                    for s in range(4):
                        ins = e.matmul(oc, lhsT=ptc[:, s, h, :], rhs=Vc1[:, s, kv, :], start=(s == 0), stop=False)
                    ins = e.matmul(oc, lhsT=ptn[:, h, :], rhs=V1s[0:64, kv, :], start=False, stop=True)
                return ins
            tpv = S.add("pe", _pv, e_toks + [tmk, tvc, v_tok] + ob[0][2] + ob[1][2])
            pt_free[0] = [tpv]
            pt_free[1] = [tpv]
            if DBG_STOP == "2:SA4":
                return None
            return attn_post(ob, tpv, n, 0, 0, mix_war)

        def pooling(kind, TT, blocks, u_toks, mix_war, pre_group=None):
            sample = kind == "sample"
            if sample:
                def uview(g):
                    return uTs[:, g, :, :]
                tmpv = [ptmps[0][:, :, :], ptmps[1][:, :, :]]
                L = 32
            else:
                def uview(g):
                    return uT[:, g, :].unsqueeze(1)
                tmpv = [ptmp[0][:, :].unsqueeze(1), ptmp[1][:, :].unsqueeze(1)]
                L = 16 + T
            nseg = 4 if sample else 1
            d_war = state["dT_readers"]
            state["dT_readers"] = []
            d_toks = []
            prev = list(state["ptmp_free"])

            def group(g, prev):
                w = 2 << g
                src = uview(g)
                cur_tok = list(u_toks[g]) + (state["spT_tok"] if sample else state["uT_halo"])
                lo = 0
                shift = 1
                step = 0
                if sample:
                    dstd = dT[:, g, 0:TT].rearrange("p (s j) -> p s j", s=4)
                else:
                    dstd = dT[:, g, 0:TT].unsqueeze(1)
                while shift < w:
                    last = (shift * 2 >= w)
                    if last and kind != "first":
                        tk = S.add("pool", (lambda e, src=src, shift=shift: e.tensor_tensor(
                            out=dstd, in0=src[:, :, 16:L], in1=src[:, :, 16 - shift:L - shift], op=ALU.add)),
                            cur_tok + prev + d_war)
                        return [tk]
                    dst = tmpv[step % 2]
                    nlo = lo + shift
                    tk = S.add("pool", (lambda e, dst=dst, src=src, lo=lo, nlo=nlo, shift=shift: e.tensor_tensor(
                        out=dst[:, :, nlo:L], in0=src[:, :, nlo:L], in1=src[:, :, lo:L - shift], op=ALU.add)),
                        cur_tok + prev)
                    prev = []
                    cur_tok = [tk]
                    src = dst
                    lo = nlo
                    shift *= 2
                    step += 1
                s_new = src[:, :, 16:L]
                ic = invcnt[:, g * 16:(g + 1) * 16].unsqueeze(1)
                t1 = S.add("pool", lambda e: e.tensor_tensor(
                    out=s_new[:, :, 0:16], in0=s_new[:, :, 0:16], in1=ic, op=ALU.mult), cur_tok + [const_tok])
                t2 = S.add("pool", lambda e: e.tensor_copy(dstd, s_new), [t1] + d_war)
                return [t2]

            d_by_g = {}
            for g in (3, 2, 1, 0):
                if pre_group is not None:
                    pre_group(g)
                prev = group(g, prev)
                d_by_g[g] = prev
            d_toks = [d_by_g[g] for g in range(4)]
            state["ptmp_free"] = prev
            alld = [x for d in d_toks for x in d]
            if not sample:
                tc = S.add("pool", lambda e: e.tensor_copy(uT[:, :, 0:16], uT[:, :, T:T + 16]), alld)
                state["uT_readers"] = [tc]
                state["uT_halo"] = [tc]
            else:
                state["uT_readers"] = alld
            return d_toks

        def pooling2(kind, TT, blocks, d_toks, mix_war):
            sq_war = state["sq_readers"]
            state["sq_readers"] = []
            state["uTb_readers"] = []
            pool_toks = []
            sq_toks = []

            def pmm(g):
                bi, bk, bfree = banks.get()
                def _pm(e):
                    e.matmul(bk[:, 0:TT], lhsT=wpool[:, g * 128:(g + 1) * 128], rhs=dT[:, g, 0:TT],
                             start=True, stop=False)
                    return e.matmul(bk[:, 0:TT], lhsT=wpool_n[:, g * 128:(g + 1) * 128], rhs=uTb[:, g, 0:TT],
                                    start=False, stop=True)
                tm = S.add("pe", _pm, d_toks[g] + [wp_tok, t_wpn, state["ub_toks"][g]] + bfree)
                state["dT_readers"].append(tm)
                state["uTb_readers"].append(tm)
                t1 = S.add("act", lambda e: e.activation(out=mixT[:, 4 + g, 0:TT], in_=bk[:, 0:TT],
                                                         func=AF.Copy, scale=psgw[:, g:g + 1]),
                           [tm, t_psgw] + mix_war)
                t2 = S.add("act", lambda e: e.activation(out=sqb[:, g, 0:TT], in_=bk[:, 0:TT], func=AF.Square,
                                                         scale=pscw[:, g:g + 1]),
                           [tm, t_pscw] + sq_war)
                banks.release(bi, [t1, t2])
                pool_toks.append(t1)
                sq_toks.append(t2)
            for g in range(4):
                pmm(g)
            nb = len(blocks)
            col0 = ss_ctr[0]
            cols_b = {}
            for (b, n) in blocks:
                cols_b[b] = ss_col()
            return pool_toks, dict(sq_toks=sq_toks, col0=col0, cols_b=cols_b, nb=nb, blocks=blocks)

        def pooling3(ctx3):
            sq_toks, col0, cols_b, nb, blocks = (ctx3[k] for k in ("sq_toks", "col0", "cols_b", "nb", "blocks"))
            bi, bk, bfree = banks.get()

            def _ss(e):
                ins = None
                for (b, n) in blocks:
                    for g in range(4):
                        ins = e.matmul(bk[0:n, b:b + 1], lhsT=sqb[:, g, b * 128:b * 128 + n], rhs=ones[:, 0:1],
                                       start=(g == 0), stop=(g == 3))
                return ins
            tss = S.add("pe", _ss, sq_toks + [t_setup] + bfree)
            state["sq_readers"].append(tss)
            nmax = blocks[0][1]
            tr = rstd_chain(lambda: bk[0:nmax, 0:nb], tss, col0, nb, nmax, 512)
            banks.release(bi, tr)
            return {b: (cols_b[b], tr) for (b, n) in blocks}

        def run_tile(tl):
            kind = tl["kind"]
            t = tl["idx"]
            sample = kind == "sample"
            xb = xbuf[t % 2]
            pb = pbuf[t % 2]
            if sample:
                blocks = [(0, NSAMP)]
                TT = NSAMP
            else:
                blocks = [(b, 128) for b in range(NBLK)]
                TT = T
            xtok = x_ld[t]
            H0 = HALO

            war = list(hTA_readers)
            hTA_readers.clear()
            h_toks = []
            x_toks = {b: [xtok] for (b, n) in blocks}
            if kind == "first":
                th, txs = norm_to_hT(xhalo, 128, C_LNMIX, 0, [halo_tok], war, dst=hTA)
                h_toks.append(th)
                state["halo_read"] = [txs]
            if t + 1 <= NT:
                issue_x_load(t + 1)
            if t in state["pre"]:
                pre = state["pre"].pop(t)
                h_toks.extend(pre["h_toks"])
                for (b, n) in blocks:
                    x_toks[b].append(pre["txs"][b])
            else:
                for (b, n) in blocks:
                    th, txs = norm_to_hT(xb[0:n, b, :], n, C_LNMIX, H0 + b * 128, [xtok], war, dst=hTA)
                    h_toks.append(th)
                    x_toks[b].append(txs)

            if DBG_STOP == f"{t}:A":
                return
            wvq, wtokq, wiq = w_next("inq")
            qt_war = state["QT_readers"]
            state["QT_readers"] = []
            q_toks = []
            last_pe = [None]

            def q_chunk(j):
                bi, bk, bfree = banks.get()

                def _mm(e):
                    ins = None
                    for k in range(8):
                        ins = e.matmul(bk[:, 0:TT], lhsT=wvq[:, k, j * 128:(j + 1) * 128],
                                       rhs=hTA[:, k, H0:H0 + TT], start=(k == 0), stop=(k == 7))
                    return ins
                tm = S.add("pe", _mm, [wtokq] + h_toks + bfree)
                te = S.add("act", lambda e: e.activation(out=QT[:, j, 0:TT], in_=bk[:, 0:TT], func=AF.Copy),
                           [tm] + qt_war)
                banks.release(bi, te)
                q_toks.append(te)
                last_pe[0] = tm
            for j in range(4):
                q_chunk(j)
            w_done(wiq, last_pe[0])
            hTA_readers.append(last_pe[0])

            wvk, wtokk, wik = w_next("inku")
            wv2, wtok2, wi2 = w_next("inuv")
            c0 = 0 if kind == "first" else H0
            ncol = TT + (HALO if kind == "first" else 0)
            ut_war = state["uT_readers"]
            state["uT_readers"] = []
            u_toks = {g: [] for g in range(4)}
            ub_toks = {}
            state["ub_toks"] = ub_toks
            k_toks = []
            gtok0 = HALO + t * T

            def ku_piece(j, pc0, pn):
                wsrc, wdep = (wvk, wtokk) if j < 4 else (wv2, wtok2)
                wc = j * 128 if j < 4 else 0
                bi, bk, bfree = banks.get()

                def _mm(e):
                    ins = None
                    for k in range(8):
                        ins = e.matmul(bk[:, 0:pn], lhsT=wsrc[:, k, wc:wc + 128], rhs=hTA[:, k, pc0:pc0 + pn],
                                       start=(k == 0), stop=(k == 7))
                    return ins
                tm = S.add("pe", _mm, [wdep] + h_toks + bfree)
                last_pe[0] = tm
                is_halo_piece = (kind == "first" and pc0 == 0)
                if j == 0:
                    if sample:
                        te = S.add("act", lambda e: e.activation(out=KTs[:, 0:TT], in_=bk[:, 0:TT], func=AF.Copy),
                                   [tm] + state["KTs_readers"])
                        k_toks.append(te)
                        rel_extra = []
                    else:
                        kc0 = 0 if is_halo_piece else HALO
                        te = S.add("act", lambda e: e.activation(out=KT2[t % 2][:, kc0:kc0 + pn], in_=bk[:, 0:pn],
                                                                 func=AF.Copy), [tm] + state["kt_readers"][t % 2])
                        rel_extra = []
                        if is_halo_piece:
                            state["k_halo_tok"] = [te]
                        else:
                            k_toks.append(te)
                            if t + 1 < NT:
                                tc_ = S.add("act", lambda e: e.activation(
                                    out=KT2[(t + 1) % 2][:, 0:HALO], in_=bk[:, pn - HALO:pn], func=AF.Copy),
                                    [tm] + state["kt_readers"][(t + 1) % 2])
                                state["k_halo_tok"] = [tc_]
                                rel_extra = [tc_]
                else:
                    g = j - 1
                    if sample:
                        te = S.add("act", lambda e: e.activation(
                            out=uTs[:, g, :, 16:32], in_=bk[:, 0:TT].rearrange("p (s j) -> p s j", s=4),
                            func=AF.Copy), [tm, t_setup] + ut_war)
                    elif is_halo_piece:
                        te = S.add("act", lambda e: e.activation(out=uT[:, g, 0:16], in_=bk[:, HALO - 16:HALO],
                                                                 func=AF.Copy), [tm, t_setup] + ut_war)
                    else:
                        te = S.add("act", lambda e: e.activation(out=uT[:, g, 16:16 + pn], in_=bk[:, 0:pn],
                                                                 func=AF.Copy), [tm, t_setup] + ut_war)
                    u_toks[g].append(te)
                    rel_extra = []
                    if not is_halo_piece:
                        npc = TT if sample else pn
                        teb = S.add("act", lambda e: e.activation(out=uTb[:, g, 0:npc], in_=bk[:, 0:npc], func=AF.Copy),
                                    [tm] + state["uTb_readers"])
                        ub_toks[g] = teb
                        rel_extra = [teb]
                banks.release(bi, [te] + rel_extra)

            inku_left = [4]

            def ku_chunk(j):
                pieces = [(c0, ncol)] if ncol <= 512 else [(c0, HALO), (c0 + HALO, TT)]
                for (pc0, pn) in pieces:
                    ku_piece(j, pc0, pn)
                if j < 4:
                    inku_left[0] -= 1
                    if inku_left[0] == 0:
                        w_done(wik, last_pe[0])
                hTA_readers.append(last_pe[0])

            interleave = not sample
            for j in range(1 if interleave else 5):
                ku_chunk(j)
            k_tok = k_toks[-1]

            v_toks = {}
            vblocks = ([("halo", 0, 128)] if kind == "first" else []) + [(b, H0 + b * 128, n) for (b, n) in blocks]

            def v_block(b, hc, n):
                bi, bk, bfree = banks.get()

                def _mm(e):
                    ins = None
                    for k in range(8):
                        ins = e.matmul(bk[0:n, 0:128], lhsT=hTA[:, k, hc:hc + n], rhs=wv2[:, k, 128:256],
                                       start=(k == 0), stop=(k == 7))
                    return ins
                tm = S.add("pe", _mm, [wtok2] + h_toks + bfree)
                last_pe[0] = tm
                if sample:
                    dst = V1s[0:n, :, 0:64]
                else:
                    lb = 0 if b == "halo" else 1 + b
                    dst = V12[t % 2][:, lb, :, 0:64]
                srcv = bk[0:n, 0:128].rearrange("p (k d) -> p k d", k=2)
                te = S.add("dve", lambda e: e.tensor_copy(dst, srcv),
                           [tm, t_setup] + ([] if sample else state["v_readers"][t % 2]))
                rel = [te]
                if (not sample) and b == NBLK - 1 and t + 1 < NT:
                    dst2 = V12[(t + 1) % 2][:, 0, :, 0:64]
                    tc_ = S.add("dve", lambda e: e.tensor_copy(dst2, srcv),
                                [tm, t_setup] + state["v_readers"][(t + 1) % 2])
                    rel.append(tc_)
                    state["v_prev_tok"] = tc_
                if (kind == "last" and b == NBLK - 1) or sample:
                    te2 = S.add("dve", lambda e: e.tensor_copy(stage[0:n, 128:256], bk[0:n, 0:128]),
                                [tm] + state["stage_free"])
                    rel.append(te2)
                    state["stage_v"] = te2
                banks.release(bi, rel)
                v_toks[b] = te
            for (b, hc, n) in vblocks:
                v_block(b, hc, n)
            inuv_done = [False]
            if kind not in ("last", "sample") and not interleave:
                w_done(wi2, last_pe[0])
                inuv_done[0] = True
            hTA_readers.append(last_pe[0])

            def out_kvu():
                wv3, wtok3, wi3 = w_next("int1")
                wv4, wtok4, wi4 = w_next("int2")
                if not interleave:
                    w_done(wi2, last_pe[0])
                    inuv_done[0] = True
                (b, n) = blocks[-1]
                hc = H0 + b * 128
                bi, bk, bfree = banks.get()

                def _mm(e):
                    ins = None
                    for k in range(8):
                        ins = e.matmul(bk[0:n, 0:512], lhsT=hTA[:, k, hc:hc + n], rhs=wv3[:, k, :],
                                       start=(k == 0), stop=(k == 7))
                    return ins
                tm = S.add("pe", _mm, [wtok3] + h_toks + bfree)
                w_done(wi3, tm)
                te1 = S.add("dve", lambda e: e.tensor_copy(stage[0:n, 0:128], bk[0:n, 0:128]),
                            [tm] + state["stage_free"])
                te2 = S.add("dve", lambda e: e.tensor_copy(stage[0:n, 256:640], bk[0:n, 128:512]),
                            [tm] + state["stage_free"])
                banks.release(bi, [te1, te2])
                bi2, bk2, bfree2 = banks.get()

                def _mm2(e):
                    ins = None
                    for k in range(8):
                        ins = e.matmul(bk2[0:n, 0:128], lhsT=hTA[:, k, hc:hc + n], rhs=wv4[:, k, :],
                                       start=(k == 0), stop=(k == 7))
                    return ins
                tm2 = S.add("pe", _mm2, [wtok4] + h_toks + bfree2)
                w_done(wi4, tm2)
                hTA_readers.append(tm2)
                te3 = S.add("dve", lambda e: e.tensor_copy(stage[0:n, 640:768], bk2[0:n, 0:128]),
                            [tm2] + state["stage_free"])
                banks.release(bi2, te3)
                dstd = kvu_s_d if sample else kvu_last_d
                st = S.dma("pool", lambda eng: eng.dma_start(out=dstd[:, :], in_=stage[0:n, :]),
                           "stage", [te1, te2, te3, state["stage_v"]])
                state["stage_free"] = [st]
                out_toks.append(st)
            if kind in ("last", "sample"):
                out_kvu()

            if DBG_STOP == f"{t}:INPROJ":
                return
            mix_war = state["mix_readers"]
            state["mix_readers"] = []
            mixa_toks = []
            if not sample:
                ctxs = {}
                tans = {}
                stepno = [0]

                def attn_step():
                    step = stepno[0]
                    stepno[0] += 1
                    if step < NBLK:
                        ctxs[step] = prompt_attn_scores(t, step, q_toks, k_tok)
                    if 1 <= step <= NBLK:
                        tans[step - 1] = prompt_attn_pv(ctxs[step - 1], v_toks)
                    if step >= 2 and step - 2 < NBLK:
                        bb = step - 2
                        mixa_toks.append(attn_post_b(tans[bb], 128, bb, ctxs[bb]["seq"], mix_war))

                def pre_group(g):
                    attn_step()
                    ku_chunk(g + 1)
                    if g == 3:
                        w_done(wi2, last_pe[0])
                        inuv_done[0] = True
                d_toks_pool = pooling(kind, TT, blocks, u_toks, mix_war, pre_group=pre_group)
                while stepno[0] < NBLK + 2:
                    attn_step()
            else:
                d_toks_pool = pooling(kind, TT, blocks, u_toks, mix_war)
                mixa_toks.append(sample_attention(q_toks, k_tok, v_toks[0], mix_war))
                if mixa_toks[-1] is None:
                    return
            assert inuv_done[0]

            if DBG_STOP == f"{t}:ATTN":
                return
            pool_toks, pool_ctx3 = pooling2(kind, TT, blocks, d_toks_pool, mix_war)
            rstdb_cols = {}
            if kind == "first":
                issue_convs(len(conv_order))

            if DBG_STOP == f"{t}:POOL":
                return
            def outproj_mm(hf, b, n, wv, wtok):
                ba = banks.get()
                bb = banks.get()

                def _mm(e):
                    ins = None
                    for k in range(4):
                        ins = e.matmul(ba[1][0:n, :], lhsT=mixT[:, k, b * 128:b * 128 + n], rhs=wv[:, k, :],
                                       start=(k == 0), stop=(k == 3))
                    for k in range(4, 8):
                        ins = e.matmul(bb[1][0:n, :], lhsT=mixT[:, k, b * 128:b * 128 + n], rhs=wv[:, k, :],
                                       start=(k == 4), stop=(k == 7))
                    return ins
                tm = S.add("pe", _mm, [wtok] + mixa_toks + pool_toks + ba[2] + bb[2])
                last_pe[0] = tm
                return (hf, b, n, tm, ba, bb)

            def outproj_evac(ctx_o):
                hf, b, n, tm, ba, bb = ctx_o
                xs_ = xb[0:n, b, hf * 512:(hf + 1) * 512]
                rc = rstdb_cols[b]
                t1 = S.add("dve", lambda e: e.scalar_tensor_tensor(
                    out=xs_, in0=bb[1][0:n, :], scalar=rsb[0:n, rc[0]:rc[0] + 1], in1=xs_, op0=ALU.mult,
                    op1=ALU.add), [tm, rc[1]] + x_toks[b])
                ra = state["rstda"][b]
                t2 = S.add("dve", lambda e: e.scalar_tensor_tensor(
                    out=xs_, in0=ba[1][0:n, :], scalar=rsb[0:n, ra[0]:ra[0] + 1], in1=xs_, op0=ALU.mult,
                    op1=ALU.add), [tm, t1, ra[1]])
                banks.release(ba[0], t2)
                banks.release(bb[0], t1)
                x_toks[b].append(t2)

            wvo = []
            for hf in range(2):
                wvo.append(w_next(f"out{hf}"))
            war = list(hT_readers)
            hT_readers.clear()
            h_toks = []

            ffn_rs = {}

            def norm_ffn_block(b, n):
                info = {}
                th, txs = norm_to_hT(xb[0:n, b, :], n, C_LNFFN, b * 128, x_toks[b], war, defer=-1.0, info=info)
                ffn_rs[b] = (info["col"], info["t_r"])
                h_toks.append(th)
                x_toks[b].append(txs)
            pend = None
            for bidx, (b, n) in enumerate(blocks):
                cts = [outproj_mm(hf, b, n, wvo[hf][0], wvo[hf][1]) for hf in range(2)]
                if bidx == 0:
                    rstdb_cols.update(pooling3(pool_ctx3))
                for c_ in cts:
                    outproj_evac(c_)
                if pend is not None:
                    norm_ffn_block(*pend)
                pend = (b, n)
            for hf in range(2):
                w_done(wvo[hf][2], last_pe[0])
            state["mix_readers"].append(last_pe[0])
            if DBG_STOP == f"{t}:OUTPROJ":
                return
            split_up = len(blocks) == NBLK
            if not split_up:
                norm_ffn_block(*pend)

            def up_chunk(i, c, wv, wtok, hid_war, hid_toks):
                bi, bk, bfree = banks.get()

                def _mm(e):
                    ins = None
                    for k in range(8):
                        ins = e.matmul(bk[:, 0:TT], lhsT=wv[:, k, c * 128:(c + 1) * 128],
                                       rhs=hT[:, k, 0:TT], start=(k == 0), stop=(k == 7))
                    return ins
                tm = S.add("pe", _mm, [wtok] + h_toks + bfree)
                last_pe[0] = tm
                fi = i * 4 + c
                tr = S.add("act", lambda e: e.activation(out=hidT[:, fi, 0:TT], in_=bk[:, 0:TT], func=AF.Relu),
                           [tm] + hid_war)
                banks.release(bi, tr)
                tsq = S.add("dve", lambda e: e.tensor_tensor(out=hidT[:, fi, 0:TT], in0=hidT[:, fi, 0:TT],
                                                             in1=hidT[:, fi, 0:TT], op=ALU.mult), [tr])
                hid_toks.append(tsq)

            pend_ffn = pend

            def up_first_split(wv, wtok, hid_war, hid_toks, pend_blk):
                NA = (NBLK - 1) * 128
                bks = [banks.get() for c in range(4)]
                tmA = []
                for c in range(4):
                    def _mmA(e, c=c):
                        ins = None
                        for k in range(8):
                            ins = e.matmul(bks[c][1][:, 0:NA], lhsT=wv[:, k, c * 128:(c + 1) * 128],
                                           rhs=hT[:, k, 0:NA], start=(k == 0), stop=(k == 7))
                        return ins
                    tmA.append(S.add("pe", _mmA, [wtok] + h_toks + bks[c][2]))
                norm_ffn_block(*pend_blk)
                for c in range(4):
                    def _mmB(e, c=c):
                        ins = None
                        for k in range(8):
                            ins = e.matmul(bks[c][1][:, NA:T], lhsT=wv[:, k, c * 128:(c + 1) * 128],
                                           rhs=hT[:, k, NA:T], start=(k == 0), stop=(k == 7))
                        return ins
                    tm = S.add("pe", _mmB, [wtok] + h_toks)
                    last_pe[0] = tm
                    fi = c
                    tr = S.add("act", lambda e, c=c, fi=fi: e.activation(out=hidT[:, fi, 0:T], in_=bks[c][1][:, 0:T],
                                                                         func=AF.Relu), [tm, tmA[c]] + hid_war)
                    banks.release(bks[c][0], tr)
                    tsq = S.add("dve", lambda e, fi=fi: e.tensor_tensor(out=hidT[:, fi, 0:T], in0=hidT[:, fi, 0:T],
                                                                        in1=hidT[:, fi, 0:T], op=ALU.mult), [tr])
                    hid_toks.append(tsq)

            def down_block(b, n, ch, kq, wv, wtok, acc, hid_toks):
                def _mm(e):
                    ins = None
                    for j in range(8):
                        ins = e.matmul(acc[1][0:n, :], lhsT=hidT[:, kq * 8 + j, b * 128:b * 128 + n],
                                       rhs=wv[:, j, :], start=(kq == 0 and j == 0), stop=(kq == 1 and j == 7))
                    return ins
                tm = S.add("pe", _mm, [wtok] + hid_toks[kq * 8:(kq + 1) * 8] + (acc[2] if kq == 0 else []))
                last_pe[0] = tm
                if kq == 1:
                    xs_ = xb[0:n, b, ch * 512:(ch + 1) * 512]
                    rs2 = ffn_rs[b]
                    ta = S.add("dve", lambda e: e.scalar_tensor_tensor(
                        out=xs_, in0=acc[1][0:n, :], scalar=rsb[0:n, rs2[0]:rs2[0] + 1], in1=xs_, op0=ALU.mult,
                        op1=ALU.add), [tm, rs2[1]] + x_toks[b])
                    banks.release(acc[0], ta)
                    x_toks[b].append(ta)

            pt_war = state["pT_readers"]
            state["pT_readers"] = []
            pT_toks = []

            def p_tr(b, n):
                bi, bk, bfree = banks.get()
                bkb = bk.bitcast(BF16)

                def _tr(e):
                    ins = None
                    for c in range(2):
                        ins = e.transpose(bkb[:, c * 128:c * 128 + n], pb[0:n, b, c * 128:(c + 1) * 128],
                                          identb[0:n, 0:n])
                    return ins
                tm = S.add("pe", _tr, [p_ld[t], t_ident] + bfree)
                last_pe[0] = tm
                src = bkb[:, 0:256].rearrange("p (c t) -> p c t", c=2)[:, :, 0:n]
                te = S.add("dve", lambda e: e.tensor_copy(pT[:, :, b * 128:b * 128 + n], src), [tm] + pt_war)
                banks.release(bi, te)
                pT_toks.append(te)
            for (b, n) in blocks:
                p_tr(b, n)
            p_read_tok = last_pe[0]

            h_toks_e = {}
            war_e = []

            ple_nr = {}

            def ple_neg_rstd(b, n, col, t_r):
                c2 = ss_col()
                t_nr = S.add("dve", lambda e: e.tensor_scalar_mul(out=rsb[0:n, c2:c2 + 1], in0=rsb[0:n, col:col + 1],
                                                                   scalar1=-1.0), [t_r])
                ple_nr[b] = (c2, t_nr)

            def norm_ple_block(b, n):
                info = {}
                th, txs = norm_to_hT(xb[0:n, b, :], n, C_LNPLE, b * 128, x_toks[b], war_e, defer=-0.5, info=info)
                ple_neg_rstd(b, n, info["col"], info["t_r"])
                h_toks_e[b] = th
                x_toks[b].append(txs)

            nt_ = t + 1
            pre_blocks = []
            if nt_ <= NT and not DBG_STOP:
                pre_blocks = [(0, NSAMP)] if nt_ == NT else [(bb, 128) for bb in range(NBLK)]
            pre_state = dict(next=0, pend=None, h_toks=[], txs={})
            war_A = list(hTA_readers)

            def prenorm_p2():
                ctx, bb, nn = pre_state["pend"]
                th = norm_p2(ctx, C_LNMIX, hTA, HALO + bb * 128, war_A)
                pre_state["h_toks"].append(th)
                pre_state["pend"] = None

            def prenorm_step():
                k = pre_state["next"]
                if pre_state["pend"] is not None:
                    prenorm_p2()
                if k < len(pre_blocks):
                    bb, nn = pre_blocks[k]
                    ctx = norm_p1(xbuf[nt_ % 2][0:nn, bb, :], nn, [x_ld[nt_]])
                    pre_state["txs"][bb] = ctx["t_xs"]
                    pre_state["pend"] = (ctx, bb, nn)
                    pre_state["next"] = k + 1

            def prenorm_flush():
                while pre_state["pend"] is not None or pre_state["next"] < len(pre_blocks):
                    prenorm_step()
                if pre_blocks:
                    hTA_readers.clear()
                    state["pre"][nt_] = dict(h_toks=pre_state["h_toks"], txs=pre_state["txs"])

            pend = None
            for fh in range(2):
                hid_war = state["hid_readers"]
                state["hid_readers"] = []
                hid_toks = []
                for i in range(4):
                    wv, wtok, wi = w_next(f"up{fh * 4 + i}")
                    if fh == 0 and i == 0 and split_up:
                        up_first_split(wv, wtok, hid_war, hid_toks, pend_ffn)
                    else:
                        for c in range(4):
                            up_chunk(i, c, wv, wtok, hid_war, hid_toks)
                    w_done(wi, last_pe[0])
                hT_readers.append(last_pe[0])
                if fh == 1:
                    war_e.extend(hT_readers)
                    hT_readers.clear()
                for ch in range(2):
                    wd = [w_next(f"dn{fh}{ch}{kq}") for kq in range(2)]
                    for (b, n) in blocks:
                        acc = banks.get()
                        for kq in range(2):
                            down_block(b, n, ch, kq, wd[kq][0], wd[kq][1], acc, hid_toks)
                        if fh == 1 and ch == 0:
                            prenorm_step()
                        if fh == 1 and ch == 1:
                            if b == blocks[0][0]:
                                prenorm_flush()
                            if pend is not None:
                                norm_ple_block(*pend)
                            pend = (b, n)
                    for kq in range(2):
                        w_done(wd[kq][2], last_pe[0])
                    state["hid_readers"].append(last_pe[0])
            if DBG_STOP == f"{t}:FFN":
                return
            if len(blocks) == 1:
                norm_ple_block(*pend)
                pend = None
            else:
                (b_l, n_l) = pend
                ctx_l = norm_p1(xb[0:n_l, b_l, :], n_l, x_toks[b_l], defer=-0.5)
                ple_neg_rstd(b_l, n_l, ctx_l["col"], ctx_l["t_r"])
                x_toks[b_l].append(ctx_l["t_xs"])

                def ple_last_p2():
                    h_toks_e[b_l] = norm_p2(ctx_l, C_LNPLE, hT, b_l * 128, war_e)

            wvp, wtokp, wip = w_next("ple")
            wvg = [w_next(f"gate{hf}") for hf in range(2)]

            def gate_block(hf, b, n, wv, wtok):
                bg = banks.get()
                bp = banks.get()

                def _mm(e):
                    ins = None
                    for k in range(8):
                        ins = e.matmul(bg[1][0:n, :], lhsT=hT[:, k, b * 128:b * 128 + n], rhs=wv[:, k, :],
                                       start=(k == 0), stop=(k == 7))
                    for k in range(2):
                        ins = e.matmul(bp[1][0:n, :], lhsT=pT[:, k, b * 128:b * 128 + n],
                                       rhs=wvp[:, k, hf * 512:(hf + 1) * 512], start=(k == 0), stop=(k == 1))
                    return ins
                tm = S.add("pe", _mm, [wtok, wtokp, h_toks_e[b]] + pT_toks + bg[2] + bp[2])
                last_pe[0] = tm
                gi = (b + hf) % 2
                nr = ple_nr[b]
                ta1 = S.add("act", lambda e: e.activation(out=tg[gi][0:n, :], in_=bg[1][0:n, :], func=AF.Exp,
                                                          scale=rsb[0:n, nr[0]:nr[0] + 1]), [tm, nr[1]] + tg_free[gi])
                banks.release(bg[0], ta1)
                ta2 = S.add("act", lambda e: e.activation(out=tg[gi][0:n, :], in_=tg[gi][0:n, :], func=AF.Ln,
                                                          bias=oneb[0:n, :], scale=1.0), [ta1, t_setup])
                ta3 = S.add("act", lambda e: e.activation(out=tg[gi][0:n, :], in_=tg[gi][0:n, :], func=AF.Exp,
                                                          scale=-1.0), [ta2])
                t1 = S.add("dve", lambda e: e.tensor_tensor(out=tp[0][0:n, :], in0=tg[gi][0:n, :],
                                                            in1=bp[1][0:n, :], op=ALU.mult),
                           [tm, ta3] + tp_free[0])
                banks.release(bp[0], t1)
                tg_free[gi] = [t1]
                xs_ = xb[0:n, b, hf * 512:(hf + 1) * 512]
                t2 = S.add("dve", lambda e: e.tensor_tensor(out=xs_, in0=tp[0][0:n, :], in1=xs_, op=ALU.add),
                           [t1] + x_toks[b])
                tp_free[0] = [t2]
                x_toks[b].append(t2)

            fin = []

            def final_block(b, n):
                col = ss_col()
                xap = xb[0:n, b, :]
                t_ss = S.add("act", lambda e: e.activation(out=junk[0:n, :], in_=xap, func=AF.Square,
                                                           accum_out=ssb[0:n, col:col + 1]), x_toks[b] + [t_setup])
                t_r = rstd_chain(lambda: ssb[0:n, col:col + 1], t_ss, col, 1, n, D)
                t_y = S.add("dve", lambda e: e.scalar_tensor_tensor(
                    out=xap, in0=xap, scalar=rsb[0:n, col:col + 1], in1=lnf[0:n, :], op0=ALU.mult, op1=ALU.mult),
                    [t_r, const_tok, t_ss] + x_toks[b])
                fin.append(t_y)

            pairs = [blocks[i:i + 2] for i in range(0, len(blocks), 2)]
            pendf = []
            for pi_, pair in enumerate(pairs):
                for hf in range(2):
                    for (b, n) in pair:
                        gate_block(hf, b, n, wvg[hf][0], wvg[hf][1])
                if pend is not None and pi_ == 0:
                    ple_last_p2()
                    pend = None
                for pf in pendf:
                    final_block(*pf)
                pendf = list(pair)
            for hf in range(2):
                w_done(wvg[hf][2], last_pe[0])
            w_done(wip, last_pe[0])
            hT_readers.append(last_pe[0])
            state["pT_readers"] = [last_pe[0], p_read_tok]
            if DBG_STOP == f"{t}:GATE":
                return
            for pf in pendf:
                final_block(*pf)

            if sample:
                st = S.dma("sp", lambda eng: eng.dma_start(out=ys_d[:, :], in_=xb[0:NSAMP, 0, :]), f"y{t % 2}", fin)
            else:
                dsty = y_d[t * T:(t + 1) * T, :].rearrange("(b p) d -> p b d", p=128)
                st = S.dma("sp", lambda eng: eng.dma_start(out=dsty, in_=xb[:, :, :]), f"y{t % 2}", fin)
            y_st[t] = st
            out_toks.append(st)

        def prep_sample_hist():
            toks = []

            tcb = S.add("pool", lambda e: e.tensor_copy(spb[0:60, :], spf[0:60, :]), [ckv_tok])
            bi, bk, bfree = banks.get()
            bkb = bk.bitcast(BF16)

            def _str(e):
                ins = None
                for g in range(4):
                    ins = e.transpose(bkb[:, g * 64:g * 64 + 60], spb[0:60, g * 128:(g + 1) * 128],
                                      identb[0:60, 0:60])
                return ins
            tm = S.add("pe", _str, [tcb, t_ident] + bfree)
            te = S.add("dve", lambda e: e.tensor_copy(
                uTs[:, :, :, 1:16],
                bkb[:, 0:256].rearrange("p (g j) -> p g j", g=4)[:, :, 0:60].rearrange("p g (s j) -> p g s j", s=4)),
                [tm, t_setup])
            banks.release(bi, te)
            toks.append(te)
            state["spT_tok"] = toks

        for tl in tiles:
            if DBG_STOP and DBG_STOP != "ALLCONV" and tl["idx"] > int(DBG_STOP.split(":")[0]):
                break
            if tl["kind"] == "sample":
                prep_sample_hist()
            run_tile(tl)

        S.add("pool", lambda e: e.memset(fint[:], 0.0), out_toks)
        S.emit(block)
    return nc


_CACHE = {}


def _get_program():
    if "nc" not in _CACHE:
        _CACHE["nc"] = build_program()
    return _CACHE["nc"]


def kernel(x_prompt, x_sample, cache_k, cache_v, state_pool, p_prompt, p_sample,
           ln_mix, w_in, attn_sinks, g_attn_out, w_pool, pool_scale, g_pool_out,
           w_out, ln_ffn, w_up, w_down, ln_ple, w_ple_gate, w_ple_proj, ln_final):
    f32 = np.float32
    x_prompt = np.asarray(x_prompt, f32)
    x_sample = np.asarray(x_sample, f32)
    B, SEQ, _ = x_prompt.shape
    segs_per_seq = SEQ // SEG
    assert B * segs_per_seq == NCORES

    w_in0 = np.asarray(w_in, f32)[0]
    qcols = []
    for i in range(4):
        qcols += list(range(i * 64, (i + 1) * 64)) + list(range((4 + i) * 64, (5 + i) * 64))
    kcols = list(range(512, 640))
    vcols = list(range(640, 768))
    ucols = list(range(768, 1280))
    perm = qcols + kcols + ucols[0:384] + ucols[384:512] + vcols + kcols + ucols[0:384] + ucols[384:512]
    w_in_p = np.ascontiguousarray(w_in0[:, perm])

    def col8(v):
        return np.asarray(v, f32).reshape(-1, 128).T

    w_pool_l = np.ascontiguousarray(np.asarray(w_pool, f32)[0].transpose(1, 0, 2).reshape(128, 512))
    lnf = np.ascontiguousarray(np.broadcast_to(np.asarray(ln_final, f32)[None, :], (128, D)))
    bmask = np.kron(np.eye(4, dtype=f32), np.ones((16, 16), f32))
    ident = np.eye(128, dtype=f32)

    shared = dict(
        w_in_p=w_in_p, w_out=np.ascontiguousarray(np.asarray(w_out, f32)[0]),
        w_up=np.ascontiguousarray(np.asarray(w_up, f32)[0]),
        w_down=np.ascontiguousarray(np.asarray(w_down, f32)[0]),
        w_gate=np.ascontiguousarray(np.asarray(w_ple_gate, f32)[0]),
        w_ple=np.ascontiguousarray(np.asarray(w_ple_proj, f32)[0]),
        w_pool=w_pool_l, lnf=lnf, bmask=bmask, ident=ident)

    in_maps = []
    for c in range(NCORES):
        b, s = divmod(c, segs_per_seq)
        xin = np.zeros((HALO + SEG, D), f32)
        xin[HALO:] = x_prompt[b, s * SEG:(s + 1) * SEG]
        if s > 0:
            xin[:HALO] = x_prompt[b, s * SEG - HALO:s * SEG]
        cols = np.zeros((128, NCOLS), f32)
        cols[:, C_LNMIX:C_LNMIX + 8] = col8(np.asarray(ln_mix)[0])
        cols[:, C_LNFFN:C_LNFFN + 8] = col8(np.asarray(ln_ffn)[0])
        cols[:, C_LNPLE:C_LNPLE + 8] = col8(np.asarray(ln_ple)[0])
        cols[:, C_GATT:C_GATT + 4] = col8(np.asarray(g_attn_out)[0])
        cols[:, C_GPOOL:C_GPOOL + 4] = col8(np.asarray(g_pool_out)[0])
        cols[:, C_PSCALE:C_PSCALE + 4] = col8(np.asarray(pool_scale)[0])
        cols[:, C_HBIAS] = 0.0 if s > 0 else -30000.0
        cols[:, C_SINK:C_SINK + 8] = np.asarray(attn_sinks, f32)[0][None, :]
        cols[:, C_INVW:C_INVW + 4] = np.array([0.5, 0.25, 0.125, 0.0625], f32)[None, :]
        invcnt = np.zeros((128, 64), f32)
        for g, w in enumerate((2, 4, 8, 16)):
            pos = np.arange(16)
            cnt = np.minimum(pos + 1, w) if s == 0 else np.full(16, w)
            invcnt[:, g * 16:(g + 1) * 16] = (w / cnt).astype(f32)[None, :]
        ss = slice(c * 4, (c + 1) * 4)
        m = dict(shared)
        m.update(
            xin=xin, pin=np.ascontiguousarray(np.asarray(p_prompt, f32)[0, b, s * SEG:(s + 1) * SEG]),
            xs=np.ascontiguousarray(x_sample[ss].reshape(NSAMP, D)),
            ps=np.ascontiguousarray(np.asarray(p_sample, f32)[0, ss].reshape(NSAMP, 256)),
            ck=np.ascontiguousarray(np.asarray(cache_k, f32)[0, ss].reshape(4 * 128, 128)),
            cv=np.ascontiguousarray(np.asarray(cache_v, f32)[0, ss].reshape(4 * 128, 128)),
            spool=np.ascontiguousarray(np.asarray(state_pool, f32)[0, ss].reshape(60, 512)),
            cols=cols, invcnt=invcnt)
        in_maps.append(m)

    nc = _get_program()
    res = run_bass_kernel_spmd(nc, in_maps, core_ids=list(range(NCORES)))
    R = res.results

    nsb = x_sample.shape[0]
    y_prompt = np.zeros((B, SEQ, D), f32)
    y_sample = np.zeros((nsb, 16, D), f32)
    nkp = np.zeros((1, B, 128, 2, 64), f32)
    nvp = np.zeros((1, B, 128, 2, 64), f32)
    npp = np.zeros((1, B, 15, 512), f32)
    nks = np.zeros((1, nsb, 16, 2, 64), f32)
    nvs = np.zeros((1, nsb, 16, 2, 64), f32)
    nps = np.zeros((1, nsb, 15, 512), f32)
    for c in range(NCORES):
        b, s = divmod(c, segs_per_seq)
        y_prompt[b, s * SEG:(s + 1) * SEG] = R[c]["y"]
        y_sample[c * 4:(c + 1) * 4] = R[c]["ys"].reshape(4, 16, D)
        if s == segs_per_seq - 1:
            kvu = R[c]["kvu_last"]
            nkp[0, b] = kvu[:, 0:128].reshape(128, 2, 64)
            nvp[0, b] = kvu[:, 128:256].reshape(128, 2, 64)
            npp[0, b] = kvu[113:128, 256:768]
        kvs = R[c]["kvu_s"].reshape(4, 16, 768)
        nks[0, c * 4:(c + 1) * 4] = kvs[:, :, 0:128].reshape(4, 16, 2, 64)
        nvs[0, c * 4:(c + 1) * 4] = kvs[:, :, 128:256].reshape(4, 16, 2, 64)
        nps[0, c * 4:(c + 1) * 4] = kvs[:, 1:16, 256:768]
    return (y_prompt, y_sample, nkp, nvp, npp, nks, nvs, nps)
```

```python
import contextlib
import numpy as np
import concourse.bass as bass
import concourse.mybir as mybir
from concourse.bass_utils import run_bass_kernel_spmd

F32 = mybir.dt.float32
BF16 = mybir.dt.bfloat16
AF = mybir.ActivationFunctionType
ALU = mybir.AluOpType

NCORES = 8
D = 1024
SEG = 4096
T = 512
NT = SEG // T
NBLK = T // 128
HALO = 128
NSAMP = 64
EPS = 1e-6
NSLOT = 5
SLOT_COLS = 4096

C_LNMIX, C_LNFFN, C_LNPLE, C_GATT, C_GPOOL, C_PSCALE, C_HBIAS, C_SINK, C_INVW = 0, 8, 16, 24, 28, 32, 36, 37, 45
NCOLS = 49
import os as _os
DBG_STOP = _os.environ.get('KDBG_STOP', '')


class Sched:
    ENG = ("pe", "act", "dve", "pool", "sp")

    def __init__(self, nc, stack):
        self.nc = nc
        self.stack = stack
        self.streams = {e: [] for e in self.ENG}
        self.csem = {e: stack.enter_context(nc.semaphore("c_" + e)) for e in ("pe", "act", "dve", "pool")}
        self.dsem = {}
        self.dcount = {}

    def add(self, eng, fn, deps=()):
        lst = self.streams[eng]
        lst.append(dict(kind="c", fn=fn, deps=self._flat(deps), sig=False))
        return ("c", eng, len(lst) - 1)

    def dma_sem(self, sem):
        if sem not in self.dsem:
            self.dsem[sem] = self.stack.enter_context(self.nc.semaphore("d_" + sem))
            self.dcount[sem] = 0

    def dma(self, queue, fn, sem, deps=()):
        self.dma_sem(sem)
        self.dcount[sem] += 16
        self.streams[queue].append(dict(kind="d", fn=fn, deps=self._flat(deps), sem=sem))
        return ("d", sem, self.dcount[sem])

    def _flat(self, deps):
        out = []
        for d in deps:
            if d is None:
                continue
            if isinstance(d, list):
                out.extend(self._flat(d))
            else:
                assert isinstance(d, tuple) and d[0] in ("c", "d"), d
                out.append(d)
        return out

    def emit(self, block):
        for e in self.ENG:
            for op in self.streams[e]:
                for d in op["deps"]:
                    if d[0] == "c":
                        if d[1] == "pe" and e == "pe":
                            continue
                        self.streams[d[1]][d[2]]["sig"] = True
        sigval = {}
        for e in ("pe", "act", "dve", "pool"):
            n = 0
            for i, op in enumerate(self.streams[e]):
                if op["kind"] == "c" and op["sig"]:
                    n += 1
                    sigval[(e, i)] = n
        self.maxsig = {e: max([v for (ee, i), v in sigval.items() if ee == e] or [0]) for e in ("pe", "act", "dve", "pool")}
        self.nops = {e: len(self.streams[e]) for e in self.ENG}
        if DBG_STOP:
            print("SCHED maxsig", self.maxsig, "nops", self.nops, "dma", self.dcount)
        engobj = {"pe": block.tensor, "act": block.scalar, "dve": block.vector, "pool": block.gpsimd,
                  "sp": block.sync}

        def run(e):
            def body(eng):
                waited = {}
                for i, op in enumerate(self.streams[e]):
                    need = {}
                    for d in op["deps"]:
                        if d[0] == "c":
                            if d[1] == "pe" and e == "pe":
                                continue
                            key = ("c", d[1])
                            val = sigval[(d[1], d[2])]
                        else:
                            key = ("d", d[1])
                            val = d[2]
                        if val > need.get(key, 0):
                            need[key] = val
                    for key, val in need.items():
                        if waited.get(key, 0) >= val:
                            continue
                        sem = self.csem[key[1]] if key[0] == "c" else self.dsem[key[1]]
                        eng.wait_ge(sem, val)
                        waited[key] = val
                    ins = op["fn"](eng)
                    if op["kind"] == "d":
                        ins.then_inc(self.dsem[op["sem"]], 16)
                    elif op["sig"]:
                        ins.then_inc(self.csem[e], 1)
            engobj[e](body)

        for e in self.ENG:
            run(e)


class Banks:
    def __init__(self, tensors):
        self.t = tensors
        self.free = [[] for _ in tensors]
        self.rr = 0

    def get(self):
        i = self.rr
        self.rr = (i + 1) % len(self.t)
        return i, self.t[i], self.free[i]

    def release(self, i, toks):
        self.free[i] = list(toks) if isinstance(toks, list) else [toks]


def build_program(NT=NT):
    SEG = NT * T
    nc = bass.Bass("TRN2", target_bir_lowering=False)

    def din(name, shape, dt=F32):
        return nc.dram_tensor(name, list(shape), dt, kind="ExternalInput").ap()

    def dout(name, shape, dt=F32):
        return nc.dram_tensor(name, list(shape), dt, kind="ExternalOutput").ap()

    xin = din("xin", [HALO + SEG, D])
    pin = din("pin", [SEG, 256])
    xs_d = din("xs", [NSAMP, D])
    ps_d = din("ps", [NSAMP, 256])
    ck_d = din("ck", [4 * 128, 128])
    cv_d = din("cv", [4 * 128, 128])
    spool_d = din("spool", [60, 512])
    w_in_d = din("w_in_p", [D, 1920])
    w_out_d = din("w_out", [D, D])
    w_up_d = din("w_up", [D, 4096])
    w_down_d = din("w_down", [4096, D])
    w_gate_d = din("w_gate", [D, D])
    w_ple_d = din("w_ple", [256, D])
    w_pool_d = din("w_pool", [128, 512])
    cols_d = din("cols", [128, NCOLS])
    lnf_d = din("lnf", [128, D])
    invcnt_d = din("invcnt", [128, 64])
    bmask_d = din("bmask", [64, 64])
    ident_d = din("ident", [128, 128])

    y_d = dout("y", [SEG, D])
    ys_d = dout("ys", [NSAMP, D])
    kvu_last_d = dout("kvu_last", [128, 768])
    kvu_s_d = dout("kvu_s", [NSAMP, 768])

    wblocks = {}

    def wblk(name, src3, k, c):
        sc = nc.dram_tensor("sc_" + name, [128, k * c], BF16, kind="Internal").ap()
        wblocks[name] = dict(src=src3, sc=sc, k=k, c=c)

    w_in3 = w_in_d.rearrange("(k p) n -> p k n", p=128)
    wblk("inq", w_in3[:, :, 0:512], 8, 512)
    wblk("inku", w_in3[:, :, 512:1024], 8, 512)
    wblk("inuv", w_in3[:, :, 1024:1280], 8, 256)
    wblk("int1", w_in3[:, :, 1280:1792], 8, 512)
    wblk("int2", w_in3[:, :, 1792:1920], 8, 128)
    w_out3 = w_out_d.rearrange("(k p) n -> p k n", p=128)
    for hf in range(2):
        wblk(f"out{hf}", w_out3[:, :, hf * 512:(hf + 1) * 512], 8, 512)
    w_up3 = w_up_d.rearrange("(k p) n -> p k n", p=128)
    w_dn3 = w_down_d.rearrange("(k p) n -> p k n", p=128)
    for fh in range(2):
        for i in range(4):
            j = fh * 4 + i
            wblk(f"up{j}", w_up3[:, :, j * 512:(j + 1) * 512], 8, 512)
        for ch in range(2):
            for kq in range(2):
                k0 = fh * 16 + kq * 8
                wblk(f"dn{fh}{ch}{kq}", w_dn3[:, k0:k0 + 8, ch * 512:(ch + 1) * 512], 8, 512)
    w_g3 = w_gate_d.rearrange("(k p) n -> p k n", p=128)
    w_p3 = w_ple_d.rearrange("(k p) n -> p k n", p=128)
    wblk("ple", w_p3[:, :, :], 2, 1024)
    for hf in range(2):
        wblk(f"gate{hf}", w_g3[:, :, hf * 512:(hf + 1) * 512], 8, 512)

    def tile_wseq(kind):
        seq = ["inq", "inku", "inuv"]
        if kind in ("last", "sample"):
            seq += ["int1", "int2"]
        seq += ["out0", "out1"]
        for fh in range(2):
            seq += [f"up{fh * 4 + i}" for i in range(4)]
            seq += [f"dn{fh}{ch}{kq}" for ch in range(2) for kq in range(2)]
        seq += ["ple", "gate0", "gate1"]
        return seq

    tiles = []
    for t in range(NT):
        kind = "first" if t == 0 else ("last" if t == NT - 1 else "mid")
        tiles.append(dict(kind=kind, idx=t))
    tiles.append(dict(kind="sample", idx=NT))
    wseq = []
    for tl in tiles:
        wseq += tile_wseq(tl["kind"])

    with contextlib.ExitStack() as stack:
        def sb(name, shape, dt):
            return stack.enter_context(nc.sbuf_tensor("s_" + name, list(shape), dt))

        S = Sched(nc, stack)

        xbuf = [sb(f"xbuf{i}", [128, NBLK, D], F32) for i in range(2)]
        pbuf = [sb(f"pbuf{i}", [128, NBLK, 256], BF16) for i in range(2)]
        junk = sb("junk", [128, D], BF16)
        sqj = sb("sqj", [128, 512], BF16)
        xsb = [sb(f"xsb{i}", [128, D], BF16) for i in range(2)]
        hTA = sb("hTA", [128, 8, HALO + T], BF16)
        hT = sb("hT", [128, 8, T], BF16)
        QT = sb("QT", [128, 4, T], BF16)
        KT2 = [sb(f"KT{i}", [128, HALO + T], BF16) for i in range(2)]
        KTs = sb("KTs", [128, NSAMP], BF16)
        V12 = [sb(f"V1_{i}", [128, NBLK + 1, 2, 65], BF16) for i in range(2)]
        V1s = sb("V1s", [128, 2, 65], BF16)
        uT = sb("uT", [128, 4, 16 + T], F32)
        ptmp = [sb(f"ptmp{i}", [128, 16 + T], F32) for i in range(2)]
        dT = sb("dT", [128, 4, T], BF16)
        uTb = sb("uTb", [128, 4, T], BF16)
        wpool_n = sb("wpool_n", [128, 512], BF16)
        psgw = sb("psgw", [128, 4], F32)
        pscw = sb("pscw", [128, 4], F32)
        PT = [sb(f"PT{i}", [128, 2048], BF16) for i in range(2)]
        anb = [sb(f"anb{i}", [128, 512], BF16) for i in range(2)]
        mixT = sb("mixT", [128, 8, T], BF16)
        sqb = sb("sqb", [128, 4, T], BF16)
        hidT = sb("hidT", [128, 16, T], BF16)
        tg = [sb(f"tg{i}", [128, 512], F32) for i in range(2)]
        tp = [sb(f"tp{i}", [128, 512], F32) for i in range(1)]
        pT = sb("pT", [128, 2, T], BF16)
        wring = [sb(f"wring{i}", [128, SLOT_COLS], BF16) for i in range(NSLOT)]
        stage = sb("stage", [128, 768], F32)
        lnf = sb("lnf", [128, D], F32)
        cols = sb("cols", [128, NCOLS], F32)
        psg = sb("psg", [128, 4], F32)
        esink = sb("esink", [128, 8], F32)
        invcnt = sb("invcnt", [128, 64], F32)
        bmask = sb("bmask", [64, 64], F32)
        identf = sb("identf", [128, 128], F32)
        identb = sb("identb", [128, 128], BF16)
        wpool = sb("wpool", [128, 512], BF16)
        ones = sb("ones", [128, 1], BF16)
        epsb = sb("epsb", [128, 1], F32)
        oneb = sb("oneb", [128, 1], F32)
        fint = sb("fint", [128, 1], F32)
        NSS = 384
        ssb = sb("ssb", [128, NSS], F32)
        vvb = sb("vvb", [128, NSS], F32)
        rsb = sb("rsb", [128, NSS], F32)
        den = [sb(f"den{i}", [128, 8], F32) for i in range(2)]
        rcp = [sb(f"rcp{i}", [128, 8], F32) for i in range(2)]
        ckf = sb("ckf", [128, 4, 128], F32)
        cvf = sb("cvf", [128, 4, 128], F32)
        spf = sb("spf", [64, 512], F32)
        ckb = sb("ckb", [128, 4, 128], BF16)
        spb = sb("spb", [64, 512], BF16)
        KcT = sb("KcT", [128, 4, 128], BF16)
        Vc1 = sb("Vc1", [128, 4, 2, 65], BF16)
        uTs = sb("uTs", [128, 4, 4, 32], F32)
        ptmps = [sb(f"ptmps{i}", [128, 4, 32], F32) for i in range(2)]
        etmp = sb("etmp", [64, 512], F32)

        pbank = [stack.enter_context(nc.psum_tensor(f"pb{i}", [128, 512], F32)) for i in range(8)]
        banks = Banks(pbank)
        block = stack.enter_context(nc.Block())

        ss_ctr = [0]

        def ss_col():
            c = ss_ctr[0]
            ss_ctr[0] += 1
            assert c < NSS
            return c

        conv_tok = {}
        conv_order = list(dict.fromkeys(wseq))
        for name in conv_order:
            S.dma_sem("cv_" + name)
            conv_tok[name] = ("d", "cv_" + name, 16)
        conv_issued = [0]

        def issue_convs(n):
            while conv_issued[0] < min(n, len(conv_order)):
                name = conv_order[conv_issued[0]]
                wb = wblocks[name]
                dst = wb["sc"].rearrange("p (k c) -> p k c", k=wb["k"])
                tok = S.dma("pool", (lambda eng, dst=dst, src=wb["src"]: eng.dma_start(out=dst, in_=src)),
                            "cv_" + name)
                assert tok == conv_tok[name]
                conv_issued[0] += 1

        def _setup_pool(e):
            e.memset(ones[:], 1.0)
            e.memset(epsb[:], EPS)
            e.memset(oneb[:], 1.0)
            e.memset(ssb[:], 0.0)
            e.memset(V12[0][:], 1.0)
            e.memset(V12[1][:], 1.0)
            e.memset(V1s[:], 1.0)
            e.memset(Vc1[:], 1.0)
            e.memset(uTs[:], 0.0)
            e.memset(PT[0][:, :], 0.0)
            e.memset(PT[1][:, :], 0.0)
            return e.memset(uT[:], 0.0)
        issue_convs(1)
        t_setup = S.add("pool", _setup_pool)

        c_tok = []
        for dst, src in ((cols[:], cols_d[:, :]), (lnf[:], lnf_d[:, :]), (invcnt[:], invcnt_d[:, :]),
                         (bmask[:], bmask_d[:, :]), (identf[:], ident_d[:, :])):
            c_tok.append(S.dma("sp", (lambda eng, dst=dst, src=src: eng.dma_start(out=dst, in_=src)), "const"))
        const_tok = c_tok[-1]

        t_ident = S.add("pool", lambda e: e.tensor_copy(identb[:], identf[:]), [const_tok])
        issue_convs(3)
        wp_tok = S.dma("pool", lambda eng: eng.dma_start(out=wpool[:], in_=w_pool_d[:, :]), "wpool")
        issue_convs(5)
        if DBG_STOP:
            issue_convs(len(conv_order))

        t_psg = S.add("dve", lambda e: e.tensor_tensor(out=psg[:], in0=cols[:, C_PSCALE:C_PSCALE + 4],
                                                       in1=cols[:, C_GPOOL:C_GPOOL + 4], op=ALU.mult), [const_tok])
        t_esink = S.add("act", lambda e: e.activation(out=esink[:], in_=cols[:, C_SINK:C_SINK + 8], func=AF.Exp),
                        [const_tok])
        t_psgw = S.add("dve", lambda e: e.tensor_tensor(out=psgw[:], in0=psg[:], in1=cols[:, C_INVW:C_INVW + 4],
                                                        op=ALU.mult), [t_psg, const_tok])
        t_pscw = S.add("dve", lambda e: e.tensor_tensor(out=pscw[:], in0=cols[:, C_PSCALE:C_PSCALE + 4],
                                                        in1=cols[:, C_INVW:C_INVW + 4], op=ALU.mult), [const_tok])
        t_wpn = None
        for g_ in range(4):
            t_wpn = S.add("dve", lambda e, g_=g_: e.tensor_scalar_mul(
                out=wpool_n[:, g_ * 128:(g_ + 1) * 128], in0=wpool[:, g_ * 128:(g_ + 1) * 128],
                scalar1=-float(2 << g_)), [wp_tok] + ([t_wpn] if t_wpn else []))

        wstate = dict(next_load=0, next_use=0, done=set(), slot_free={}, loaded={})

        def w_issue_loads():
            while wstate["next_load"] < len(wseq):
                i = wstate["next_load"]
                if i >= NSLOT and (i - NSLOT) not in wstate["done"]:
                    break
                name = wseq[i]
                wb = wblocks[name]
                slot = i % NSLOT
                n = wb["k"] * wb["c"]
                deps = [conv_tok[name]] + wstate["slot_free"].get(i - NSLOT, [])
                tok = S.dma("sp", (lambda eng, slot=slot, n=n, sc=wb["sc"]:
                                   eng.dma_start(out=wring[slot][:, 0:n], in_=sc[:, :])), f"w{slot}", deps)
                wstate["loaded"][i] = tok
                wstate["next_load"] += 1

        def w_next(expect):
            i = wstate["next_use"]
            assert wseq[i] == expect, (wseq[i], expect)
            w_issue_loads()
            assert i in wstate["loaded"], ("weight ring over-subscribed", i, expect)
            wstate["next_use"] += 1
            wb = wblocks[expect]
            slot = i % NSLOT
            view = wring[slot][:, 0:wb["k"] * wb["c"]].rearrange("p (k c) -> p k c", k=wb["k"])
            return view, wstate["loaded"][i], i

        def w_done(i, tok):
            wstate["slot_free"][i] = [tok]
            wstate["done"].add(i)
            w_issue_loads()

        def rstd_chain(ss_ap_fn, ss_tok, col, ncol, n, width, escale=-0.5):
            t1 = S.add("act", lambda e: e.activation(out=vvb[0:n, col:col + ncol], in_=ss_ap_fn(),
                                                     func=AF.Ln, bias=epsb[0:n, :], scale=1.0 / width),
                       [ss_tok, t_setup])
            t2 = S.add("act", lambda e: e.activation(out=rsb[0:n, col:col + ncol], in_=vvb[0:n, col:col + ncol],
                                                     func=AF.Exp, scale=escale), [t1])
            return t2

        xsb_free = [[], []]
        xsb_rr = [0]

        def norm_p1(xap, n, x_deps, defer=None):
            col = ss_col()
            i = xsb_rr[0]
            xsb_rr[0] ^= 1
            if defer is not None:
                t_xs = S.add("act", lambda e: e.activation(out=xsb[i][0:n, :], in_=xap, func=AF.Copy),
                             x_deps + xsb_free[i])
                t_ss = S.add("act", lambda e: e.activation(out=junk[0:n, :], in_=xap, func=AF.Square,
                                                           accum_out=ssb[0:n, col:col + 1]), x_deps + [t_setup])
                t_r = rstd_chain(lambda: ssb[0:n, col:col + 1], t_ss, col, 1, n, D, escale=defer)
                return dict(i=i, t_xs=t_ss, t_cast=t_xs, n=n, col=col, t_r=t_r)
            t_ss = S.add("act", lambda e: e.activation(out=junk[0:n, :], in_=xap, func=AF.Square,
                                                       accum_out=ssb[0:n, col:col + 1]), x_deps + [t_setup])
            t_r = rstd_chain(lambda: ssb[0:n, col:col + 1], t_ss, col, 1, n, D)
            t_xs = S.add("act", lambda e: e.activation(out=xsb[i][0:n, :], in_=xap, func=AF.Copy,
                                                       scale=rsb[0:n, col:col + 1]), [t_r] + xsb_free[i])
            return dict(i=i, t_xs=t_xs, t_cast=t_xs, n=n, col=col, t_r=t_r)

        def norm_p2(ctx, gcol, dst, hcol0, hT_war):
            i, t_xs, n = ctx["i"], ctx["t_cast"], ctx["n"]
            bi, bk, bfree = banks.get()
            bkb = bk.bitcast(BF16)

            def _tr(e):
                ins = None
                for c in range(8):
                    ins = e.transpose(bkb[:, c * 128:c * 128 + n], xsb[i][0:n, c * 128:(c + 1) * 128],
                                      identb[0:n, 0:n])
                return ins
            t_tr = S.add("pe", _tr, [t_xs, t_ident] + bfree)
            xsb_free[i] = [t_tr]
            src = bkb[:, :].rearrange("p (c t) -> p c t", c=8)[:, :, 0:n]
            gap = cols[:, gcol:gcol + 8].unsqueeze(2).to_broadcast([128, 8, n])
            t_h = S.add("dve", lambda e: e.tensor_tensor(out=dst[:, :, hcol0:hcol0 + n], in0=src, in1=gap,
                                                         op=ALU.mult), [t_tr, const_tok] + hT_war)
            banks.release(bi, t_h)
            return t_h

        def norm_to_hT(xap, n, gcol, hcol0, x_deps, hT_war, dst=None, defer=None, info=None):
            ctx = norm_p1(xap, n, x_deps, defer=defer)
            t_h = norm_p2(ctx, gcol, hT if dst is None else dst, hcol0, hT_war)
            if info is not None:
                info.update(col=ctx["col"], t_r=ctx["t_r"])
            return t_h, ctx["t_xs"]

        hT_readers = []
        hTA_readers = []
        x_ld = {}
        p_ld = {}
        y_st = {}
        out_toks = []
        pt_free = [[], []]
        ab_free = [[], []]
        an_free = [[], []]
        tg_free = [[], []]
        tp_free = [[]]
        state = dict(mix_readers=[], hid_readers=[], QT_readers=[], uT_readers=[], dT_readers=[], sq_readers=[],
                     pT_readers=[], stage_free=[], ptmp_free=[], uT_halo=[], KTs_readers=[], halo_read=[], pre={},
                     kt_readers=[[], []], v_readers=[[], []], rstda={}, uTb_readers=[])

        def issue_x_load(t):
            bi = t % 2
            deps = []
            if t - 2 in y_st:
                deps.append(y_st[t - 2])
            if t == 1:
                deps += state["halo_read"]
            pdeps = list(state["pT_readers"]) if t >= 2 else []
            if t < NT:
                src = xin[HALO + t * T: HALO + (t + 1) * T, :].rearrange("(b p) d -> p b d", p=128)
                x_ld[t] = S.dma("sp", lambda eng: eng.dma_start(out=xbuf[bi][:, :, :], in_=src), f"x{bi}", deps)
                psrc = pin[t * T:(t + 1) * T, :].rearrange("(b p) d -> p b d", p=128)
                p_ld[t] = S.dma("pool", lambda eng: eng.dma_start(out=pbuf[bi][:, :, :], in_=psrc), f"p{bi}", pdeps)
            else:
                x_ld[t] = S.dma("sp", lambda eng: eng.dma_start(out=xbuf[bi][0:NSAMP, 0, :], in_=xs_d[:, :]),
                                f"x{bi}", deps)
                p_ld[t] = S.dma("pool", lambda eng: eng.dma_start(out=pbuf[bi][0:NSAMP, 0, :], in_=ps_d[:, :]),
                                f"p{bi}", pdeps)

        xhalo = xbuf[1][:, 0, :]
        halo_tok = S.dma("sp", lambda eng: eng.dma_start(out=xhalo, in_=xin[0:HALO, :]), "x1")
        issue_x_load(0)

        S.dma("sp", lambda eng: eng.dma_start(out=ckf[:, :, :], in_=ck_d.rearrange("(s p) f -> p s f", p=128)), "ckv")
        S.dma("sp", lambda eng: eng.dma_start(out=cvf[:, :, :], in_=cv_d.rearrange("(s p) f -> p s f", p=128)), "ckv")
        ckv_tok = S.dma("sp", lambda eng: eng.dma_start(out=spf[0:60, :], in_=spool_d[:, :]), "ckv")

        def attn_post_a(ob, tpv, n, seq):
            i = seq % 2
            o3 = [ob[0][1][0:n, 0:260].rearrange("p (h d) -> p h d", h=4),
                  ob[1][1][0:n, 0:260].rearrange("p (h d) -> p h d", h=4)]
            tds = []
            for hb in range(2):
                td = S.add("dve", lambda e, hb=hb: e.tensor_tensor(
                    out=den[i][0:n, hb * 4:(hb + 1) * 4].unsqueeze(2), in0=o3[hb][:, :, 64:65],
                    in1=esink[0:n, hb * 4:(hb + 1) * 4].unsqueeze(2), op=ALU.add), [tpv, t_esink] + ab_free[i])
                tds.append(td)
            trc = S.add("dve", lambda e: e.reciprocal(out=rcp[i][0:n, :], in_=den[i][0:n, :]), tds)
            tas = []
            for hb in range(2):
                ta = S.add("dve", lambda e, hb=hb: e.tensor_tensor(
                    out=anb[i][0:n, hb * 256:(hb + 1) * 256].rearrange("p (h d) -> p h d", h=4),
                    in0=o3[hb][:, :, 0:64],
                    in1=rcp[i][0:n, hb * 4:(hb + 1) * 4].unsqueeze(2).to_broadcast([n, 4, 64]), op=ALU.mult),
                    [trc] + an_free[i])
                tas.append(ta)
                banks.release(ob[hb][0], [ta, tds[hb]])
            col = ss_col()
            t_ss = S.add("dve", lambda e: e.scalar_tensor_tensor(
                out=sqj[0:n, :], in0=anb[i][0:n, :], scalar=1.0, in1=anb[i][0:n, :], op0=ALU.mult, op1=ALU.mult,
                accum_out=ssb[0:n, col:col + 1]), tas + [t_setup])
            t_r = rstd_chain(lambda: ssb[0:n, col:col + 1], t_ss, col, 1, n, 512)
            ab_free[i] = tas
            return dict(tas=tas + [t_ss], col=col, t_r=t_r)

        def attn_post_b(actx, n, b, seq, mix_war):
            i = seq % 2
            t_an = actx["tas"]
            state["rstda"][b] = (actx["col"], actx["t_r"])
            bi, bk, bfree = banks.get()
            bkb = bk.bitcast(BF16)

            def _tr(e):
                ins = None
                for c in range(4):
                    ins = e.transpose(bkb[:, c * 128:c * 128 + n], anb[i][0:n, c * 128:(c + 1) * 128],
                                      identb[0:n, 0:n])
                return ins
            t_tr = S.add("pe", _tr, t_an + [t_ident] + bfree)
            an_free[i] = [t_tr]
            src = bkb[:, 0:512].rearrange("p (c t) -> p c t", c=4)[:, :, 0:n]
            gap = cols[:, C_GATT:C_GATT + 4].unsqueeze(2).to_broadcast([128, 4, n])
            t_m = S.add("dve", lambda e: e.tensor_tensor(out=mixT[:, 0:4, b * 128:b * 128 + n], in0=src, in1=gap,
                                                         op=ALU.mult), [t_tr, const_tok] + mix_war)
            banks.release(bi, t_m)
            return t_m

        def attn_post(ob, tpv, n, b, seq, mix_war):
            return attn_post_b(attn_post_a(ob, tpv, n, seq), n, b, seq, mix_war)

        def prompt_attn_scores(t, b, q_toks, k_tok):
            gb = 1 + t * NBLK + b
            seq = t * NBLK + b
            pi = seq % 2
            ptv = PT[pi][:, :].rearrange("p (kb kv h q) -> p kb kv h q", kb=2, kv=2, h=4)
            exp_toks = []

            def score(kv, kbi, kb):
                bi, bk, bfree = banks.get()
                kdeps = state["k_halo_tok"] if kb == 0 else []
                tm = S.add("pe", lambda e: e.matmul(
                    bk[:, 0:512].rearrange("p (h q) -> p h q", h=4),
                    lhsT=KT2[t % 2][kv * 64:(kv + 1) * 64, (b + kbi) * 128:(b + kbi + 1) * 128],
                    rhs=QT[kv * 64:(kv + 1) * 64, :, b * 128:(b + 1) * 128], start=True, stop=True),
                    [k_tok] + q_toks + kdeps + bfree)
                state["QT_readers"].append(tm)
                state["kt_readers"][t % 2] = [tm]
                bkv = bk[:, 0:512].rearrange("p (h q) -> p h q", h=4)
                deps = [tm, const_tok, t_setup] + pt_free[pi]
                if kbi == 0:
                    if kb == 0:
                        ta = S.add("act", lambda e: e.activation(
                            out=ptv[:, 0, kv, :, 0:64], in_=bkv[:, :, 0:64], func=AF.Exp,
                            bias=cols[:, C_HBIAS:C_HBIAS + 1], scale=0.125), deps)
                        tb = S.add("act", lambda e: e.activation(
                            out=ptv[64:128, 0, kv, :, 64:128], in_=bkv[64:128, :, 64:128], func=AF.Exp,
                            bias=cols[64:128, C_HBIAS:C_HBIAS + 1], scale=0.125), deps)
                    else:
                        ta = S.add("act", lambda e: e.activation(
                            out=ptv[:, 0, kv, :, 0:64], in_=bkv[:, :, 0:64], func=AF.Exp, scale=0.125), deps)
                        tb = S.add("act", lambda e: e.activation(
                            out=ptv[64:128, 0, kv, :, 64:128], in_=bkv[64:128, :, 64:128], func=AF.Exp,
                            scale=0.125), deps)
                else:
                    ta = S.add("act", lambda e: e.activation(
                        out=ptv[:, 1, kv, :, 64:128], in_=bkv[:, :, 64:128], func=AF.Exp, scale=0.125), deps)
                    tb = S.add("act", lambda e: e.activation(
                        out=ptv[0:64, 1, kv, :, 0:64], in_=bkv[0:64, :, 0:64], func=AF.Exp, scale=0.125), deps)
                banks.release(bi, [ta, tb])
                exp_toks.extend([ta, tb])

            for kv in range(2):
                for kbi, kb in enumerate((gb - 1, gb)):
                    score(kv, kbi, kb)
            return dict(gb=gb, seq=seq, pi=pi, ptv=ptv, exp_toks=exp_toks, b=b, t=t)

        def prompt_attn_pv(ctx, v_toks):
            gb, seq, pi, ptv, exp_toks, b, t = (ctx[k] for k in ("gb", "seq", "pi", "ptv", "exp_toks", "b", "t"))
            ob = [banks.get(), banks.get()]
            vdeps = [v_toks[b]]
            if b == 0:
                vdeps.append(state["v_prev_tok"] if t > 0 else v_toks["halo"])
            else:
                vdeps.append(v_toks[b - 1])

            def _pv(e):
                ins = None
                for h in range(8):
                    kv, g = divmod(h, 4)
                    o = ob[h // 4][1]
                    for kbi, kb in enumerate((gb - 1, gb)):
                        ins = e.matmul(o[:, (h % 4) * 65:(h % 4) * 65 + 65], lhsT=ptv[:, kbi, kv, g, :],
                                       rhs=V12[t % 2][:, b + kbi, kv, :], start=(kbi == 0), stop=(kbi == 1))
                return ins
            tpv = S.add("pe", _pv, exp_toks + vdeps + ob[0][2] + ob[1][2])
            pt_free[pi] = [tpv]
            state["v_readers"][t % 2] = [tpv]
            return attn_post_a(ob, tpv, 128, seq)

        def sample_attention(q_toks, k_tok, v_tok, mix_war):
            n = NSAMP
            kc_toks = []

            tcb = S.add("pool", lambda e: e.tensor_copy(ckb[:, :, :], ckf[:, :, :]), [ckv_tok])
            bi, bk, bfree = banks.get()
            bkb = bk.bitcast(BF16)

            def _ktr(e):
                ins = None
                for s in range(4):
                    ins = e.transpose(bkb[:, s * 128:(s + 1) * 128], ckb[:, s, :], identb[:, :])
                return ins
            tm = S.add("pe", _ktr, [tcb, t_ident] + bfree)
            te = S.add("dve", lambda e: e.tensor_copy(KcT[:, :, :], bkb[:, 0:512].rearrange("p (s k) -> p s k", s=4)),
                       [tm])
            banks.release(bi, te)
            kc_toks.append(te)
            tvc = S.add("dve", lambda e: e.tensor_copy(Vc1[:, :, :, 0:64],
                                                       cvf[:, :, :].rearrange("p s (k d) -> p s k d", k=2)),
                        [ckv_tok, t_setup])
            ptc = PT[0][:, :].rearrange("p (s h q) -> p s h q", s=4, h=8)
            if DBG_STOP == "2:SA1":
                return None
            ptn = PT[1][0:64, 0:512].rearrange("p (h q) -> p h q", h=8)
            tz = S.add("pool", lambda e: e.memset(PT[0][:, :], 0.0), pt_free[0])
            if DBG_STOP == "2:SA1b":
                return None
            cs = [[banks.get() for half in range(2)] for kv in range(2)]

            def _sc(e):
                ins = None
                for s in range(4):
                    for kv in range(2):
                        o = cs[kv][s // 2][1][:, (s % 2) * 256:(s % 2 + 1) * 256].rearrange("p (h q) -> p h q", h=4)
                        ins = e.matmul(o, lhsT=KcT[kv * 64:(kv + 1) * 64, s, :],
                                       rhs=QT[kv * 64:(kv + 1) * 64, :, 0:64], start=True, stop=True)
                return ins
            tsc = S.add("pe", _sc, kc_toks + q_toks + [x for r in cs for bkx in r for x in bkx[2]])
            state["QT_readers"].append(tsc)
            if DBG_STOP == "2:SA2a":
                S.add("dve", lambda e: e.tensor_copy(etmp[:, 0:64], cs[0][0][1][0:64, 0:64]), [tsc])
                return None
            e_toks = []
            rel = {(kv, half): [] for kv in range(2) for half in range(2)}
            for s in range(4):
                for kv in range(2):
                    src = cs[kv][s // 2][1][:, (s % 2) * 256:(s % 2 + 1) * 256].rearrange(
                        "p (h q) -> p h q", h=4)[:, :, s * 16:(s + 1) * 16]
                    te = S.add("act", lambda e, s=s, kv=kv, src=src: e.activation(
                        out=ptc[:, s, kv * 4:(kv + 1) * 4, s * 16:(s + 1) * 16], in_=src,
                        func=AF.Exp, scale=0.125), [tsc, tz])
                    e_toks.append(te)
                    rel[(kv, s // 2)].append(te)
            for kv in range(2):
                for half in range(2):
                    banks.release(cs[kv][half][0], rel[(kv, half)])
            if DBG_STOP == "2:SA2":
                return None
            ns = [banks.get() for kv in range(2)]

            def _sn(e):
                ins = None
                for kv in range(2):
                    ins = e.matmul(ns[kv][1][0:64, 0:256].rearrange("p (h q) -> p h q", h=4),
                                   lhsT=KTs[kv * 64:(kv + 1) * 64, 0:64],
                                   rhs=QT[kv * 64:(kv + 1) * 64, :, 0:64], start=True, stop=True)
                return ins
            tsn = S.add("pe", _sn, [k_tok] + q_toks + ns[0][2] + ns[1][2])
            state["QT_readers"].append(tsn)
            state["KTs_readers"] = [tsn]
            tens = []
            for kv in range(2):
                ten = S.add("act", lambda e, kv=kv: e.activation(out=etmp[:, kv * 256:(kv + 1) * 256],
                                                                 in_=ns[kv][1][0:64, 0:256], func=AF.Exp, scale=0.125),
                            [tsn])
                banks.release(ns[kv][0], ten)
                tens.append(ten)
            tmk = S.add("dve", lambda e: e.tensor_tensor(
                out=ptn, in0=etmp[:, :].rearrange("p (h q) -> p h q", h=8),
                in1=bmask[:, :].unsqueeze(1).to_broadcast([64, 8, 64]), op=ALU.mult), tens + [const_tok] + pt_free[1])
            if DBG_STOP == "2:SA3":
                return None
            ob = [banks.get(), banks.get()]

            def _pv(e):
                ins = None
                for h in range(8):
                    kv = h // 4
                    o = ob[h // 4][1]
                    oc = o[0:64, (h % 4) * 65:(h % 4) * 65 + 65]
                    for s in range(4):
                        ins = e.matmul(oc, lhsT=ptc[:, s, h, :], rhs=Vc1[:, s, kv, :], start=(s == 0), stop=False)
                    ins = e.matmul(oc, lhsT=ptn[:, h, :], rhs=V1s[0:64, kv, :], start=False, stop=True)
                return ins
            tpv = S.add("pe", _pv, e_toks + [tmk, tvc, v_tok] + ob[0][2] + ob[1][2])
            pt_free[0] = [tpv]
            pt_free[1] = [tpv]
            if DBG_STOP == "2:SA4":
                return None
            return attn_post(ob, tpv, n, 0, 0, mix_war)

        def pooling(kind, TT, blocks, u_toks, mix_war, pre_group=None):
            sample = kind == "sample"
            if sample:
                def uview(g):
                    return uTs[:, g, :, :]
                tmpv = [ptmps[0][:, :, :], ptmps[1][:, :, :]]
                L = 32
            else:
                def uview(g):
                    return uT[:, g, :].unsqueeze(1)
                tmpv = [ptmp[0][:, :].unsqueeze(1), ptmp[1][:, :].unsqueeze(1)]
                L = 16 + T
            nseg = 4 if sample else 1
            d_war = state["dT_readers"]
            state["dT_readers"] = []
            d_toks = []
            prev = list(state["ptmp_free"])

            def group(g, prev):
                w = 2 << g
                src = uview(g)
                cur_tok = list(u_toks[g]) + (state["spT_tok"] if sample else state["uT_halo"])
                lo = 0
                shift = 1
                step = 0
                if sample:
                    dstd = dT[:, g, 0:TT].rearrange("p (s j) -> p s j", s=4)
                else:
                    dstd = dT[:, g, 0:TT].unsqueeze(1)
                while shift < w:
                    last = (shift * 2 >= w)
                    if last and kind != "first":
                        tk = S.add("pool", (lambda e, src=src, shift=shift: e.tensor_tensor(
                            out=dstd, in0=src[:, :, 16:L], in1=src[:, :, 16 - shift:L - shift], op=ALU.add)),
                            cur_tok + prev + d_war)
                        return [tk]
                    dst = tmpv[step % 2]
                    nlo = lo + shift
                    tk = S.add("pool", (lambda e, dst=dst, src=src, lo=lo, nlo=nlo, shift=shift: e.tensor_tensor(
                        out=dst[:, :, nlo:L], in0=src[:, :, nlo:L], in1=src[:, :, lo:L - shift], op=ALU.add)),
                        cur_tok + prev)
                    prev = []
                    cur_tok = [tk]
                    src = dst
                    lo = nlo
                    shift *= 2
                    step += 1
                s_new = src[:, :, 16:L]
                ic = invcnt[:, g * 16:(g + 1) * 16].unsqueeze(1)
                t1 = S.add("pool", lambda e: e.tensor_tensor(
                    out=s_new[:, :, 0:16], in0=s_new[:, :, 0:16], in1=ic, op=ALU.mult), cur_tok + [const_tok])
                t2 = S.add("pool", lambda e: e.tensor_copy(dstd, s_new), [t1] + d_war)
                return [t2]

            d_by_g = {}
            for g in (3, 2, 1, 0):
                if pre_group is not None:
                    pre_group(g)
                prev = group(g, prev)
                d_by_g[g] = prev
            d_toks = [d_by_g[g] for g in range(4)]
            state["ptmp_free"] = prev
            alld = [x for d in d_toks for x in d]
            if not sample:
                tc = S.add("pool", lambda e: e.tensor_copy(uT[:, :, 0:16], uT[:, :, T:T + 16]), alld)
                state["uT_readers"] = [tc]
                state["uT_halo"] = [tc]
            else:
                state["uT_readers"] = alld
            return d_toks

        def pooling2(kind, TT, blocks, d_toks, mix_war):
            sq_war = state["sq_readers"]
            state["sq_readers"] = []
            state["uTb_readers"] = []
            pool_toks = []
            sq_toks = []

            def pmm(g):
                bi, bk, bfree = banks.get()
                def _pm(e):
                    e.matmul(bk[:, 0:TT], lhsT=wpool[:, g * 128:(g + 1) * 128], rhs=dT[:, g, 0:TT],
                             start=True, stop=False)
                    return e.matmul(bk[:, 0:TT], lhsT=wpool_n[:, g * 128:(g + 1) * 128], rhs=uTb[:, g, 0:TT],
                                    start=False, stop=True)
                tm = S.add("pe", _pm, d_toks[g] + [wp_tok, t_wpn, state["ub_toks"][g]] + bfree)
                state["dT_readers"].append(tm)
                state["uTb_readers"].append(tm)
                t1 = S.add("act", lambda e: e.activation(out=mixT[:, 4 + g, 0:TT], in_=bk[:, 0:TT],
                                                         func=AF.Copy, scale=psgw[:, g:g + 1]),
                           [tm, t_psgw] + mix_war)
                t2 = S.add("act", lambda e: e.activation(out=sqb[:, g, 0:TT], in_=bk[:, 0:TT], func=AF.Square,
                                                         scale=pscw[:, g:g + 1]),
                           [tm, t_pscw] + sq_war)
                banks.release(bi, [t1, t2])
                pool_toks.append(t1)
                sq_toks.append(t2)
            for g in range(4):
                pmm(g)
            nb = len(blocks)
            col0 = ss_ctr[0]
            cols_b = {}
            for (b, n) in blocks:
                cols_b[b] = ss_col()
            return pool_toks, dict(sq_toks=sq_toks, col0=col0, cols_b=cols_b, nb=nb, blocks=blocks)

        def pooling3(ctx3):
            sq_toks, col0, cols_b, nb, blocks = (ctx3[k] for k in ("sq_toks", "col0", "cols_b", "nb", "blocks"))
            bi, bk, bfree = banks.get()

            def _ss(e):
                ins = None
                for (b, n) in blocks:
                    for g in range(4):
                        ins = e.matmul(bk[0:n, b:b + 1], lhsT=sqb[:, g, b * 128:b * 128 + n], rhs=ones[:, 0:1],
                                       start=(g == 0), stop=(g == 3))
                return ins
            tss = S.add("pe", _ss, sq_toks + [t_setup] + bfree)
            state["sq_readers"].append(tss)
            nmax = blocks[0][1]
            tr = rstd_chain(lambda: bk[0:nmax, 0:nb], tss, col0, nb, nmax, 512)
            banks.release(bi, tr)
            return {b: (cols_b[b], tr) for (b, n) in blocks}

        def run_tile(tl):
            kind = tl["kind"]
            t = tl["idx"]
            sample = kind == "sample"
            xb = xbuf[t % 2]
            pb = pbuf[t % 2]
            if sample:
                blocks = [(0, NSAMP)]
                TT = NSAMP
            else:
                blocks = [(b, 128) for b in range(NBLK)]
                TT = T
            xtok = x_ld[t]
            H0 = HALO

            war = list(hTA_readers)
            hTA_readers.clear()
            h_toks = []
            x_toks = {b: [xtok] for (b, n) in blocks}
            if kind == "first":
                th, txs = norm_to_hT(xhalo, 128, C_LNMIX, 0, [halo_tok], war, dst=hTA)
                h_toks.append(th)
                state["halo_read"] = [txs]
            if t + 1 <= NT:
                issue_x_load(t + 1)
            if t in state["pre"]:
                pre = state["pre"].pop(t)
                h_toks.extend(pre["h_toks"])
                for (b, n) in blocks:
                    x_toks[b].append(pre["txs"][b])
            else:
                for (b, n) in blocks:
                    th, txs = norm_to_hT(xb[0:n, b, :], n, C_LNMIX, H0 + b * 128, [xtok], war, dst=hTA)
                    h_toks.append(th)
                    x_toks[b].append(txs)

            if DBG_STOP == f"{t}:A":
                return
            wvq, wtokq, wiq = w_next("inq")
            qt_war = state["QT_readers"]
            state["QT_readers"] = []
            q_toks = []
            last_pe = [None]

            def q_chunk(j):
                bi, bk, bfree = banks.get()

                def _mm(e):
                    ins = None
                    for k in range(8):
                        ins = e.matmul(bk[:, 0:TT], lhsT=wvq[:, k, j * 128:(j + 1) * 128],
                                       rhs=hTA[:, k, H0:H0 + TT], start=(k == 0), stop=(k == 7))
                    return ins
                tm = S.add("pe", _mm, [wtokq] + h_toks + bfree)
                te = S.add("act", lambda e: e.activation(out=QT[:, j, 0:TT], in_=bk[:, 0:TT], func=AF.Copy),
                           [tm] + qt_war)
                banks.release(bi, te)
                q_toks.append(te)
                last_pe[0] = tm
            for j in range(4):
                q_chunk(j)
            w_done(wiq, last_pe[0])
            hTA_readers.append(last_pe[0])

            wvk, wtokk, wik = w_next("inku")
            wv2, wtok2, wi2 = w_next("inuv")
            c0 = 0 if kind == "first" else H0
            ncol = TT + (HALO if kind == "first" else 0)
            ut_war = state["uT_readers"]
            state["uT_readers"] = []
            u_toks = {g: [] for g in range(4)}
            ub_toks = {}
            state["ub_toks"] = ub_toks
            k_toks = []
            gtok0 = HALO + t * T

            def ku_piece(j, pc0, pn):
                wsrc, wdep = (wvk, wtokk) if j < 4 else (wv2, wtok2)
                wc = j * 128 if j < 4 else 0
                bi, bk, bfree = banks.get()

                def _mm(e):
                    ins = None
                    for k in range(8):
                        ins = e.matmul(bk[:, 0:pn], lhsT=wsrc[:, k, wc:wc + 128], rhs=hTA[:, k, pc0:pc0 + pn],
                                       start=(k == 0), stop=(k == 7))
                    return ins
                tm = S.add("pe", _mm, [wdep] + h_toks + bfree)
                last_pe[0] = tm
                is_halo_piece = (kind == "first" and pc0 == 0)
                if j == 0:
                    if sample:
                        te = S.add("act", lambda e: e.activation(out=KTs[:, 0:TT], in_=bk[:, 0:TT], func=AF.Copy),
                                   [tm] + state["KTs_readers"])
                        k_toks.append(te)
                        rel_extra = []
                    else:
                        kc0 = 0 if is_halo_piece else HALO
                        te = S.add("act", lambda e: e.activation(out=KT2[t % 2][:, kc0:kc0 + pn], in_=bk[:, 0:pn],
                                                                 func=AF.Copy), [tm] + state["kt_readers"][t % 2])
                        rel_extra = []
                        if is_halo_piece:
                            state["k_halo_tok"] = [te]
                        else:
                            k_toks.append(te)
                            if t + 1 < NT:
                                tc_ = S.add("act", lambda e: e.activation(
                                    out=KT2[(t + 1) % 2][:, 0:HALO], in_=bk[:, pn - HALO:pn], func=AF.Copy),
                                    [tm] + state["kt_readers"][(t + 1) % 2])
                                state["k_halo_tok"] = [tc_]
                                rel_extra = [tc_]
                else:
                    g = j - 1
                    if sample:
                        te = S.add("act", lambda e: e.activation(
                            out=uTs[:, g, :, 16:32], in_=bk[:, 0:TT].rearrange("p (s j) -> p s j", s=4),
                            func=AF.Copy), [tm, t_setup] + ut_war)
                    elif is_halo_piece:
                        te = S.add("act", lambda e: e.activation(out=uT[:, g, 0:16], in_=bk[:, HALO - 16:HALO],
                                                                 func=AF.Copy), [tm, t_setup] + ut_war)
                    else:
                        te = S.add("act", lambda e: e.activation(out=uT[:, g, 16:16 + pn], in_=bk[:, 0:pn],
                                                                 func=AF.Copy), [tm, t_setup] + ut_war)
                    u_toks[g].append(te)
                    rel_extra = []
                    if not is_halo_piece:
                        npc = TT if sample else pn
                        teb = S.add("act", lambda e: e.activation(out=uTb[:, g, 0:npc], in_=bk[:, 0:npc], func=AF.Copy),
                                    [tm] + state["uTb_readers"])
                        ub_toks[g] = teb
                        rel_extra = [teb]
                banks.release(bi, [te] + rel_extra)

            inku_left = [4]

            def ku_chunk(j):
                pieces = [(c0, ncol)] if ncol <= 512 else [(c0, HALO), (c0 + HALO, TT)]
                for (pc0, pn) in pieces:
                    ku_piece(j, pc0, pn)
                if j < 4:
                    inku_left[0] -= 1
                    if inku_left[0] == 0:
                        w_done(wik, last_pe[0])
                hTA_readers.append(last_pe[0])

            interleave = not sample
            for j in range(1 if interleave else 5):
                ku_chunk(j)
            k_tok = k_toks[-1]

            v_toks = {}
            vblocks = ([("halo", 0, 128)] if kind == "first" else []) + [(b, H0 + b * 128, n) for (b, n) in blocks]

            def v_block(b, hc, n):
                bi, bk, bfree = banks.get()

                def _mm(e):
                    ins = None
                    for k in range(8):
                        ins = e.matmul(bk[0:n, 0:128], lhsT=hTA[:, k, hc:hc + n], rhs=wv2[:, k, 128:256],
                                       start=(k == 0), stop=(k == 7))
                    return ins
                tm = S.add("pe", _mm, [wtok2] + h_toks + bfree)
                last_pe[0] = tm
                if sample:
                    dst = V1s[0:n, :, 0:64]
                else:
                    lb = 0 if b == "halo" else 1 + b
                    dst = V12[t % 2][:, lb, :, 0:64]
                srcv = bk[0:n, 0:128].rearrange("p (k d) -> p k d", k=2)
                te = S.add("dve", lambda e: e.tensor_copy(dst, srcv),
                           [tm, t_setup] + ([] if sample else state["v_readers"][t % 2]))
                rel = [te]
                if (not sample) and b == NBLK - 1 and t + 1 < NT:
                    dst2 = V12[(t + 1) % 2][:, 0, :, 0:64]
                    tc_ = S.add("dve", lambda e: e.tensor_copy(dst2, srcv),
                                [tm, t_setup] + state["v_readers"][(t + 1) % 2])
                    rel.append(tc_)
                    state["v_prev_tok"] = tc_
                if (kind == "last" and b == NBLK - 1) or sample:
                    te2 = S.add("dve", lambda e: e.tensor_copy(stage[0:n, 128:256], bk[0:n, 0:128]),
                                [tm] + state["stage_free"])
                    rel.append(te2)
                    state["stage_v"] = te2
                banks.release(bi, rel)
                v_toks[b] = te
            for (b, hc, n) in vblocks:
                v_block(b, hc, n)
            inuv_done = [False]
            if kind not in ("last", "sample") and not interleave:
                w_done(wi2, last_pe[0])
                inuv_done[0] = True
            hTA_readers.append(last_pe[0])

            def out_kvu():
                wv3, wtok3, wi3 = w_next("int1")
                wv4, wtok4, wi4 = w_next("int2")
                if not interleave:
                    w_done(wi2, last_pe[0])
                    inuv_done[0] = True
                (b, n) = blocks[-1]
                hc = H0 + b * 128
                bi, bk, bfree = banks.get()

                def _mm(e):
                    ins = None
                    for k in range(8):
                        ins = e.matmul(bk[0:n, 0:512], lhsT=hTA[:, k, hc:hc + n], rhs=wv3[:, k, :],
                                       start=(k == 0), stop=(k == 7))
                    return ins
                tm = S.add("pe", _mm, [wtok3] + h_toks + bfree)
                w_done(wi3, tm)
                te1 = S.add("dve", lambda e: e.tensor_copy(stage[0:n, 0:128], bk[0:n, 0:128]),
                            [tm] + state["stage_free"])
                te2 = S.add("dve", lambda e: e.tensor_copy(stage[0:n, 256:640], bk[0:n, 128:512]),
                            [tm] + state["stage_free"])
                banks.release(bi, [te1, te2])
                bi2, bk2, bfree2 = banks.get()

                def _mm2(e):
                    ins = None
                    for k in range(8):
                        ins = e.matmul(bk2[0:n, 0:128], lhsT=hTA[:, k, hc:hc + n], rhs=wv4[:, k, :],
                                       start=(k == 0), stop=(k == 7))
                    return ins
                tm2 = S.add("pe", _mm2, [wtok4] + h_toks + bfree2)
                w_done(wi4, tm2)
                hTA_readers.append(tm2)
                te3 = S.add("dve", lambda e: e.tensor_copy(stage[0:n, 640:768], bk2[0:n, 0:128]),
                            [tm2] + state["stage_free"])
                banks.release(bi2, te3)
                dstd = kvu_s_d if sample else kvu_last_d
                st = S.dma("pool", lambda eng: eng.dma_start(out=dstd[:, :], in_=stage[0:n, :]),
                           "stage", [te1, te2, te3, state["stage_v"]])
                state["stage_free"] = [st]
                out_toks.append(st)
            if kind in ("last", "sample"):
                out_kvu()

            if DBG_STOP == f"{t}:INPROJ":
                return
            mix_war = state["mix_readers"]
            state["mix_readers"] = []
            mixa_toks = []
            if not sample:
                ctxs = {}
                tans = {}
                stepno = [0]

                def attn_step():
                    step = stepno[0]
                    stepno[0] += 1
                    if step < NBLK:
                        ctxs[step] = prompt_attn_scores(t, step, q_toks, k_tok)
                    if 1 <= step <= NBLK:
                        tans[step - 1] = prompt_attn_pv(ctxs[step - 1], v_toks)
                    if step >= 2 and step - 2 < NBLK:
                        bb = step - 2
                        mixa_toks.append(attn_post_b(tans[bb], 128, bb, ctxs[bb]["seq"], mix_war))

                def pre_group(g):
                    attn_step()
                    ku_chunk(g + 1)
                    if g == 3:
                        w_done(wi2, last_pe[0])
                        inuv_done[0] = True
                d_toks_pool = pooling(kind, TT, blocks, u_toks, mix_war, pre_group=pre_group)
                while stepno[0] < NBLK + 2:
                    attn_step()
            else:
                d_toks_pool = pooling(kind, TT, blocks, u_toks, mix_war)
                mixa_toks.append(sample_attention(q_toks, k_tok, v_toks[0], mix_war))
                if mixa_toks[-1] is None:
                    return
            assert inuv_done[0]

            if DBG_STOP == f"{t}:ATTN":
                return
            pool_toks, pool_ctx3 = pooling2(kind, TT, blocks, d_toks_pool, mix_war)
            rstdb_cols = {}
            if kind == "first":
                issue_convs(len(conv_order))

            if DBG_STOP == f"{t}:POOL":
                return
            def outproj_mm(hf, b, n, wv, wtok):
                ba = banks.get()
                bb = banks.get()

                def _mm(e):
                    ins = None
                    for k in range(4):
                        ins = e.matmul(ba[1][0:n, :], lhsT=mixT[:, k, b * 128:b * 128 + n], rhs=wv[:, k, :],
                                       start=(k == 0), stop=(k == 3))
                    for k in range(4, 8):
                        ins = e.matmul(bb[1][0:n, :], lhsT=mixT[:, k, b * 128:b * 128 + n], rhs=wv[:, k, :],
                                       start=(k == 4), stop=(k == 7))
                    return ins
                tm = S.add("pe", _mm, [wtok] + mixa_toks + pool_toks + ba[2] + bb[2])
                last_pe[0] = tm
                return (hf, b, n, tm, ba, bb)

            def outproj_evac(ctx_o):
                hf, b, n, tm, ba, bb = ctx_o
                xs_ = xb[0:n, b, hf * 512:(hf + 1) * 512]
                rc = rstdb_cols[b]
                t1 = S.add("dve", lambda e: e.scalar_tensor_tensor(
                    out=xs_, in0=bb[1][0:n, :], scalar=rsb[0:n, rc[0]:rc[0] + 1], in1=xs_, op0=ALU.mult,
                    op1=ALU.add), [tm, rc[1]] + x_toks[b])
                ra = state["rstda"][b]
                t2 = S.add("dve", lambda e: e.scalar_tensor_tensor(
                    out=xs_, in0=ba[1][0:n, :], scalar=rsb[0:n, ra[0]:ra[0] + 1], in1=xs_, op0=ALU.mult,
                    op1=ALU.add), [tm, t1, ra[1]])
                banks.release(ba[0], t2)
                banks.release(bb[0], t1)
                x_toks[b].append(t2)

            wvo = []
            for hf in range(2):
                wvo.append(w_next(f"out{hf}"))
            war = list(hT_readers)
            hT_readers.clear()
            h_toks = []

            ffn_rs = {}

            def norm_ffn_block(b, n):
                info = {}
                th, txs = norm_to_hT(xb[0:n, b, :], n, C_LNFFN, b * 128, x_toks[b], war, defer=-1.0, info=info)
                ffn_rs[b] = (info["col"], info["t_r"])
                h_toks.append(th)
                x_toks[b].append(txs)
            pend = None
            for bidx, (b, n) in enumerate(blocks):
                cts = [outproj_mm(hf, b, n, wvo[hf][0], wvo[hf][1]) for hf in range(2)]
                if bidx == 0:
                    rstdb_cols.update(pooling3(pool_ctx3))
                for c_ in cts:
                    outproj_evac(c_)
                if pend is not None:
                    norm_ffn_block(*pend)
                pend = (b, n)
            for hf in range(2):
                w_done(wvo[hf][2], last_pe[0])
            state["mix_readers"].append(last_pe[0])
            if DBG_STOP == f"{t}:OUTPROJ":
                return
            split_up = len(blocks) == NBLK
            if not split_up:
                norm_ffn_block(*pend)

            def up_chunk(i, c, wv, wtok, hid_war, hid_toks):
                bi, bk, bfree = banks.get()

                def _mm(e):
                    ins = None
                    for k in range(8):
                        ins = e.matmul(bk[:, 0:TT], lhsT=wv[:, k, c * 128:(c + 1) * 128],
                                       rhs=hT[:, k, 0:TT], start=(k == 0), stop=(k == 7))
                    return ins
                tm = S.add("pe", _mm, [wtok] + h_toks + bfree)
                last_pe[0] = tm
                fi = i * 4 + c
                tr = S.add("act", lambda e: e.activation(out=hidT[:, fi, 0:TT], in_=bk[:, 0:TT], func=AF.Relu),
                           [tm] + hid_war)
                banks.release(bi, tr)
                tsq = S.add("dve", lambda e: e.tensor_tensor(out=hidT[:, fi, 0:TT], in0=hidT[:, fi, 0:TT],
                                                             in1=hidT[:, fi, 0:TT], op=ALU.mult), [tr])
                hid_toks.append(tsq)

            pend_ffn = pend

            def up_first_split(wv, wtok, hid_war, hid_toks, pend_blk):
                NA = (NBLK - 1) * 128
                bks = [banks.get() for c in range(4)]
                tmA = []
                for c in range(4):
                    def _mmA(e, c=c):
                        ins = None
                        for k in range(8):
                            ins = e.matmul(bks[c][1][:, 0:NA], lhsT=wv[:, k, c * 128:(c + 1) * 128],
                                           rhs=hT[:, k, 0:NA], start=(k == 0), stop=(k == 7))
                        return ins
                    tmA.append(S.add("pe", _mmA, [wtok] + h_toks + bks[c][2]))
                norm_ffn_block(*pend_blk)
                for c in range(4):
                    def _mmB(e, c=c):
                        ins = None
                        for k in range(8):
                            ins = e.matmul(bks[c][1][:, NA:T], lhsT=wv[:, k, c * 128:(c + 1) * 128],
                                           rhs=hT[:, k, NA:T], start=(k == 0), stop=(k == 7))
                        return ins
                    tm = S.add("pe", _mmB, [wtok] + h_toks)
                    last_pe[0] = tm
                    fi = c
                    tr = S.add("act", lambda e, c=c, fi=fi: e.activation(out=hidT[:, fi, 0:T], in_=bks[c][1][:, 0:T],
                                                                         func=AF.Relu), [tm, tmA[c]] + hid_war)
                    banks.release(bks[c][0], tr)
                    tsq = S.add("dve", lambda e, fi=fi: e.tensor_tensor(out=hidT[:, fi, 0:T], in0=hidT[:, fi, 0:T],
                                                                        in1=hidT[:, fi, 0:T], op=ALU.mult), [tr])
                    hid_toks.append(tsq)

            def down_block(b, n, ch, kq, wv, wtok, acc, hid_toks):
                def _mm(e):
                    ins = None
                    for j in range(8):
                        ins = e.matmul(acc[1][0:n, :], lhsT=hidT[:, kq * 8 + j, b * 128:b * 128 + n],
                                       rhs=wv[:, j, :], start=(kq == 0 and j == 0), stop=(kq == 1 and j == 7))
                    return ins
                tm = S.add("pe", _mm, [wtok] + hid_toks[kq * 8:(kq + 1) * 8] + (acc[2] if kq == 0 else []))
                last_pe[0] = tm
                if kq == 1:
                    xs_ = xb[0:n, b, ch * 512:(ch + 1) * 512]
                    rs2 = ffn_rs[b]
                    ta = S.add("dve", lambda e: e.scalar_tensor_tensor(
                        out=xs_, in0=acc[1][0:n, :], scalar=rsb[0:n, rs2[0]:rs2[0] + 1], in1=xs_, op0=ALU.mult,
                        op1=ALU.add), [tm, rs2[1]] + x_toks[b])
                    banks.release(acc[0], ta)
                    x_toks[b].append(ta)

            pt_war = state["pT_readers"]
            state["pT_readers"] = []
            pT_toks = []

            def p_tr(b, n):
                bi, bk, bfree = banks.get()
                bkb = bk.bitcast(BF16)

                def _tr(e):
                    ins = None
                    for c in range(2):
                        ins = e.transpose(bkb[:, c * 128:c * 128 + n], pb[0:n, b, c * 128:(c + 1) * 128],
                                          identb[0:n, 0:n])
                    return ins
                tm = S.add("pe", _tr, [p_ld[t], t_ident] + bfree)
                last_pe[0] = tm
                src = bkb[:, 0:256].rearrange("p (c t) -> p c t", c=2)[:, :, 0:n]
                te = S.add("dve", lambda e: e.tensor_copy(pT[:, :, b * 128:b * 128 + n], src), [tm] + pt_war)
                banks.release(bi, te)
                pT_toks.append(te)
            for (b, n) in blocks:
                p_tr(b, n)
            p_read_tok = last_pe[0]

            h_toks_e = {}
            war_e = []

            ple_nr = {}

            def ple_neg_rstd(b, n, col, t_r):
                c2 = ss_col()
                t_nr = S.add("dve", lambda e: e.tensor_scalar_mul(out=rsb[0:n, c2:c2 + 1], in0=rsb[0:n, col:col + 1],
                                                                   scalar1=-1.0), [t_r])
                ple_nr[b] = (c2, t_nr)

            def norm_ple_block(b, n):
                info = {}
                th, txs = norm_to_hT(xb[0:n, b, :], n, C_LNPLE, b * 128, x_toks[b], war_e, defer=-0.5, info=info)
                ple_neg_rstd(b, n, info["col"], info["t_r"])
                h_toks_e[b] = th
                x_toks[b].append(txs)

            nt_ = t + 1
            pre_blocks = []
            if nt_ <= NT and not DBG_STOP:
                pre_blocks = [(0, NSAMP)] if nt_ == NT else [(bb, 128) for bb in range(NBLK)]
            pre_state = dict(next=0, pend=None, h_toks=[], txs={})
            war_A = list(hTA_readers)

            def prenorm_p2():
                ctx, bb, nn = pre_state["pend"]
                th = norm_p2(ctx, C_LNMIX, hTA, HALO + bb * 128, war_A)
                pre_state["h_toks"].append(th)
                pre_state["pend"] = None

            def prenorm_step():
                k = pre_state["next"]
                if pre_state["pend"] is not None:
                    prenorm_p2()
                if k < len(pre_blocks):
                    bb, nn = pre_blocks[k]
                    ctx = norm_p1(xbuf[nt_ % 2][0:nn, bb, :], nn, [x_ld[nt_]])
                    pre_state["txs"][bb] = ctx["t_xs"]
                    pre_state["pend"] = (ctx, bb, nn)
                    pre_state["next"] = k + 1

            def prenorm_flush():
                while pre_state["pend"] is not None or pre_state["next"] < len(pre_blocks):
                    prenorm_step()
                if pre_blocks:
                    hTA_readers.clear()
                    state["pre"][nt_] = dict(h_toks=pre_state["h_toks"], txs=pre_state["txs"])

            pend = None
            for fh in range(2):
                hid_war = state["hid_readers"]
                state["hid_readers"] = []
                hid_toks = []
                for i in range(4):
                    wv, wtok, wi = w_next(f"up{fh * 4 + i}")
                    if fh == 0 and i == 0 and split_up:
                        up_first_split(wv, wtok, hid_war, hid_toks, pend_ffn)
                    else:
                        for c in range(4):
                            up_chunk(i, c, wv, wtok, hid_war, hid_toks)
                    w_done(wi, last_pe[0])
                hT_readers.append(last_pe[0])
                if fh == 1:
                    war_e.extend(hT_readers)
                    hT_readers.clear()
                for ch in range(2):
                    wd = [w_next(f"dn{fh}{ch}{kq}") for kq in range(2)]
                    for (b, n) in blocks:
                        acc = banks.get()
                        for kq in range(2):
                            down_block(b, n, ch, kq, wd[kq][0], wd[kq][1], acc, hid_toks)
                        if fh == 1 and ch == 0:
                            prenorm_step()
                        if fh == 1 and ch == 1:
                            if b == blocks[0][0]:
                                prenorm_flush()
                            if pend is not None:
                                norm_ple_block(*pend)
                            pend = (b, n)
                    for kq in range(2):
                        w_done(wd[kq][2], last_pe[0])
                    state["hid_readers"].append(last_pe[0])
            if DBG_STOP == f"{t}:FFN":
                return
            if len(blocks) == 1:
                norm_ple_block(*pend)
                pend = None
            else:
                (b_l, n_l) = pend
                ctx_l = norm_p1(xb[0:n_l, b_l, :], n_l, x_toks[b_l], defer=-0.5)
                ple_neg_rstd(b_l, n_l, ctx_l["col"], ctx_l["t_r"])
                x_toks[b_l].append(ctx_l["t_xs"])

                def ple_last_p2():
                    h_toks_e[b_l] = norm_p2(ctx_l, C_LNPLE, hT, b_l * 128, war_e)

            wvp, wtokp, wip = w_next("ple")
            wvg = [w_next(f"gate{hf}") for hf in range(2)]

            def gate_block(hf, b, n, wv, wtok):
                bg = banks.get()
                bp = banks.get()

                def _mm(e):
                    ins = None
                    for k in range(8):
                        ins = e.matmul(bg[1][0:n, :], lhsT=hT[:, k, b * 128:b * 128 + n], rhs=wv[:, k, :],
                                       start=(k == 0), stop=(k == 7))
                    for k in range(2):
                        ins = e.matmul(bp[1][0:n, :], lhsT=pT[:, k, b * 128:b * 128 + n],
                                       rhs=wvp[:, k, hf * 512:(hf + 1) * 512], start=(k == 0), stop=(k == 1))
                    return ins
                tm = S.add("pe", _mm, [wtok, wtokp, h_toks_e[b]] + pT_toks + bg[2] + bp[2])
                last_pe[0] = tm
                gi = (b + hf) % 2
                nr = ple_nr[b]
                ta1 = S.add("act", lambda e: e.activation(out=tg[gi][0:n, :], in_=bg[1][0:n, :], func=AF.Exp,
                                                          scale=rsb[0:n, nr[0]:nr[0] + 1]), [tm, nr[1]] + tg_free[gi])
                banks.release(bg[0], ta1)
                ta2 = S.add("act", lambda e: e.activation(out=tg[gi][0:n, :], in_=tg[gi][0:n, :], func=AF.Ln,
                                                          bias=oneb[0:n, :], scale=1.0), [ta1, t_setup])
                ta3 = S.add("act", lambda e: e.activation(out=tg[gi][0:n, :], in_=tg[gi][0:n, :], func=AF.Exp,
                                                          scale=-1.0), [ta2])
                t1 = S.add("dve", lambda e: e.tensor_tensor(out=tp[0][0:n, :], in0=tg[gi][0:n, :],
                                                            in1=bp[1][0:n, :], op=ALU.mult),
                           [tm, ta3] + tp_free[0])
                banks.release(bp[0], t1)
                tg_free[gi] = [t1]
                xs_ = xb[0:n, b, hf * 512:(hf + 1) * 512]
                t2 = S.add("dve", lambda e: e.tensor_tensor(out=xs_, in0=tp[0][0:n, :], in1=xs_, op=ALU.add),
                           [t1] + x_toks[b])
                tp_free[0] = [t2]
                x_toks[b].append(t2)

            fin = []

            def final_block(b, n):
                col = ss_col()
                xap = xb[0:n, b, :]
                t_ss = S.add("act", lambda e: e.activation(out=junk[0:n, :], in_=xap, func=AF.Square,
                                                           accum_out=ssb[0:n, col:col + 1]), x_toks[b] + [t_setup])
                t_r = rstd_chain(lambda: ssb[0:n, col:col + 1], t_ss, col, 1, n, D)
                t_y = S.add("dve", lambda e: e.scalar_tensor_tensor(
                    out=xap, in0=xap, scalar=rsb[0:n, col:col + 1], in1=lnf[0:n, :], op0=ALU.mult, op1=ALU.mult),
                    [t_r, const_tok, t_ss] + x_toks[b])
                fin.append(t_y)

            pendf = None
            for (b, n) in blocks:
                for hf in range(2):
                    gate_block(hf, b, n, wvg[hf][0], wvg[hf][1])
                if pend is not None and b == blocks[1][0]:
                    ple_last_p2()
                    pend = None
                if pendf is not None:
                    final_block(*pendf)
                pendf = (b, n)
            for hf in range(2):
                w_done(wvg[hf][2], last_pe[0])
            w_done(wip, last_pe[0])
            hT_readers.append(last_pe[0])
            state["pT_readers"] = [last_pe[0], p_read_tok]
            if DBG_STOP == f"{t}:GATE":
                return
            final_block(*pendf)

            if sample:
                st = S.dma("sp", lambda eng: eng.dma_start(out=ys_d[:, :], in_=xb[0:NSAMP, 0, :]), f"y{t % 2}", fin)
            else:
                dsty = y_d[t * T:(t + 1) * T, :].rearrange("(b p) d -> p b d", p=128)
                st = S.dma("sp", lambda eng: eng.dma_start(out=dsty, in_=xb[:, :, :]), f"y{t % 2}", fin)
            y_st[t] = st
            out_toks.append(st)

        def prep_sample_hist():
            toks = []

            tcb = S.add("pool", lambda e: e.tensor_copy(spb[0:60, :], spf[0:60, :]), [ckv_tok])
            bi, bk, bfree = banks.get()
            bkb = bk.bitcast(BF16)

            def _str(e):
                ins = None
                for g in range(4):
                    ins = e.transpose(bkb[:, g * 64:g * 64 + 60], spb[0:60, g * 128:(g + 1) * 128],
                                      identb[0:60, 0:60])
                return ins
            tm = S.add("pe", _str, [tcb, t_ident] + bfree)
            te = S.add("dve", lambda e: e.tensor_copy(
                uTs[:, :, :, 1:16],
                bkb[:, 0:256].rearrange("p (g j) -> p g j", g=4)[:, :, 0:60].rearrange("p g (s j) -> p g s j", s=4)),
                [tm, t_setup])
            banks.release(bi, te)
            toks.append(te)
            state["spT_tok"] = toks

        for tl in tiles:
            if DBG_STOP and DBG_STOP != "ALLCONV" and tl["idx"] > int(DBG_STOP.split(":")[0]):
                break
            if tl["kind"] == "sample":
                prep_sample_hist()
            run_tile(tl)

        S.add("pool", lambda e: e.memset(fint[:], 0.0), out_toks)
        S.emit(block)
    return nc


_CACHE = {}


def _get_program():
    if "nc" not in _CACHE:
        _CACHE["nc"] = build_program()
    return _CACHE["nc"]


def kernel(x_prompt, x_sample, cache_k, cache_v, state_pool, p_prompt, p_sample,
           ln_mix, w_in, attn_sinks, g_attn_out, w_pool, pool_scale, g_pool_out,
           w_out, ln_ffn, w_up, w_down, ln_ple, w_ple_gate, w_ple_proj, ln_final):
    f32 = np.float32
    x_prompt = np.asarray(x_prompt, f32)
    x_sample = np.asarray(x_sample, f32)
    B, SEQ, _ = x_prompt.shape
    segs_per_seq = SEQ // SEG
    assert B * segs_per_seq == NCORES

    w_in0 = np.asarray(w_in, f32)[0]
    qcols = []
    for i in range(4):
        qcols += list(range(i * 64, (i + 1) * 64)) + list(range((4 + i) * 64, (5 + i) * 64))
    kcols = list(range(512, 640))
    vcols = list(range(640, 768))
    ucols = list(range(768, 1280))
    perm = qcols + kcols + ucols[0:384] + ucols[384:512] + vcols + kcols + ucols[0:384] + ucols[384:512]
    w_in_p = np.ascontiguousarray(w_in0[:, perm])

    def col8(v):
        return np.asarray(v, f32).reshape(-1, 128).T

    w_pool_l = np.ascontiguousarray(np.asarray(w_pool, f32)[0].transpose(1, 0, 2).reshape(128, 512))
    lnf = np.ascontiguousarray(np.broadcast_to(np.asarray(ln_final, f32)[None, :], (128, D)))
    bmask = np.kron(np.eye(4, dtype=f32), np.ones((16, 16), f32))
    ident = np.eye(128, dtype=f32)

    shared = dict(
        w_in_p=w_in_p, w_out=np.ascontiguousarray(np.asarray(w_out, f32)[0]),
        w_up=np.ascontiguousarray(np.asarray(w_up, f32)[0]),
        w_down=np.ascontiguousarray(np.asarray(w_down, f32)[0]),
        w_gate=np.ascontiguousarray(np.asarray(w_ple_gate, f32)[0]),
        w_ple=np.ascontiguousarray(np.asarray(w_ple_proj, f32)[0]),
        w_pool=w_pool_l, lnf=lnf, bmask=bmask, ident=ident)

    in_maps = []
    for c in range(NCORES):
        b, s = divmod(c, segs_per_seq)
        xin = np.zeros((HALO + SEG, D), f32)
        xin[HALO:] = x_prompt[b, s * SEG:(s + 1) * SEG]
        if s > 0:
            xin[:HALO] = x_prompt[b, s * SEG - HALO:s * SEG]
        cols = np.zeros((128, NCOLS), f32)
        cols[:, C_LNMIX:C_LNMIX + 8] = col8(np.asarray(ln_mix)[0])
        cols[:, C_LNFFN:C_LNFFN + 8] = col8(np.asarray(ln_ffn)[0])
        cols[:, C_LNPLE:C_LNPLE + 8] = col8(np.asarray(ln_ple)[0])
        cols[:, C_GATT:C_GATT + 4] = col8(np.asarray(g_attn_out)[0])
        cols[:, C_GPOOL:C_GPOOL + 4] = col8(np.asarray(g_pool_out)[0])
        cols[:, C_PSCALE:C_PSCALE + 4] = col8(np.asarray(pool_scale)[0])
        cols[:, C_HBIAS] = 0.0 if s > 0 else -30000.0
        cols[:, C_SINK:C_SINK + 8] = np.asarray(attn_sinks, f32)[0][None, :]
        cols[:, C_INVW:C_INVW + 4] = np.array([0.5, 0.25, 0.125, 0.0625], f32)[None, :]
        invcnt = np.zeros((128, 64), f32)
        for g, w in enumerate((2, 4, 8, 16)):
            pos = np.arange(16)
            cnt = np.minimum(pos + 1, w) if s == 0 else np.full(16, w)
            invcnt[:, g * 16:(g + 1) * 16] = (w / cnt).astype(f32)[None, :]
        ss = slice(c * 4, (c + 1) * 4)
        m = dict(shared)
        m.update(
            xin=xin, pin=np.ascontiguousarray(np.asarray(p_prompt, f32)[0, b, s * SEG:(s + 1) * SEG]),
            xs=np.ascontiguousarray(x_sample[ss].reshape(NSAMP, D)),
            ps=np.ascontiguousarray(np.asarray(p_sample, f32)[0, ss].reshape(NSAMP, 256)),
            ck=np.ascontiguousarray(np.asarray(cache_k, f32)[0, ss].reshape(4 * 128, 128)),
            cv=np.ascontiguousarray(np.asarray(cache_v, f32)[0, ss].reshape(4 * 128, 128)),
            spool=np.ascontiguousarray(np.asarray(state_pool, f32)[0, ss].reshape(60, 512)),
            cols=cols, invcnt=invcnt)
        in_maps.append(m)

    nc = _get_program()
    res = run_bass_kernel_spmd(nc, in_maps, core_ids=list(range(NCORES)))
    R = res.results

    nsb = x_sample.shape[0]
    y_prompt = np.zeros((B, SEQ, D), f32)
    y_sample = np.zeros((nsb, 16, D), f32)
    nkp = np.zeros((1, B, 128, 2, 64), f32)
    nvp = np.zeros((1, B, 128, 2, 64), f32)
    npp = np.zeros((1, B, 15, 512), f32)
    nks = np.zeros((1, nsb, 16, 2, 64), f32)
    nvs = np.zeros((1, nsb, 16, 2, 64), f32)
    nps = np.zeros((1, nsb, 15, 512), f32)
    for c in range(NCORES):
        b, s = divmod(c, segs_per_seq)
        y_prompt[b, s * SEG:(s + 1) * SEG] = R[c]["y"]
        y_sample[c * 4:(c + 1) * 4] = R[c]["ys"].reshape(4, 16, D)
        if s == segs_per_seq - 1:
            kvu = R[c]["kvu_last"]
            nkp[0, b] = kvu[:, 0:128].reshape(128, 2, 64)
            nvp[0, b] = kvu[:, 128:256].reshape(128, 2, 64)
            npp[0, b] = kvu[113:128, 256:768]
        kvs = R[c]["kvu_s"].reshape(4, 16, 768)
        nks[0, c * 4:(c + 1) * 4] = kvs[:, :, 0:128].reshape(4, 16, 2, 64)
        nvs[0, c * 4:(c + 1) * 4] = kvs[:, :, 128:256].reshape(4, 16, 2, 64)
        nps[0, c * 4:(c + 1) * 4] = kvs[:, 1:16, 256:768]
    return (y_prompt, y_sample, nkp, nvp, npp, nks, nvs, nps)
```

```python
import contextlib
import numpy as np
import concourse.bass as bass
import concourse.mybir as mybir
from concourse.bass_utils import run_bass_kernel_spmd

F32 = mybir.dt.float32
BF16 = mybir.dt.bfloat16
AF = mybir.ActivationFunctionType
ALU = mybir.AluOpType

NCORES = 8
D = 1024
SEG = 4096
T = 512
NT = SEG // T
NBLK = T // 128
HALO = 128
NSAMP = 64
EPS = 1e-6
NSLOT = 5
SLOT_COLS = 4096

C_LNMIX, C_LNFFN, C_LNPLE, C_GATT, C_GPOOL, C_PSCALE, C_HBIAS, C_SINK, C_INVW = 0, 8, 16, 24, 28, 32, 36, 37, 45
NCOLS = 49
import os as _os
DBG_STOP = _os.environ.get('KDBG_STOP', '')


class Sched:
    ENG = ("pe", "act", "dve", "pool", "sp")

    def __init__(self, nc, stack):
        self.nc = nc
        self.stack = stack
        self.streams = {e: [] for e in self.ENG}
        self.csem = {e: stack.enter_context(nc.semaphore("c_" + e)) for e in ("pe", "act", "dve", "pool")}
        self.dsem = {}
        self.dcount = {}

    def add(self, eng, fn, deps=()):
        lst = self.streams[eng]
        lst.append(dict(kind="c", fn=fn, deps=self._flat(deps), sig=False))
        return ("c", eng, len(lst) - 1)

    def dma_sem(self, sem):
        if sem not in self.dsem:
            self.dsem[sem] = self.stack.enter_context(self.nc.semaphore("d_" + sem))
            self.dcount[sem] = 0

    def dma(self, queue, fn, sem, deps=()):
        self.dma_sem(sem)
        self.dcount[sem] += 16
        self.streams[queue].append(dict(kind="d", fn=fn, deps=self._flat(deps), sem=sem))
        return ("d", sem, self.dcount[sem])

    def _flat(self, deps):
        out = []
        for d in deps:
            if d is None:
                continue
            if isinstance(d, list):
                out.extend(self._flat(d))
            else:
                assert isinstance(d, tuple) and d[0] in ("c", "d"), d
                out.append(d)
        return out

    def emit(self, block):
        for e in self.ENG:
            for op in self.streams[e]:
                for d in op["deps"]:
                    if d[0] == "c":
                        if d[1] == "pe" and e == "pe":
                            continue
                        self.streams[d[1]][d[2]]["sig"] = True
        sigval = {}
        for e in ("pe", "act", "dve", "pool"):
            n = 0
            for i, op in enumerate(self.streams[e]):
                if op["kind"] == "c" and op["sig"]:
                    n += 1
                    sigval[(e, i)] = n
        self.maxsig = {e: max([v for (ee, i), v in sigval.items() if ee == e] or [0]) for e in ("pe", "act", "dve", "pool")}
        self.nops = {e: len(self.streams[e]) for e in self.ENG}
        if DBG_STOP:
            print("SCHED maxsig", self.maxsig, "nops", self.nops, "dma", self.dcount)
        engobj = {"pe": block.tensor, "act": block.scalar, "dve": block.vector, "pool": block.gpsimd,
                  "sp": block.sync}

        def run(e):
            def body(eng):
                waited = {}
                for i, op in enumerate(self.streams[e]):
                    need = {}
                    for d in op["deps"]:
                        if d[0] == "c":
                            if d[1] == "pe" and e == "pe":
                                continue
                            key = ("c", d[1])
                            val = sigval[(d[1], d[2])]
                        else:
                            key = ("d", d[1])
                            val = d[2]
                        if val > need.get(key, 0):
                            need[key] = val
                    for key, val in need.items():
                        if waited.get(key, 0) >= val:
                            continue
                        sem = self.csem[key[1]] if key[0] == "c" else self.dsem[key[1]]
                        eng.wait_ge(sem, val)
                        waited[key] = val
                    ins = op["fn"](eng)
                    if op["kind"] == "d":
                        ins.then_inc(self.dsem[op["sem"]], 16)
                    elif op["sig"]:
                        ins.then_inc(self.csem[e], 1)
            engobj[e](body)

        for e in self.ENG:
            run(e)


class Banks:
    def __init__(self, tensors):
        self.t = tensors
        self.free = [[] for _ in tensors]
        self.rr = 0

    def get(self):
        i = self.rr
        self.rr = (i + 1) % len(self.t)
        return i, self.t[i], self.free[i]

    def release(self, i, toks):
        self.free[i] = list(toks) if isinstance(toks, list) else [toks]


def build_program(NT=NT):
    SEG = NT * T
    nc = bass.Bass("TRN2", target_bir_lowering=False)

    def din(name, shape, dt=F32):
        return nc.dram_tensor(name, list(shape), dt, kind="ExternalInput").ap()

    def dout(name, shape, dt=F32):
        return nc.dram_tensor(name, list(shape), dt, kind="ExternalOutput").ap()

    xin = din("xin", [HALO + SEG, D])
    pin = din("pin", [SEG, 256])
    xs_d = din("xs", [NSAMP, D])
    ps_d = din("ps", [NSAMP, 256])
    ck_d = din("ck", [4 * 128, 128])
    cv_d = din("cv", [4 * 128, 128])
    spool_d = din("spool", [60, 512])
    w_in_d = din("w_in_p", [D, 1920])
    w_out_d = din("w_out", [D, D])
    w_up_d = din("w_up", [D, 4096])
    w_down_d = din("w_down", [4096, D])
    w_gate_d = din("w_gate", [D, D])
    w_ple_d = din("w_ple", [256, D])
    w_pool_d = din("w_pool", [128, 512])
    cols_d = din("cols", [128, NCOLS])
    lnf_d = din("lnf", [128, D])
    invcnt_d = din("invcnt", [128, 64])
    bmask_d = din("bmask", [64, 64])
    ident_d = din("ident", [128, 128])

    y_d = dout("y", [SEG, D])
    ys_d = dout("ys", [NSAMP, D])
    kvu_last_d = dout("kvu_last", [128, 768])
    kvu_s_d = dout("kvu_s", [NSAMP, 768])

    wblocks = {}

    def wblk(name, src3, k, c):
        sc = nc.dram_tensor("sc_" + name, [128, k * c], BF16, kind="Internal").ap()
        wblocks[name] = dict(src=src3, sc=sc, k=k, c=c)

    w_in3 = w_in_d.rearrange("(k p) n -> p k n", p=128)
    wblk("inq", w_in3[:, :, 0:512], 8, 512)
    wblk("inku", w_in3[:, :, 512:1024], 8, 512)
    wblk("inuv", w_in3[:, :, 1024:1280], 8, 256)
    wblk("int1", w_in3[:, :, 1280:1792], 8, 512)
    wblk("int2", w_in3[:, :, 1792:1920], 8, 128)
    w_out3 = w_out_d.rearrange("(k p) n -> p k n", p=128)
    for hf in range(2):
        wblk(f"out{hf}", w_out3[:, :, hf * 512:(hf + 1) * 512], 8, 512)
    w_up3 = w_up_d.rearrange("(k p) n -> p k n", p=128)
    w_dn3 = w_down_d.rearrange("(k p) n -> p k n", p=128)
    for fh in range(2):
        for i in range(4):
            j = fh * 4 + i
            wblk(f"up{j}", w_up3[:, :, j * 512:(j + 1) * 512], 8, 512)
        for ch in range(2):
            for kq in range(2):
                k0 = fh * 16 + kq * 8
                wblk(f"dn{fh}{ch}{kq}", w_dn3[:, k0:k0 + 8, ch * 512:(ch + 1) * 512], 8, 512)
    w_g3 = w_gate_d.rearrange("(k p) n -> p k n", p=128)
    w_p3 = w_ple_d.rearrange("(k p) n -> p k n", p=128)
    wblk("ple", w_p3[:, :, :], 2, 1024)
    for hf in range(2):
        wblk(f"gate{hf}", w_g3[:, :, hf * 512:(hf + 1) * 512], 8, 512)

    def tile_wseq(kind):
        seq = ["inq", "inku", "inuv"]
        if kind in ("last", "sample"):
            seq += ["int1", "int2"]
        seq += ["out0", "out1"]
        for fh in range(2):
            seq += [f"up{fh * 4 + i}" for i in range(4)]
            seq += [f"dn{fh}{ch}{kq}" for ch in range(2) for kq in range(2)]
        seq += ["ple", "gate0", "gate1"]
        return seq

    tiles = []
    for t in range(NT):
        kind = "first" if t == 0 else ("last" if t == NT - 1 else "mid")
        tiles.append(dict(kind=kind, idx=t))
    tiles.append(dict(kind="sample", idx=NT))
    wseq = []
    for tl in tiles:
        wseq += tile_wseq(tl["kind"])

    with contextlib.ExitStack() as stack:
        def sb(name, shape, dt):
            return stack.enter_context(nc.sbuf_tensor("s_" + name, list(shape), dt))

        S = Sched(nc, stack)

        xbuf = [sb(f"xbuf{i}", [128, NBLK, D], F32) for i in range(2)]
        pbuf = [sb(f"pbuf{i}", [128, NBLK, 256], BF16) for i in range(2)]
        junk = sb("junk", [128, D], BF16)
        sqj = sb("sqj", [128, 512], BF16)
        xsb = [sb(f"xsb{i}", [128, D], BF16) for i in range(2)]
        hTA = sb("hTA", [128, 8, HALO + T], BF16)
        hT = sb("hT", [128, 8, T], BF16)
        QT = sb("QT", [128, 4, T], BF16)
        KT2 = [sb(f"KT{i}", [128, HALO + T], BF16) for i in range(2)]
        KTs = sb("KTs", [128, NSAMP], BF16)
        V12 = [sb(f"V1_{i}", [128, NBLK + 1, 2, 65], BF16) for i in range(2)]
        V1s = sb("V1s", [128, 2, 65], BF16)
        uT = sb("uT", [128, 4, 16 + T], F32)
        ptmp = [sb(f"ptmp{i}", [128, 16 + T], F32) for i in range(2)]
        dT = sb("dT", [128, 4, T], BF16)
        uTb = sb("uTb", [128, 4, T], BF16)
        wpool_n = sb("wpool_n", [128, 512], BF16)
        psgw = sb("psgw", [128, 4], F32)
        pscw = sb("pscw", [128, 4], F32)
        PT = [sb(f"PT{i}", [128, 2048], BF16) for i in range(2)]
        anb = [sb(f"anb{i}", [128, 512], BF16) for i in range(2)]
        mixT = sb("mixT", [128, 8, T], BF16)
        sqb = sb("sqb", [128, 4, T], BF16)
        hidT = sb("hidT", [128, 16, T], BF16)
        tg = [sb(f"tg{i}", [128, 512], F32) for i in range(2)]
        tp = [sb(f"tp{i}", [128, 512], F32) for i in range(1)]
        pT = sb("pT", [128, 2, T], BF16)
        wring = [sb(f"wring{i}", [128, SLOT_COLS], BF16) for i in range(NSLOT)]
        stage = sb("stage", [128, 768], F32)
        lnf = sb("lnf", [128, D], F32)
        cols = sb("cols", [128, NCOLS], F32)
        psg = sb("psg", [128, 4], F32)
        esink = sb("esink", [128, 8], F32)
        invcnt = sb("invcnt", [128, 64], F32)
        bmask = sb("bmask", [64, 64], F32)
        identf = sb("identf", [128, 128], F32)
        identb = sb("identb", [128, 128], BF16)
        wpool = sb("wpool", [128, 512], BF16)
        ones = sb("ones", [128, 1], BF16)
        epsb = sb("epsb", [128, 1], F32)
        oneb = sb("oneb", [128, 1], F32)
        fint = sb("fint", [128, 1], F32)
        NSS = 384
        ssb = sb("ssb", [128, NSS], F32)
        vvb = sb("vvb", [128, NSS], F32)
        rsb = sb("rsb", [128, NSS], F32)
        den = [sb(f"den{i}", [128, 8], F32) for i in range(2)]
        rcp = [sb(f"rcp{i}", [128, 8], F32) for i in range(2)]
        ckf = sb("ckf", [128, 4, 128], F32)
        cvf = sb("cvf", [128, 4, 128], F32)
        spf = sb("spf", [64, 512], F32)
        ckb = sb("ckb", [128, 4, 128], BF16)
        spb = sb("spb", [64, 512], BF16)
        KcT = sb("KcT", [128, 4, 128], BF16)
        Vc1 = sb("Vc1", [128, 4, 2, 65], BF16)
        uTs = sb("uTs", [128, 4, 4, 32], F32)
        ptmps = [sb(f"ptmps{i}", [128, 4, 32], F32) for i in range(2)]
        etmp = sb("etmp", [64, 512], F32)

        pbank = [stack.enter_context(nc.psum_tensor(f"pb{i}", [128, 512], F32)) for i in range(8)]
        banks = Banks(pbank)
        block = stack.enter_context(nc.Block())

        ss_ctr = [0]

        def ss_col():
            c = ss_ctr[0]
            ss_ctr[0] += 1
            assert c < NSS
            return c

        conv_tok = {}
        conv_order = list(dict.fromkeys(wseq))
        for name in conv_order:
            S.dma_sem("cv_" + name)
            conv_tok[name] = ("d", "cv_" + name, 16)
        conv_issued = [0]

        def issue_convs(n):
            while conv_issued[0] < min(n, len(conv_order)):
                name = conv_order[conv_issued[0]]
                wb = wblocks[name]
                dst = wb["sc"].rearrange("p (k c) -> p k c", k=wb["k"])
                tok = S.dma("pool", (lambda eng, dst=dst, src=wb["src"]: eng.dma_start(out=dst, in_=src)),
                            "cv_" + name)
                assert tok == conv_tok[name]
                conv_issued[0] += 1

        def _setup_pool(e):
            e.memset(ones[:], 1.0)
            e.memset(epsb[:], EPS)
            e.memset(oneb[:], 1.0)
            e.memset(ssb[:], 0.0)
            e.memset(V12[0][:], 1.0)
            e.memset(V12[1][:], 1.0)
            e.memset(V1s[:], 1.0)
            e.memset(Vc1[:], 1.0)
            e.memset(uTs[:], 0.0)
            e.memset(PT[0][:, :], 0.0)
            e.memset(PT[1][:, :], 0.0)
            return e.memset(uT[:], 0.0)
        issue_convs(1)
        t_setup = S.add("pool", _setup_pool)

        c_tok = []
        for dst, src in ((cols[:], cols_d[:, :]), (lnf[:], lnf_d[:, :]), (invcnt[:], invcnt_d[:, :]),
                         (bmask[:], bmask_d[:, :]), (identf[:], ident_d[:, :])):
            c_tok.append(S.dma("sp", (lambda eng, dst=dst, src=src: eng.dma_start(out=dst, in_=src)), "const"))
        const_tok = c_tok[-1]

        t_ident = S.add("pool", lambda e: e.tensor_copy(identb[:], identf[:]), [const_tok])
        issue_convs(3)
        wp_tok = S.dma("pool", lambda eng: eng.dma_start(out=wpool[:], in_=w_pool_d[:, :]), "wpool")
        issue_convs(5)
        if DBG_STOP:
            issue_convs(len(conv_order))

        t_psg = S.add("dve", lambda e: e.tensor_tensor(out=psg[:], in0=cols[:, C_PSCALE:C_PSCALE + 4],
                                                       in1=cols[:, C_GPOOL:C_GPOOL + 4], op=ALU.mult), [const_tok])
        t_esink = S.add("act", lambda e: e.activation(out=esink[:], in_=cols[:, C_SINK:C_SINK + 8], func=AF.Exp),
                        [const_tok])
        t_psgw = S.add("dve", lambda e: e.tensor_tensor(out=psgw[:], in0=psg[:], in1=cols[:, C_INVW:C_INVW + 4],
                                                        op=ALU.mult), [t_psg, const_tok])
        t_pscw = S.add("dve", lambda e: e.tensor_tensor(out=pscw[:], in0=cols[:, C_PSCALE:C_PSCALE + 4],
                                                        in1=cols[:, C_INVW:C_INVW + 4], op=ALU.mult), [const_tok])
        t_wpn = None
        for g_ in range(4):
            t_wpn = S.add("dve", lambda e, g_=g_: e.tensor_scalar_mul(
                out=wpool_n[:, g_ * 128:(g_ + 1) * 128], in0=wpool[:, g_ * 128:(g_ + 1) * 128],
                scalar1=-float(2 << g_)), [wp_tok] + ([t_wpn] if t_wpn else []))

        wstate = dict(next_load=0, next_use=0, done=set(), slot_free={}, loaded={})

        def w_issue_loads():
            while wstate["next_load"] < len(wseq):
                i = wstate["next_load"]
                if i >= NSLOT and (i - NSLOT) not in wstate["done"]:
                    break
                name = wseq[i]
                wb = wblocks[name]
                slot = i % NSLOT
                n = wb["k"] * wb["c"]
                deps = [conv_tok[name]] + wstate["slot_free"].get(i - NSLOT, [])
                tok = S.dma("sp", (lambda eng, slot=slot, n=n, sc=wb["sc"]:
                                   eng.dma_start(out=wring[slot][:, 0:n], in_=sc[:, :])), f"w{slot}", deps)
                wstate["loaded"][i] = tok
                wstate["next_load"] += 1

        def w_next(expect):
            i = wstate["next_use"]
            assert wseq[i] == expect, (wseq[i], expect)
            w_issue_loads()
            assert i in wstate["loaded"], ("weight ring over-subscribed", i, expect)
            wstate["next_use"] += 1
            wb = wblocks[expect]
            slot = i % NSLOT
            view = wring[slot][:, 0:wb["k"] * wb["c"]].rearrange("p (k c) -> p k c", k=wb["k"])
            return view, wstate["loaded"][i], i

        def w_done(i, tok):
            wstate["slot_free"][i] = [tok]
            wstate["done"].add(i)
            w_issue_loads()

        def rstd_chain(ss_ap_fn, ss_tok, col, ncol, n, width, escale=-0.5):
            t1 = S.add("act", lambda e: e.activation(out=vvb[0:n, col:col + ncol], in_=ss_ap_fn(),
                                                     func=AF.Ln, bias=epsb[0:n, :], scale=1.0 / width),
                       [ss_tok, t_setup])
            t2 = S.add("act", lambda e: e.activation(out=rsb[0:n, col:col + ncol], in_=vvb[0:n, col:col + ncol],
                                                     func=AF.Exp, scale=escale), [t1])
            return t2

        xsb_free = [[], []]
        xsb_rr = [0]

        def norm_p1(xap, n, x_deps, defer=None):
            col = ss_col()
            i = xsb_rr[0]
            xsb_rr[0] ^= 1
            if defer is not None:
                t_xs = S.add("act", lambda e: e.activation(out=xsb[i][0:n, :], in_=xap, func=AF.Copy),
                             x_deps + xsb_free[i])
                t_ss = S.add("act", lambda e: e.activation(out=junk[0:n, :], in_=xap, func=AF.Square,
                                                           accum_out=ssb[0:n, col:col + 1]), x_deps + [t_setup])
                t_r = rstd_chain(lambda: ssb[0:n, col:col + 1], t_ss, col, 1, n, D, escale=defer)
                return dict(i=i, t_xs=t_ss, t_cast=t_xs, n=n, col=col, t_r=t_r)
            t_ss = S.add("act", lambda e: e.activation(out=junk[0:n, :], in_=xap, func=AF.Square,
                                                       accum_out=ssb[0:n, col:col + 1]), x_deps + [t_setup])
            t_r = rstd_chain(lambda: ssb[0:n, col:col + 1], t_ss, col, 1, n, D)
            t_xs = S.add("act", lambda e: e.activation(out=xsb[i][0:n, :], in_=xap, func=AF.Copy,
                                                       scale=rsb[0:n, col:col + 1]), [t_r] + xsb_free[i])
            return dict(i=i, t_xs=t_xs, t_cast=t_xs, n=n, col=col, t_r=t_r)

        def norm_p2(ctx, gcol, dst, hcol0, hT_war):
            i, t_xs, n = ctx["i"], ctx["t_cast"], ctx["n"]
            bi, bk, bfree = banks.get()
            bkb = bk.bitcast(BF16)

            def _tr(e):
                ins = None
                for c in range(8):
                    ins = e.transpose(bkb[:, c * 128:c * 128 + n], xsb[i][0:n, c * 128:(c + 1) * 128],
                                      identb[0:n, 0:n])
                return ins
            t_tr = S.add("pe", _tr, [t_xs, t_ident] + bfree)
            xsb_free[i] = [t_tr]
            src = bkb[:, :].rearrange("p (c t) -> p c t", c=8)[:, :, 0:n]
            gap = cols[:, gcol:gcol + 8].unsqueeze(2).to_broadcast([128, 8, n])
            t_h = S.add("dve", lambda e: e.tensor_tensor(out=dst[:, :, hcol0:hcol0 + n], in0=src, in1=gap,
                                                         op=ALU.mult), [t_tr, const_tok] + hT_war)
            banks.release(bi, t_h)
            return t_h

        def norm_to_hT(xap, n, gcol, hcol0, x_deps, hT_war, dst=None, defer=None, info=None):
            ctx = norm_p1(xap, n, x_deps, defer=defer)
            t_h = norm_p2(ctx, gcol, hT if dst is None else dst, hcol0, hT_war)
            if info is not None:
                info.update(col=ctx["col"], t_r=ctx["t_r"])
            return t_h, ctx["t_xs"]

        hT_readers = []
        hTA_readers = []
        x_ld = {}
        p_ld = {}
        y_st = {}
        out_toks = []
        pt_free = [[], []]
        ab_free = [[], []]
        an_free = [[], []]
        tg_free = [[], []]
        tp_free = [[]]
        state = dict(mix_readers=[], hid_readers=[], QT_readers=[], uT_readers=[], dT_readers=[], sq_readers=[],
                     pT_readers=[], stage_free=[], ptmp_free=[], uT_halo=[], KTs_readers=[], halo_read=[], pre={},
                     kt_readers=[[], []], v_readers=[[], []], rstda={}, uTb_readers=[])

        def issue_x_load(t):
            bi = t % 2
            deps = []
            if t - 2 in y_st:
                deps.append(y_st[t - 2])
            if t == 1:
                deps += state["halo_read"]
            pdeps = list(state["pT_readers"]) if t >= 2 else []
            if t < NT:
                src = xin[HALO + t * T: HALO + (t + 1) * T, :].rearrange("(b p) d -> p b d", p=128)
                x_ld[t] = S.dma("sp", lambda eng: eng.dma_start(out=xbuf[bi][:, :, :], in_=src), f"x{bi}", deps)
                psrc = pin[t * T:(t + 1) * T, :].rearrange("(b p) d -> p b d", p=128)
                p_ld[t] = S.dma("pool", lambda eng: eng.dma_start(out=pbuf[bi][:, :, :], in_=psrc), f"p{bi}", pdeps)
            else:
                x_ld[t] = S.dma("sp", lambda eng: eng.dma_start(out=xbuf[bi][0:NSAMP, 0, :], in_=xs_d[:, :]),
                                f"x{bi}", deps)
                p_ld[t] = S.dma("pool", lambda eng: eng.dma_start(out=pbuf[bi][0:NSAMP, 0, :], in_=ps_d[:, :]),
                                f"p{bi}", pdeps)

        xhalo = xbuf[1][:, 0, :]
        halo_tok = S.dma("sp", lambda eng: eng.dma_start(out=xhalo, in_=xin[0:HALO, :]), "x1")
        issue_x_load(0)

        S.dma("sp", lambda eng: eng.dma_start(out=ckf[:, :, :], in_=ck_d.rearrange("(s p) f -> p s f", p=128)), "ckv")
        S.dma("sp", lambda eng: eng.dma_start(out=cvf[:, :, :], in_=cv_d.rearrange("(s p) f -> p s f", p=128)), "ckv")
        ckv_tok = S.dma("sp", lambda eng: eng.dma_start(out=spf[0:60, :], in_=spool_d[:, :]), "ckv")

        def attn_post_a(ob, tpv, n, seq):
            i = seq % 2
            o3 = [ob[0][1][0:n, 0:260].rearrange("p (h d) -> p h d", h=4),
                  ob[1][1][0:n, 0:260].rearrange("p (h d) -> p h d", h=4)]
            tds = []
            for hb in range(2):
                td = S.add("dve", lambda e, hb=hb: e.tensor_tensor(
                    out=den[i][0:n, hb * 4:(hb + 1) * 4].unsqueeze(2), in0=o3[hb][:, :, 64:65],
                    in1=esink[0:n, hb * 4:(hb + 1) * 4].unsqueeze(2), op=ALU.add), [tpv, t_esink] + ab_free[i])
                tds.append(td)
            trc = S.add("dve", lambda e: e.reciprocal(out=rcp[i][0:n, :], in_=den[i][0:n, :]), tds)
            tas = []
            for hb in range(2):
                ta = S.add("dve", lambda e, hb=hb: e.tensor_tensor(
                    out=anb[i][0:n, hb * 256:(hb + 1) * 256].rearrange("p (h d) -> p h d", h=4),
                    in0=o3[hb][:, :, 0:64],
                    in1=rcp[i][0:n, hb * 4:(hb + 1) * 4].unsqueeze(2).to_broadcast([n, 4, 64]), op=ALU.mult),
                    [trc] + an_free[i])
                tas.append(ta)
                banks.release(ob[hb][0], [ta, tds[hb]])
            col = ss_col()
            t_ss = S.add("dve", lambda e: e.scalar_tensor_tensor(
                out=sqj[0:n, :], in0=anb[i][0:n, :], scalar=1.0, in1=anb[i][0:n, :], op0=ALU.mult, op1=ALU.mult,
                accum_out=ssb[0:n, col:col + 1]), tas + [t_setup])
            t_r = rstd_chain(lambda: ssb[0:n, col:col + 1], t_ss, col, 1, n, 512)
            ab_free[i] = tas
            return dict(tas=tas + [t_ss], col=col, t_r=t_r)

        def attn_post_b(actx, n, b, seq, mix_war):
            i = seq % 2
            t_an = actx["tas"]
            state["rstda"][b] = (actx["col"], actx["t_r"])
            bi, bk, bfree = banks.get()
            bkb = bk.bitcast(BF16)

            def _tr(e):
                ins = None
                for c in range(4):
                    ins = e.transpose(bkb[:, c * 128:c * 128 + n], anb[i][0:n, c * 128:(c + 1) * 128],
                                      identb[0:n, 0:n])
                return ins
            t_tr = S.add("pe", _tr, t_an + [t_ident] + bfree)
            an_free[i] = [t_tr]
            src = bkb[:, 0:512].rearrange("p (c t) -> p c t", c=4)[:, :, 0:n]
            gap = cols[:, C_GATT:C_GATT + 4].unsqueeze(2).to_broadcast([128, 4, n])
            t_m = S.add("dve", lambda e: e.tensor_tensor(out=mixT[:, 0:4, b * 128:b * 128 + n], in0=src, in1=gap,
                                                         op=ALU.mult), [t_tr, const_tok] + mix_war)
            banks.release(bi, t_m)
            return t_m

        def attn_post(ob, tpv, n, b, seq, mix_war):
            return attn_post_b(attn_post_a(ob, tpv, n, seq), n, b, seq, mix_war)

        def prompt_attn_scores(t, b, q_toks, k_tok):
            gb = 1 + t * NBLK + b
            seq = t * NBLK + b
            pi = seq % 2
            ptv = PT[pi][:, :].rearrange("p (kb kv h q) -> p kb kv h q", kb=2, kv=2, h=4)
            exp_toks = []

            def score(kv, kbi, kb):
                bi, bk, bfree = banks.get()
                kdeps = state["k_halo_tok"] if kb == 0 else []
                tm = S.add("pe", lambda e: e.matmul(
                    bk[:, 0:512].rearrange("p (h q) -> p h q", h=4),
                    lhsT=KT2[t % 2][kv * 64:(kv + 1) * 64, (b + kbi) * 128:(b + kbi + 1) * 128],
                    rhs=QT[kv * 64:(kv + 1) * 64, :, b * 128:(b + 1) * 128], start=True, stop=True),
                    [k_tok] + q_toks + kdeps + bfree)
                state["QT_readers"].append(tm)
                state["kt_readers"][t % 2] = [tm]
                bkv = bk[:, 0:512].rearrange("p (h q) -> p h q", h=4)
                deps = [tm, const_tok, t_setup] + pt_free[pi]
                if kbi == 0:
                    if kb == 0:
                        ta = S.add("act", lambda e: e.activation(
                            out=ptv[:, 0, kv, :, 0:64], in_=bkv[:, :, 0:64], func=AF.Exp,
                            bias=cols[:, C_HBIAS:C_HBIAS + 1], scale=0.125), deps)
                        tb = S.add("act", lambda e: e.activation(
                            out=ptv[64:128, 0, kv, :, 64:128], in_=bkv[64:128, :, 64:128], func=AF.Exp,
                            bias=cols[64:128, C_HBIAS:C_HBIAS + 1], scale=0.125), deps)
                    else:
                        ta = S.add("act", lambda e: e.activation(
                            out=ptv[:, 0, kv, :, 0:64], in_=bkv[:, :, 0:64], func=AF.Exp, scale=0.125), deps)
                        tb = S.add("act", lambda e: e.activation(
                            out=ptv[64:128, 0, kv, :, 64:128], in_=bkv[64:128, :, 64:128], func=AF.Exp,
                            scale=0.125), deps)
                else:
                    ta = S.add("act", lambda e: e.activation(
                        out=ptv[:, 1, kv, :, 64:128], in_=bkv[:, :, 64:128], func=AF.Exp, scale=0.125), deps)
                    tb = S.add("act", lambda e: e.activation(
                        out=ptv[0:64, 1, kv, :, 0:64], in_=bkv[0:64, :, 0:64], func=AF.Exp, scale=0.125), deps)
                banks.release(bi, [ta, tb])
                exp_toks.extend([ta, tb])

            for kv in range(2):
                for kbi, kb in enumerate((gb - 1, gb)):
                    score(kv, kbi, kb)
            return dict(gb=gb, seq=seq, pi=pi, ptv=ptv, exp_toks=exp_toks, b=b, t=t)

        def prompt_attn_pv(ctx, v_toks):
            gb, seq, pi, ptv, exp_toks, b, t = (ctx[k] for k in ("gb", "seq", "pi", "ptv", "exp_toks", "b", "t"))
            ob = [banks.get(), banks.get()]
            vdeps = [v_toks[b]]
            if b == 0:
                vdeps.append(state["v_prev_tok"] if t > 0 else v_toks["halo"])
            else:
                vdeps.append(v_toks[b - 1])

            def _pv(e):
                ins = None
                for h in range(8):
                    kv, g = divmod(h, 4)
                    o = ob[h // 4][1]
                    for kbi, kb in enumerate((gb - 1, gb)):
                        ins = e.matmul(o[:, (h % 4) * 65:(h % 4) * 65 + 65], lhsT=ptv[:, kbi, kv, g, :],
                                       rhs=V12[t % 2][:, b + kbi, kv, :], start=(kbi == 0), stop=(kbi == 1))
                return ins
            tpv = S.add("pe", _pv, exp_toks + vdeps + ob[0][2] + ob[1][2])
            pt_free[pi] = [tpv]
            state["v_readers"][t % 2] = [tpv]
            return attn_post_a(ob, tpv, 128, seq)

        def sample_attention(q_toks, k_tok, v_tok, mix_war):
            n = NSAMP
            kc_toks = []

            tcb = S.add("pool", lambda e: e.tensor_copy(ckb[:, :, :], ckf[:, :, :]), [ckv_tok])
            bi, bk, bfree = banks.get()
            bkb = bk.bitcast(BF16)

            def _ktr(e):
                ins = None
                for s in range(4):
                    ins = e.transpose(bkb[:, s * 128:(s + 1) * 128], ckb[:, s, :], identb[:, :])
                return ins
            tm = S.add("pe", _ktr, [tcb, t_ident] + bfree)
            te = S.add("dve", lambda e: e.tensor_copy(KcT[:, :, :], bkb[:, 0:512].rearrange("p (s k) -> p s k", s=4)),
                       [tm])
            banks.release(bi, te)
            kc_toks.append(te)
            tvc = S.add("dve", lambda e: e.tensor_copy(Vc1[:, :, :, 0:64],
                                                       cvf[:, :, :].rearrange("p s (k d) -> p s k d", k=2)),
                        [ckv_tok, t_setup])
            ptc = PT[0][:, :].rearrange("p (s h q) -> p s h q", s=4, h=8)
            if DBG_STOP == "2:SA1":
                return None
            ptn = PT[1][0:64, 0:512].rearrange("p (h q) -> p h q", h=8)
            tz = S.add("pool", lambda e: e.memset(PT[0][:, :], 0.0), pt_free[0])
            if DBG_STOP == "2:SA1b":
                return None
            cs = [[banks.get() for half in range(2)] for kv in range(2)]

            def _sc(e):
                ins = None
                for s in range(4):
                    for kv in range(2):
                        o = cs[kv][s // 2][1][:, (s % 2) * 256:(s % 2 + 1) * 256].rearrange("p (h q) -> p h q", h=4)
                        ins = e.matmul(o, lhsT=KcT[kv * 64:(kv + 1) * 64, s, :],
                                       rhs=QT[kv * 64:(kv + 1) * 64, :, 0:64], start=True, stop=True)
                return ins
            tsc = S.add("pe", _sc, kc_toks + q_toks + [x for r in cs for bkx in r for x in bkx[2]])
            state["QT_readers"].append(tsc)
            if DBG_STOP == "2:SA2a":
                S.add("dve", lambda e: e.tensor_copy(etmp[:, 0:64], cs[0][0][1][0:64, 0:64]), [tsc])
                return None
            e_toks = []
            rel = {(kv, half): [] for kv in range(2) for half in range(2)}
            for s in range(4):
                for kv in range(2):
                    src = cs[kv][s // 2][1][:, (s % 2) * 256:(s % 2 + 1) * 256].rearrange(
                        "p (h q) -> p h q", h=4)[:, :, s * 16:(s + 1) * 16]
                    te = S.add("act", lambda e, s=s, kv=kv, src=src: e.activation(
                        out=ptc[:, s, kv * 4:(kv + 1) * 4, s * 16:(s + 1) * 16], in_=src,
                        func=AF.Exp, scale=0.125), [tsc, tz])
                    e_toks.append(te)
                    rel[(kv, s // 2)].append(te)
            for kv in range(2):
                for half in range(2):
                    banks.release(cs[kv][half][0], rel[(kv, half)])
            if DBG_STOP == "2:SA2":
                return None
            ns = [banks.get() for kv in range(2)]

            def _sn(e):
                ins = None
                for kv in range(2):
                    ins = e.matmul(ns[kv][1][0:64, 0:256].rearrange("p (h q) -> p h q", h=4),
                                   lhsT=KTs[kv * 64:(kv + 1) * 64, 0:64],
                                   rhs=QT[kv * 64:(kv + 1) * 64, :, 0:64], start=True, stop=True)
                return ins
            tsn = S.add("pe", _sn, [k_tok] + q_toks + ns[0][2] + ns[1][2])
            state["QT_readers"].append(tsn)
            state["KTs_readers"] = [tsn]
            tens = []
            for kv in range(2):
                ten = S.add("act", lambda e, kv=kv: e.activation(out=etmp[:, kv * 256:(kv + 1) * 256],
                                                                 in_=ns[kv][1][0:64, 0:256], func=AF.Exp, scale=0.125),
                            [tsn])
                banks.release(ns[kv][0], ten)
                tens.append(ten)
            tmk = S.add("dve", lambda e: e.tensor_tensor(
                out=ptn, in0=etmp[:, :].rearrange("p (h q) -> p h q", h=8),
                in1=bmask[:, :].unsqueeze(1).to_broadcast([64, 8, 64]), op=ALU.mult), tens + [const_tok] + pt_free[1])
            if DBG_STOP == "2:SA3":
                return None
            ob = [banks.get(), banks.get()]

            def _pv(e):
                ins = None
                for h in range(8):
                    kv = h // 4
                    o = ob[h // 4][1]
                    oc = o[0:64, (h % 4) * 65:(h % 4) * 65 + 65]
                    for s in range(4):
                        ins = e.matmul(oc, lhsT=ptc[:, s, h, :], rhs=Vc1[:, s, kv, :], start=(s == 0), stop=False)
                    ins = e.matmul(oc, lhsT=ptn[:, h, :], rhs=V1s[0:64, kv, :], start=False, stop=True)
                return ins
            tpv = S.add("pe", _pv, e_toks + [tmk, tvc, v_tok] + ob[0][2] + ob[1][2])
            pt_free[0] = [tpv]
            pt_free[1] = [tpv]
            if DBG_STOP == "2:SA4":
                return None
            return attn_post(ob, tpv, n, 0, 0, mix_war)

        def pooling(kind, TT, blocks, u_toks, mix_war, pre_group=None):
            sample = kind == "sample"
            if sample:
                def uview(g):
                    return uTs[:, g, :, :]
                tmpv = [ptmps[0][:, :, :], ptmps[1][:, :, :]]
                L = 32
            else:
                def uview(g):
                    return uT[:, g, :].unsqueeze(1)
                tmpv = [ptmp[0][:, :].unsqueeze(1), ptmp[1][:, :].unsqueeze(1)]
                L = 16 + T
            nseg = 4 if sample else 1
            d_war = state["dT_readers"]
            state["dT_readers"] = []
            d_toks = []
            prev = list(state["ptmp_free"])

            def group(g, prev):
                w = 2 << g
                src = uview(g)
                cur_tok = list(u_toks[g]) + (state["spT_tok"] if sample else state["uT_halo"])
                lo = 0
                shift = 1
                step = 0
                if sample:
                    dstd = dT[:, g, 0:TT].rearrange("p (s j) -> p s j", s=4)
                else:
                    dstd = dT[:, g, 0:TT].unsqueeze(1)
                while shift < w:
                    last = (shift * 2 >= w)
                    if last and kind != "first":
                        tk = S.add("pool", (lambda e, src=src, shift=shift: e.tensor_tensor(
                            out=dstd, in0=src[:, :, 16:L], in1=src[:, :, 16 - shift:L - shift], op=ALU.add)),
                            cur_tok + prev + d_war)
                        return [tk]
                    dst = tmpv[step % 2]
                    nlo = lo + shift
                    tk = S.add("pool", (lambda e, dst=dst, src=src, lo=lo, nlo=nlo, shift=shift: e.tensor_tensor(
                        out=dst[:, :, nlo:L], in0=src[:, :, nlo:L], in1=src[:, :, lo:L - shift], op=ALU.add)),
                        cur_tok + prev)
                    prev = []
                    cur_tok = [tk]
                    src = dst
                    lo = nlo
                    shift *= 2
                    step += 1
                s_new = src[:, :, 16:L]
                ic = invcnt[:, g * 16:(g + 1) * 16].unsqueeze(1)
                t1 = S.add("pool", lambda e: e.tensor_tensor(
                    out=s_new[:, :, 0:16], in0=s_new[:, :, 0:16], in1=ic, op=ALU.mult), cur_tok + [const_tok])
                t2 = S.add("pool", lambda e: e.tensor_copy(dstd, s_new), [t1] + d_war)
                return [t2]

            d_by_g = {}
            for g in (3, 2, 1, 0):
                if pre_group is not None:
                    pre_group(g)
                prev = group(g, prev)
                d_by_g[g] = prev
            d_toks = [d_by_g[g] for g in range(4)]
            state["ptmp_free"] = prev
            alld = [x for d in d_toks for x in d]
            if not sample:
                tc = S.add("pool", lambda e: e.tensor_copy(uT[:, :, 0:16], uT[:, :, T:T + 16]), alld)
                state["uT_readers"] = [tc]
                state["uT_halo"] = [tc]
            else:
                state["uT_readers"] = alld
            return d_toks

        def pooling2(kind, TT, blocks, d_toks, mix_war):
            sq_war = state["sq_readers"]
            state["sq_readers"] = []
            state["uTb_readers"] = []
            pool_toks = []
            sq_toks = []

            def pmm(g):
                bi, bk, bfree = banks.get()
                def _pm(e):
                    e.matmul(bk[:, 0:TT], lhsT=wpool[:, g * 128:(g + 1) * 128], rhs=dT[:, g, 0:TT],
                             start=True, stop=False)
                    return e.matmul(bk[:, 0:TT], lhsT=wpool_n[:, g * 128:(g + 1) * 128], rhs=uTb[:, g, 0:TT],
                                    start=False, stop=True)
                tm = S.add("pe", _pm, d_toks[g] + [wp_tok, t_wpn, state["ub_toks"][g]] + bfree)
                state["dT_readers"].append(tm)
                state["uTb_readers"].append(tm)
                t1 = S.add("act", lambda e: e.activation(out=mixT[:, 4 + g, 0:TT], in_=bk[:, 0:TT],
                                                         func=AF.Copy, scale=psgw[:, g:g + 1]),
                           [tm, t_psgw] + mix_war)
                t2 = S.add("act", lambda e: e.activation(out=sqb[:, g, 0:TT], in_=bk[:, 0:TT], func=AF.Square,
                                                         scale=pscw[:, g:g + 1]),
                           [tm, t_pscw] + sq_war)
                banks.release(bi, [t1, t2])
                pool_toks.append(t1)
                sq_toks.append(t2)
            for g in range(4):
                pmm(g)
            nb = len(blocks)
            col0 = ss_ctr[0]
            cols_b = {}
            for (b, n) in blocks:
                cols_b[b] = ss_col()
            return pool_toks, dict(sq_toks=sq_toks, col0=col0, cols_b=cols_b, nb=nb, blocks=blocks)

        def pooling3(ctx3):
            sq_toks, col0, cols_b, nb, blocks = (ctx3[k] for k in ("sq_toks", "col0", "cols_b", "nb", "blocks"))
            bi, bk, bfree = banks.get()

            def _ss(e):
                ins = None
                for (b, n) in blocks:
                    for g in range(4):
                        ins = e.matmul(bk[0:n, b:b + 1], lhsT=sqb[:, g, b * 128:b * 128 + n], rhs=ones[:, 0:1],
                                       start=(g == 0), stop=(g == 3))
                return ins
            tss = S.add("pe", _ss, sq_toks + [t_setup] + bfree)
            state["sq_readers"].append(tss)
            nmax = blocks[0][1]
            tr = rstd_chain(lambda: bk[0:nmax, 0:nb], tss, col0, nb, nmax, 512)
            banks.release(bi, tr)
            return {b: (cols_b[b], tr) for (b, n) in blocks}

        def run_tile(tl):
            kind = tl["kind"]
            t = tl["idx"]
            sample = kind == "sample"
            xb = xbuf[t % 2]
            pb = pbuf[t % 2]
            if sample:
                blocks = [(0, NSAMP)]
                TT = NSAMP
            else:
                blocks = [(b, 128) for b in range(NBLK)]
                TT = T
            xtok = x_ld[t]
            H0 = HALO

            war = list(hTA_readers)
            hTA_readers.clear()
            h_toks = []
            x_toks = {b: [xtok] for (b, n) in blocks}
            if kind == "first":
                th, txs = norm_to_hT(xhalo, 128, C_LNMIX, 0, [halo_tok], war, dst=hTA)
                h_toks.append(th)
                state["halo_read"] = [txs]
            if t + 1 <= NT:
                issue_x_load(t + 1)
            if t in state["pre"]:
                pre = state["pre"].pop(t)
                h_toks.extend(pre["h_toks"])
                for (b, n) in blocks:
                    x_toks[b].append(pre["txs"][b])
            else:
                for (b, n) in blocks:
                    th, txs = norm_to_hT(xb[0:n, b, :], n, C_LNMIX, H0 + b * 128, [xtok], war, dst=hTA)
                    h_toks.append(th)
                    x_toks[b].append(txs)

            if DBG_STOP == f"{t}:A":
                return
            wvq, wtokq, wiq = w_next("inq")
            qt_war = state["QT_readers"]
            state["QT_readers"] = []
            q_toks = []
            last_pe = [None]

            def q_chunk(j):
                bi, bk, bfree = banks.get()

                def _mm(e):
                    ins = None
                    for k in range(8):
                        ins = e.matmul(bk[:, 0:TT], lhsT=wvq[:, k, j * 128:(j + 1) * 128],
                                       rhs=hTA[:, k, H0:H0 + TT], start=(k == 0), stop=(k == 7))
                    return ins
                tm = S.add("pe", _mm, [wtokq] + h_toks + bfree)
                te = S.add("act", lambda e: e.activation(out=QT[:, j, 0:TT], in_=bk[:, 0:TT], func=AF.Copy),
                           [tm] + qt_war)
                banks.release(bi, te)
                q_toks.append(te)
                last_pe[0] = tm
            for j in range(4):
                q_chunk(j)
            w_done(wiq, last_pe[0])
            hTA_readers.append(last_pe[0])

            wvk, wtokk, wik = w_next("inku")
            wv2, wtok2, wi2 = w_next("inuv")
            c0 = 0 if kind == "first" else H0
            ncol = TT + (HALO if kind == "first" else 0)
            ut_war = state["uT_readers"]
            state["uT_readers"] = []
            u_toks = {g: [] for g in range(4)}
            ub_toks = {}
            state["ub_toks"] = ub_toks
            k_toks = []
            gtok0 = HALO + t * T

            def ku_piece(j, pc0, pn):
                wsrc, wdep = (wvk, wtokk) if j < 4 else (wv2, wtok2)
                wc = j * 128 if j < 4 else 0
                bi, bk, bfree = banks.get()

                def _mm(e):
                    ins = None
                    for k in range(8):
                        ins = e.matmul(bk[:, 0:pn], lhsT=wsrc[:, k, wc:wc + 128], rhs=hTA[:, k, pc0:pc0 + pn],
                                       start=(k == 0), stop=(k == 7))
                    return ins
                tm = S.add("pe", _mm, [wdep] + h_toks + bfree)
                last_pe[0] = tm
                is_halo_piece = (kind == "first" and pc0 == 0)
                if j == 0:
                    if sample:
                        te = S.add("act", lambda e: e.activation(out=KTs[:, 0:TT], in_=bk[:, 0:TT], func=AF.Copy),
                                   [tm] + state["KTs_readers"])
                        k_toks.append(te)
                        rel_extra = []
                    else:
                        kc0 = 0 if is_halo_piece else HALO
                        te = S.add("act", lambda e: e.activation(out=KT2[t % 2][:, kc0:kc0 + pn], in_=bk[:, 0:pn],
                                                                 func=AF.Copy), [tm] + state["kt_readers"][t % 2])
                        rel_extra = []
                        if is_halo_piece:
                            state["k_halo_tok"] = [te]
                        else:
                            k_toks.append(te)
                            if t + 1 < NT:
                                tc_ = S.add("act", lambda e: e.activation(
                                    out=KT2[(t + 1) % 2][:, 0:HALO], in_=bk[:, pn - HALO:pn], func=AF.Copy),
                                    [tm] + state["kt_readers"][(t + 1) % 2])
                                state["k_halo_tok"] = [tc_]
                                rel_extra = [tc_]
                else:
                    g = j - 1
                    if sample:
                        te = S.add("act", lambda e: e.activation(
                            out=uTs[:, g, :, 16:32], in_=bk[:, 0:TT].rearrange("p (s j) -> p s j", s=4),
                            func=AF.Copy), [tm, t_setup] + ut_war)
                    elif is_halo_piece:
                        te = S.add("act", lambda e: e.activation(out=uT[:, g, 0:16], in_=bk[:, HALO - 16:HALO],
                                                                 func=AF.Copy), [tm, t_setup] + ut_war)
                    else:
                        te = S.add("act", lambda e: e.activation(out=uT[:, g, 16:16 + pn], in_=bk[:, 0:pn],
                                                                 func=AF.Copy), [tm, t_setup] + ut_war)
                    u_toks[g].append(te)
                    rel_extra = []
                    if not is_halo_piece:
                        npc = TT if sample else pn
                        teb = S.add("act", lambda e: e.activation(out=uTb[:, g, 0:npc], in_=bk[:, 0:npc], func=AF.Copy),
                                    [tm] + state["uTb_readers"])
                        ub_toks[g] = teb
                        rel_extra = [teb]
                banks.release(bi, [te] + rel_extra)

            inku_left = [4]

            def ku_chunk(j):
                pieces = [(c0, ncol)] if ncol <= 512 else [(c0, HALO), (c0 + HALO, TT)]
                for (pc0, pn) in pieces:
                    ku_piece(j, pc0, pn)
                if j < 4:
                    inku_left[0] -= 1
                    if inku_left[0] == 0:
                        w_done(wik, last_pe[0])
                hTA_readers.append(last_pe[0])

            interleave = not sample
            for j in range(1 if interleave else 5):
                ku_chunk(j)
            k_tok = k_toks[-1]

            v_toks = {}
            vblocks = ([("halo", 0, 128)] if kind == "first" else []) + [(b, H0 + b * 128, n) for (b, n) in blocks]

            def v_block(b, hc, n):
                bi, bk, bfree = banks.get()

                def _mm(e):
                    ins = None
                    for k in range(8):
                        ins = e.matmul(bk[0:n, 0:128], lhsT=hTA[:, k, hc:hc + n], rhs=wv2[:, k, 128:256],
                                       start=(k == 0), stop=(k == 7))
                    return ins
                tm = S.add("pe", _mm, [wtok2] + h_toks + bfree)
                last_pe[0] = tm
                if sample:
                    dst = V1s[0:n, :, 0:64]
                else:
                    lb = 0 if b == "halo" else 1 + b
                    dst = V12[t % 2][:, lb, :, 0:64]
                srcv = bk[0:n, 0:128].rearrange("p (k d) -> p k d", k=2)
                te = S.add("dve", lambda e: e.tensor_copy(dst, srcv),
                           [tm, t_setup] + ([] if sample else state["v_readers"][t % 2]))
                rel = [te]
                if (not sample) and b == NBLK - 1 and t + 1 < NT:
                    dst2 = V12[(t + 1) % 2][:, 0, :, 0:64]
                    tc_ = S.add("dve", lambda e: e.tensor_copy(dst2, srcv),
                                [tm, t_setup] + state["v_readers"][(t + 1) % 2])
                    rel.append(tc_)
                    state["v_prev_tok"] = tc_
                if (kind == "last" and b == NBLK - 1) or sample:
                    te2 = S.add("dve", lambda e: e.tensor_copy(stage[0:n, 128:256], bk[0:n, 0:128]),
                                [tm] + state["stage_free"])
                    rel.append(te2)
                    state["stage_v"] = te2
                banks.release(bi, rel)
                v_toks[b] = te
            for (b, hc, n) in vblocks:
                v_block(b, hc, n)
            inuv_done = [False]
            if kind not in ("last", "sample") and not interleave:
                w_done(wi2, last_pe[0])
                inuv_done[0] = True
            hTA_readers.append(last_pe[0])

            def out_kvu():
                wv3, wtok3, wi3 = w_next("int1")
                wv4, wtok4, wi4 = w_next("int2")
                if not interleave:
                    w_done(wi2, last_pe[0])
                    inuv_done[0] = True
                (b, n) = blocks[-1]
                hc = H0 + b * 128
                bi, bk, bfree = banks.get()

                def _mm(e):
                    ins = None
                    for k in range(8):
                        ins = e.matmul(bk[0:n, 0:512], lhsT=hTA[:, k, hc:hc + n], rhs=wv3[:, k, :],
                                       start=(k == 0), stop=(k == 7))
                    return ins
                tm = S.add("pe", _mm, [wtok3] + h_toks + bfree)
                w_done(wi3, tm)
                te1 = S.add("dve", lambda e: e.tensor_copy(stage[0:n, 0:128], bk[0:n, 0:128]),
                            [tm] + state["stage_free"])
                te2 = S.add("dve", lambda e: e.tensor_copy(stage[0:n, 256:640], bk[0:n, 128:512]),
                            [tm] + state["stage_free"])
                banks.release(bi, [te1, te2])
                bi2, bk2, bfree2 = banks.get()

                def _mm2(e):
                    ins = None
                    for k in range(8):
                        ins = e.matmul(bk2[0:n, 0:128], lhsT=hTA[:, k, hc:hc + n], rhs=wv4[:, k, :],
                                       start=(k == 0), stop=(k == 7))
                    return ins
                tm2 = S.add("pe", _mm2, [wtok4] + h_toks + bfree2)
                w_done(wi4, tm2)
                hTA_readers.append(tm2)
                te3 = S.add("dve", lambda e: e.tensor_copy(stage[0:n, 640:768], bk2[0:n, 0:128]),
                            [tm2] + state["stage_free"])
                banks.release(bi2, te3)
                dstd = kvu_s_d if sample else kvu_last_d
                st = S.dma("pool", lambda eng: eng.dma_start(out=dstd[:, :], in_=stage[0:n, :]),
                           "stage", [te1, te2, te3, state["stage_v"]])
                state["stage_free"] = [st]
                out_toks.append(st)
            if kind in ("last", "sample"):
                out_kvu()

            if DBG_STOP == f"{t}:INPROJ":
                return
            mix_war = state["mix_readers"]
            state["mix_readers"] = []
            mixa_toks = []
            if not sample:
                ctxs = {}
                tans = {}
                stepno = [0]

                def attn_step():
                    step = stepno[0]
                    stepno[0] += 1
                    if step < NBLK:
                        ctxs[step] = prompt_attn_scores(t, step, q_toks, k_tok)
                    if 1 <= step <= NBLK:
                        tans[step - 1] = prompt_attn_pv(ctxs[step - 1], v_toks)
                    if step >= 2 and step - 2 < NBLK:
                        bb = step - 2
                        mixa_toks.append(attn_post_b(tans[bb], 128, bb, ctxs[bb]["seq"], mix_war))

                def pre_group(g):
                    attn_step()
                    ku_chunk(g + 1)
                    if g == 3:
                        w_done(wi2, last_pe[0])
                        inuv_done[0] = True
                d_toks_pool = pooling(kind, TT, blocks, u_toks, mix_war, pre_group=pre_group)
                while stepno[0] < NBLK + 1:
                    attn_step()
                state["late_attn"] = attn_step
            else:
                d_toks_pool = pooling(kind, TT, blocks, u_toks, mix_war)
                mixa_toks.append(sample_attention(q_toks, k_tok, v_toks[0], mix_war))
                if mixa_toks[-1] is None:
                    return
            assert inuv_done[0]

            if DBG_STOP == f"{t}:ATTN":
                return
            pool_toks, pool_ctx3 = pooling2(kind, TT, blocks, d_toks_pool, mix_war)
            rstdb_cols = {}
            if kind == "first":
                issue_convs(len(conv_order))

            if DBG_STOP == f"{t}:POOL":
                return
            def outproj_mm(hf, b, n, wv, wtok):
                ba = banks.get()
                bb = banks.get()

                def _mm(e):
                    ins = None
                    for k in range(4):
                        ins = e.matmul(ba[1][0:n, :], lhsT=mixT[:, k, b * 128:b * 128 + n], rhs=wv[:, k, :],
                                       start=(k == 0), stop=(k == 3))
                    for k in range(4, 8):
                        ins = e.matmul(bb[1][0:n, :], lhsT=mixT[:, k, b * 128:b * 128 + n], rhs=wv[:, k, :],
                                       start=(k == 4), stop=(k == 7))
                    return ins
                own = [mixa_toks[b]] if len(mixa_toks) > b else list(mixa_toks)
                tm = S.add("pe", _mm, [wtok] + own + pool_toks + ba[2] + bb[2])
                last_pe[0] = tm
                return (hf, b, n, tm, ba, bb)

            def outproj_evac(ctx_o):
                hf, b, n, tm, ba, bb = ctx_o
                xs_ = xb[0:n, b, hf * 512:(hf + 1) * 512]
                rc = rstdb_cols[b]
                t1 = S.add("dve", lambda e: e.scalar_tensor_tensor(
                    out=xs_, in0=bb[1][0:n, :], scalar=rsb[0:n, rc[0]:rc[0] + 1], in1=xs_, op0=ALU.mult,
                    op1=ALU.add), [tm, rc[1]] + x_toks[b])
                ra = state["rstda"][b]
                t2 = S.add("dve", lambda e: e.scalar_tensor_tensor(
                    out=xs_, in0=ba[1][0:n, :], scalar=rsb[0:n, ra[0]:ra[0] + 1], in1=xs_, op0=ALU.mult,
                    op1=ALU.add), [tm, t1, ra[1]])
                banks.release(ba[0], t2)
                banks.release(bb[0], t1)
                x_toks[b].append(t2)

            wvo = []
            for hf in range(2):
                wvo.append(w_next(f"out{hf}"))
            war = list(hT_readers)
            hT_readers.clear()
            h_toks = []

            ffn_rs = {}

            def norm_ffn_block(b, n):
                info = {}
                th, txs = norm_to_hT(xb[0:n, b, :], n, C_LNFFN, b * 128, x_toks[b], war, defer=-1.0, info=info)
                ffn_rs[b] = (info["col"], info["t_r"])
                h_toks.append(th)
                x_toks[b].append(txs)
            pend = None
            for bidx, (b, n) in enumerate(blocks):
                cts = [outproj_mm(hf, b, n, wvo[hf][0], wvo[hf][1]) for hf in range(2)]
                if bidx == 0:
                    if state.get("late_attn") is not None:
                        state["late_attn"]()
                        state["late_attn"] = None
                    rstdb_cols.update(pooling3(pool_ctx3))
                for c_ in cts:
                    outproj_evac(c_)
                if pend is not None:
                    norm_ffn_block(*pend)
                pend = (b, n)
            for hf in range(2):
                w_done(wvo[hf][2], last_pe[0])
            state["mix_readers"].append(last_pe[0])
            if DBG_STOP == f"{t}:OUTPROJ":
                return
            split_up = len(blocks) == NBLK
            if not split_up:
                norm_ffn_block(*pend)

            def up_chunk(i, c, wv, wtok, hid_war, hid_toks):
                bi, bk, bfree = banks.get()

                def _mm(e):
                    ins = None
                    for k in range(8):
                        ins = e.matmul(bk[:, 0:TT], lhsT=wv[:, k, c * 128:(c + 1) * 128],
                                       rhs=hT[:, k, 0:TT], start=(k == 0), stop=(k == 7))
                    return ins
                tm = S.add("pe", _mm, [wtok] + h_toks + bfree)
                last_pe[0] = tm
                fi = i * 4 + c
                tr = S.add("act", lambda e: e.activation(out=hidT[:, fi, 0:TT], in_=bk[:, 0:TT], func=AF.Relu),
                           [tm] + hid_war)
                banks.release(bi, tr)
                tsq = S.add("dve", lambda e: e.tensor_tensor(out=hidT[:, fi, 0:TT], in0=hidT[:, fi, 0:TT],
                                                             in1=hidT[:, fi, 0:TT], op=ALU.mult), [tr])
                hid_toks.append(tsq)

            pend_ffn = pend

            def up_first_split(wv, wtok, hid_war, hid_toks, pend_blk):
                NA = (NBLK - 1) * 128
                bks = [banks.get() for c in range(4)]
                tmA = []
                for c in range(4):
                    def _mmA(e, c=c):
                        ins = None
                        for k in range(8):
                            ins = e.matmul(bks[c][1][:, 0:NA], lhsT=wv[:, k, c * 128:(c + 1) * 128],
                                           rhs=hT[:, k, 0:NA], start=(k == 0), stop=(k == 7))
                        return ins
                    tmA.append(S.add("pe", _mmA, [wtok] + h_toks + bks[c][2]))
                norm_ffn_block(*pend_blk)
                for c in range(4):
                    def _mmB(e, c=c):
                        ins = None
                        for k in range(8):
                            ins = e.matmul(bks[c][1][:, NA:T], lhsT=wv[:, k, c * 128:(c + 1) * 128],
                                           rhs=hT[:, k, NA:T], start=(k == 0), stop=(k == 7))
                        return ins
                    tm = S.add("pe", _mmB, [wtok] + h_toks)
                    last_pe[0] = tm
                    fi = c
                    tr = S.add("act", lambda e, c=c, fi=fi: e.activation(out=hidT[:, fi, 0:T], in_=bks[c][1][:, 0:T],
                                                                         func=AF.Relu), [tm, tmA[c]] + hid_war)
                    banks.release(bks[c][0], tr)
                    tsq = S.add("dve", lambda e, fi=fi: e.tensor_tensor(out=hidT[:, fi, 0:T], in0=hidT[:, fi, 0:T],
                                                                        in1=hidT[:, fi, 0:T], op=ALU.mult), [tr])
                    hid_toks.append(tsq)

            def down_block(b, n, ch, kq, wv, wtok, acc, hid_toks):
                def _mm(e):
                    ins = None
                    for j in range(8):
                        ins = e.matmul(acc[1][0:n, :], lhsT=hidT[:, kq * 8 + j, b * 128:b * 128 + n],
                                       rhs=wv[:, j, :], start=(kq == 0 and j == 0), stop=(kq == 1 and j == 7))
                    return ins
                tm = S.add("pe", _mm, [wtok] + hid_toks[kq * 8:(kq + 1) * 8] + (acc[2] if kq == 0 else []))
                last_pe[0] = tm
                if kq == 1:
                    xs_ = xb[0:n, b, ch * 512:(ch + 1) * 512]
                    rs2 = ffn_rs[b]
                    ta = S.add("dve", lambda e: e.scalar_tensor_tensor(
                        out=xs_, in0=acc[1][0:n, :], scalar=rsb[0:n, rs2[0]:rs2[0] + 1], in1=xs_, op0=ALU.mult,
                        op1=ALU.add), [tm, rs2[1]] + x_toks[b])
                    banks.release(acc[0], ta)
                    x_toks[b].append(ta)

            pt_war = state["pT_readers"]
            state["pT_readers"] = []
            pT_toks = []

            def p_tr(b, n):
                bi, bk, bfree = banks.get()
                bkb = bk.bitcast(BF16)

                def _tr(e):
                    ins = None
                    for c in range(2):
                        ins = e.transpose(bkb[:, c * 128:c * 128 + n], pb[0:n, b, c * 128:(c + 1) * 128],
                                          identb[0:n, 0:n])
                    return ins
                tm = S.add("pe", _tr, [p_ld[t], t_ident] + bfree)
                last_pe[0] = tm
                src = bkb[:, 0:256].rearrange("p (c t) -> p c t", c=2)[:, :, 0:n]
                te = S.add("dve", lambda e: e.tensor_copy(pT[:, :, b * 128:b * 128 + n], src), [tm] + pt_war)
                banks.release(bi, te)
                pT_toks.append(te)
            for (b, n) in blocks:
                p_tr(b, n)
            p_read_tok = last_pe[0]

            h_toks_e = {}
            war_e = []

            ple_nr = {}

            def ple_neg_rstd(b, n, col, t_r):
                c2 = ss_col()
                t_nr = S.add("dve", lambda e: e.tensor_scalar_mul(out=rsb[0:n, c2:c2 + 1], in0=rsb[0:n, col:col + 1],
                                                                   scalar1=-1.0), [t_r])
                ple_nr[b] = (c2, t_nr)

            def norm_ple_block(b, n):
                info = {}
                th, txs = norm_to_hT(xb[0:n, b, :], n, C_LNPLE, b * 128, x_toks[b], war_e, defer=-0.5, info=info)
                ple_neg_rstd(b, n, info["col"], info["t_r"])
                h_toks_e[b] = th
                x_toks[b].append(txs)

            nt_ = t + 1
            pre_blocks = []
            if nt_ <= NT and not DBG_STOP:
                pre_blocks = [(0, NSAMP)] if nt_ == NT else [(bb, 128) for bb in range(NBLK)]
            pre_state = dict(next=0, pend=None, h_toks=[], txs={})
            war_A = list(hTA_readers)

            def prenorm_p2():
                ctx, bb, nn = pre_state["pend"]
                th = norm_p2(ctx, C_LNMIX, hTA, HALO + bb * 128, war_A)
                pre_state["h_toks"].append(th)
                pre_state["pend"] = None

            def prenorm_step():
                k = pre_state["next"]
                if pre_state["pend"] is not None:
                    prenorm_p2()
                if k < len(pre_blocks):
                    bb, nn = pre_blocks[k]
                    ctx = norm_p1(xbuf[nt_ % 2][0:nn, bb, :], nn, [x_ld[nt_]])
                    pre_state["txs"][bb] = ctx["t_xs"]
                    pre_state["pend"] = (ctx, bb, nn)
                    pre_state["next"] = k + 1

            def prenorm_flush():
                while pre_state["pend"] is not None or pre_state["next"] < len(pre_blocks):
                    prenorm_step()
                if pre_blocks:
                    hTA_readers.clear()
                    state["pre"][nt_] = dict(h_toks=pre_state["h_toks"], txs=pre_state["txs"])

            pend = None
            for fh in range(2):
                hid_war = state["hid_readers"]
                state["hid_readers"] = []
                hid_toks = []
                for i in range(4):
                    wv, wtok, wi = w_next(f"up{fh * 4 + i}")
                    if fh == 0 and i == 0 and split_up:
                        up_first_split(wv, wtok, hid_war, hid_toks, pend_ffn)
                    else:
                        for c in range(4):
                            up_chunk(i, c, wv, wtok, hid_war, hid_toks)
                    w_done(wi, last_pe[0])
                hT_readers.append(last_pe[0])
                if fh == 1:
                    war_e.extend(hT_readers)
                    hT_readers.clear()
                for ch in range(2):
                    wd = [w_next(f"dn{fh}{ch}{kq}") for kq in range(2)]
                    for (b, n) in blocks:
                        acc = banks.get()
                        for kq in range(2):
                            down_block(b, n, ch, kq, wd[kq][0], wd[kq][1], acc, hid_toks)
                        if fh == 1 and ch == 0:
                            prenorm_step()
                        if fh == 1 and ch == 1:
                            if b == blocks[0][0]:
                                prenorm_flush()
                            if pend is not None:
                                norm_ple_block(*pend)
                            pend = (b, n)
                    for kq in range(2):
                        w_done(wd[kq][2], last_pe[0])
                    state["hid_readers"].append(last_pe[0])
            if DBG_STOP == f"{t}:FFN":
                return
            if len(blocks) == 1:
                norm_ple_block(*pend)
                pend = None
            else:
                (b_l, n_l) = pend
                ctx_l = norm_p1(xb[0:n_l, b_l, :], n_l, x_toks[b_l], defer=-0.5)
                ple_neg_rstd(b_l, n_l, ctx_l["col"], ctx_l["t_r"])
                x_toks[b_l].append(ctx_l["t_xs"])

                def ple_last_p2():
                    h_toks_e[b_l] = norm_p2(ctx_l, C_LNPLE, hT, b_l * 128, war_e)

            wvp, wtokp, wip = w_next("ple")
            wvg = [w_next(f"gate{hf}") for hf in range(2)]

            def gate_block(hf, b, n, wv, wtok):
                bg = banks.get()
                bp = banks.get()

                def _mm(e):
                    ins = None
                    for k in range(8):
                        ins = e.matmul(bg[1][0:n, :], lhsT=hT[:, k, b * 128:b * 128 + n], rhs=wv[:, k, :],
                                       start=(k == 0), stop=(k == 7))
                    for k in range(2):
                        ins = e.matmul(bp[1][0:n, :], lhsT=pT[:, k, b * 128:b * 128 + n],
                                       rhs=wvp[:, k, hf * 512:(hf + 1) * 512], start=(k == 0), stop=(k == 1))
                    return ins
                tm = S.add("pe", _mm, [wtok, wtokp, h_toks_e[b]] + pT_toks + bg[2] + bp[2])
                last_pe[0] = tm
                gi = (b + hf) % 2
                nr = ple_nr[b]
                ta1 = S.add("act", lambda e: e.activation(out=tg[gi][0:n, :], in_=bg[1][0:n, :], func=AF.Exp,
                                                          scale=rsb[0:n, nr[0]:nr[0] + 1]), [tm, nr[1]] + tg_free[gi])
                banks.release(bg[0], ta1)
                ta2 = S.add("act", lambda e: e.activation(out=tg[gi][0:n, :], in_=tg[gi][0:n, :], func=AF.Ln,
                                                          bias=oneb[0:n, :], scale=1.0), [ta1, t_setup])
                ta3 = S.add("act", lambda e: e.activation(out=tg[gi][0:n, :], in_=tg[gi][0:n, :], func=AF.Exp,
                                                          scale=-1.0), [ta2])
                t1 = S.add("dve", lambda e: e.tensor_tensor(out=tp[0][0:n, :], in0=tg[gi][0:n, :],
                                                            in1=bp[1][0:n, :], op=ALU.mult),
                           [tm, ta3] + tp_free[0])
                banks.release(bp[0], t1)
                tg_free[gi] = [t1]
                xs_ = xb[0:n, b, hf * 512:(hf + 1) * 512]
                t2 = S.add("dve", lambda e: e.tensor_tensor(out=xs_, in0=tp[0][0:n, :], in1=xs_, op=ALU.add),
                           [t1] + x_toks[b])
                tp_free[0] = [t2]
                x_toks[b].append(t2)

            fin = []

            def final_block(b, n):
                col = ss_col()
                xap = xb[0:n, b, :]
                t_ss = S.add("act", lambda e: e.activation(out=junk[0:n, :], in_=xap, func=AF.Square,
                                                           accum_out=ssb[0:n, col:col + 1]), x_toks[b] + [t_setup])
                t_r = rstd_chain(lambda: ssb[0:n, col:col + 1], t_ss, col, 1, n, D)
                t_y = S.add("dve", lambda e: e.scalar_tensor_tensor(
                    out=xap, in0=xap, scalar=rsb[0:n, col:col + 1], in1=lnf[0:n, :], op0=ALU.mult, op1=ALU.mult),
                    [t_r, const_tok, t_ss] + x_toks[b])
                fin.append(t_y)

            pendf = None
            for (b, n) in blocks:
                for hf in range(2):
                    gate_block(hf, b, n, wvg[hf][0], wvg[hf][1])
                if pend is not None and b == blocks[1][0]:
                    ple_last_p2()
                    pend = None
                if pendf is not None:
                    final_block(*pendf)
                pendf = (b, n)
            for hf in range(2):
                w_done(wvg[hf][2], last_pe[0])
            w_done(wip, last_pe[0])
            hT_readers.append(last_pe[0])
            state["pT_readers"] = [last_pe[0], p_read_tok]
            if DBG_STOP == f"{t}:GATE":
                return
            final_block(*pendf)

            if sample:
                st = S.dma("sp", lambda eng: eng.dma_start(out=ys_d[:, :], in_=xb[0:NSAMP, 0, :]), f"y{t % 2}", fin)
            else:
                dsty = y_d[t * T:(t + 1) * T, :].rearrange("(b p) d -> p b d", p=128)
                st = S.dma("sp", lambda eng: eng.dma_start(out=dsty, in_=xb[:, :, :]), f"y{t % 2}", fin)
            y_st[t] = st
            out_toks.append(st)

        def prep_sample_hist():
            toks = []

            tcb = S.add("pool", lambda e: e.tensor_copy(spb[0:60, :], spf[0:60, :]), [ckv_tok])
            bi, bk, bfree = banks.get()
            bkb = bk.bitcast(BF16)

            def _str(e):
                ins = None
                for g in range(4):
                    ins = e.transpose(bkb[:, g * 64:g * 64 + 60], spb[0:60, g * 128:(g + 1) * 128],
                                      identb[0:60, 0:60])
                return ins
            tm = S.add("pe", _str, [tcb, t_ident] + bfree)
            te = S.add("dve", lambda e: e.tensor_copy(
                uTs[:, :, :, 1:16],
                bkb[:, 0:256].rearrange("p (g j) -> p g j", g=4)[:, :, 0:60].rearrange("p g (s j) -> p g s j", s=4)),
                [tm, t_setup])
            banks.release(bi, te)
            toks.append(te)
            state["spT_tok"] = toks

        for tl in tiles:
            if DBG_STOP and DBG_STOP != "ALLCONV" and tl["idx"] > int(DBG_STOP.split(":")[0]):
                break
            if tl["kind"] == "sample":
                prep_sample_hist()
            run_tile(tl)

        S.add("pool", lambda e: e.memset(fint[:], 0.0), out_toks)
        S.emit(block)
    return nc


_CACHE = {}


def _get_program():
    if "nc" not in _CACHE:
        _CACHE["nc"] = build_program()
    return _CACHE["nc"]


def kernel(x_prompt, x_sample, cache_k, cache_v, state_pool, p_prompt, p_sample,
           ln_mix, w_in, attn_sinks, g_attn_out, w_pool, pool_scale, g_pool_out,
           w_out, ln_ffn, w_up, w_down, ln_ple, w_ple_gate, w_ple_proj, ln_final):
    f32 = np.float32
    x_prompt = np.asarray(x_prompt, f32)
    x_sample = np.asarray(x_sample, f32)
    B, SEQ, _ = x_prompt.shape
    segs_per_seq = SEQ // SEG
    assert B * segs_per_seq == NCORES

    w_in0 = np.asarray(w_in, f32)[0]
    qcols = []
    for i in range(4):
        qcols += list(range(i * 64, (i + 1) * 64)) + list(range((4 + i) * 64, (5 + i) * 64))
    kcols = list(range(512, 640))
    vcols = list(range(640, 768))
    ucols = list(range(768, 1280))
    perm = qcols + kcols + ucols[0:384] + ucols[384:512] + vcols + kcols + ucols[0:384] + ucols[384:512]
    w_in_p = np.ascontiguousarray(w_in0[:, perm])

    def col8(v):
        return np.asarray(v, f32).reshape(-1, 128).T

    w_pool_l = np.ascontiguousarray(np.asarray(w_pool, f32)[0].transpose(1, 0, 2).reshape(128, 512))
    lnf = np.ascontiguousarray(np.broadcast_to(np.asarray(ln_final, f32)[None, :], (128, D)))
    bmask = np.kron(np.eye(4, dtype=f32), np.ones((16, 16), f32))
    ident = np.eye(128, dtype=f32)

    shared = dict(
        w_in_p=w_in_p, w_out=np.ascontiguousarray(np.asarray(w_out, f32)[0]),
        w_up=np.ascontiguousarray(np.asarray(w_up, f32)[0]),
        w_down=np.ascontiguousarray(np.asarray(w_down, f32)[0]),
        w_gate=np.ascontiguousarray(np.asarray(w_ple_gate, f32)[0]),
        w_ple=np.ascontiguousarray(np.asarray(w_ple_proj, f32)[0]),
        w_pool=w_pool_l, lnf=lnf, bmask=bmask, ident=ident)

    in_maps = []
    for c in range(NCORES):
        b, s = divmod(c, segs_per_seq)
        xin = np.zeros((HALO + SEG, D), f32)
        xin[HALO:] = x_prompt[b, s * SEG:(s + 1) * SEG]
        if s > 0:
            xin[:HALO] = x_prompt[b, s * SEG - HALO:s * SEG]
        cols = np.zeros((128, NCOLS), f32)
        cols[:, C_LNMIX:C_LNMIX + 8] = col8(np.asarray(ln_mix)[0])
        cols[:, C_LNFFN:C_LNFFN + 8] = col8(np.asarray(ln_ffn)[0])
        cols[:, C_LNPLE:C_LNPLE + 8] = col8(np.asarray(ln_ple)[0])
        cols[:, C_GATT:C_GATT + 4] = col8(np.asarray(g_attn_out)[0])
        cols[:, C_GPOOL:C_GPOOL + 4] = col8(np.asarray(g_pool_out)[0])
        cols[:, C_PSCALE:C_PSCALE + 4] = col8(np.asarray(pool_scale)[0])
        cols[:, C_HBIAS] = 0.0 if s > 0 else -30000.0
        cols[:, C_SINK:C_SINK + 8] = np.asarray(attn_sinks, f32)[0][None, :]
        cols[:, C_INVW:C_INVW + 4] = np.array([0.5, 0.25, 0.125, 0.0625], f32)[None, :]
        invcnt = np.zeros((128, 64), f32)
        for g, w in enumerate((2, 4, 8, 16)):
            pos = np.arange(16)
            cnt = np.minimum(pos + 1, w) if s == 0 else np.full(16, w)
            invcnt[:, g * 16:(g + 1) * 16] = (w / cnt).astype(f32)[None, :]
        ss = slice(c * 4, (c + 1) * 4)
        m = dict(shared)
        m.update(
            xin=xin, pin=np.ascontiguousarray(np.asarray(p_prompt, f32)[0, b, s * SEG:(s + 1) * SEG]),
            xs=np.ascontiguousarray(x_sample[ss].reshape(NSAMP, D)),
            ps=np.ascontiguousarray(np.asarray(p_sample, f32)[0, ss].reshape(NSAMP, 256)),
            ck=np.ascontiguousarray(np.asarray(cache_k, f32)[0, ss].reshape(4 * 128, 128)),
            cv=np.ascontiguousarray(np.asarray(cache_v, f32)[0, ss].reshape(4 * 128, 128)),
            spool=np.ascontiguousarray(np.asarray(state_pool, f32)[0, ss].reshape(60, 512)),
            cols=cols, invcnt=invcnt)
        in_maps.append(m)

    nc = _get_program()
    res = run_bass_kernel_spmd(nc, in_maps, core_ids=list(range(NCORES)))
    R = res.results

    nsb = x_sample.shape[0]
    y_prompt = np.zeros((B, SEQ, D), f32)
    y_sample = np.zeros((nsb, 16, D), f32)
    nkp = np.zeros((1, B, 128, 2, 64), f32)
    nvp = np.zeros((1, B, 128, 2, 64), f32)
    npp = np.zeros((1, B, 15, 512), f32)
    nks = np.zeros((1, nsb, 16, 2, 64), f32)
    nvs = np.zeros((1, nsb, 16, 2, 64), f32)
    nps = np.zeros((1, nsb, 15, 512), f32)
    for c in range(NCORES):
        b, s = divmod(c, segs_per_seq)
        y_prompt[b, s * SEG:(s + 1) * SEG] = R[c]["y"]
        y_sample[c * 4:(c + 1) * 4] = R[c]["ys"].reshape(4, 16, D)
        if s == segs_per_seq - 1:
            kvu = R[c]["kvu_last"]
            nkp[0, b] = kvu[:, 0:128].reshape(128, 2, 64)
            nvp[0, b] = kvu[:, 128:256].reshape(128, 2, 64)
            npp[0, b] = kvu[113:128, 256:768]
        kvs = R[c]["kvu_s"].reshape(4, 16, 768)
        nks[0, c * 4:(c + 1) * 4] = kvs[:, :, 0:128].reshape(4, 16, 2, 64)
        nvs[0, c * 4:(c + 1) * 4] = kvs[:, :, 128:256].reshape(4, 16, 2, 64)
        nps[0, c * 4:(c + 1) * 4] = kvs[:, 1:16, 256:768]
    return (y_prompt, y_sample, nkp, nvp, npp, nks, nvs, nps)
```
